# Optimizing a Trainium2 kernel written in Bass

```python
import math
import jax, jax.numpy as jnp
from jax import lax
import numpy as np

D_MODEL = 2048
BATCH = 4
SEQ = 2048
DEPTH = 2

CHUNK = 64

CONV_WIDTH = D_MODEL
CONV_K = 3
SSM_WIDTH = D_MODEL // 2
SSM_GROUP = 16
SSM_GROUPS = SSM_WIDTH // SSM_GROUP
SSM_STATE = 64
DT_MIN = 0.001
DT_MAX = 0.1
RMS_EPS = 1e-6

_SIZES = (CONV_WIDTH, CONV_WIDTH, CONV_WIDTH, CONV_WIDTH,
          SSM_WIDTH, SSM_WIDTH,
          D_MODEL, D_MODEL)
N_IN = int(sum(_SIZES))
SPLITS = tuple(int(s) for s in np.cumsum(_SIZES)[:-1])

kernel_name = "hybrid_conv_s5_gated_encoder"


def rmsnorm(x, g):
    x32 = x.astype(jnp.float32)
    y = x32 * lax.rsqrt(jnp.mean(x32 * x32, axis=-1, keepdims=True) + RMS_EPS)
    return (y * g.astype(jnp.float32)).astype(x.dtype)


def causal_depthwise_conv(v, w):
    L = v.shape[1]
    vp = jnp.pad(v, ((0, 0), (CONV_K - 1, 0), (0, 0)))
    out = w[0] * vp[:, 0:L]
    for k in range(1, CONV_K):
        out = out + w[k] * vp[:, k:k + L]
    return out


def s5_scan(u, a_re, a_im, log_dt, b_re, b_im, c_re, c_im, d_skip):
    bsz, L, _ = u.shape
    f32 = jnp.float32
    ug = u.astype(f32).reshape(bsz, L, SSM_GROUPS, SSM_GROUP)
    lam = lax.complex(a_re.astype(f32), a_im.astype(f32))
    dt = jnp.exp(log_dt.astype(f32))[:, None]
    lam_bar = jnp.exp(lam * dt)
    b = lax.complex(b_re.astype(f32), b_im.astype(f32))
    b_bar = ((lam_bar - 1.0) / lam)[..., None] * b
    bu = jnp.einsum('gpc,blgc->blgp', b_bar, ug.astype(jnp.complex64))
    a = jnp.broadcast_to(lam_bar, bu.shape)

    def combine(e1, e2):
        a1, h1 = e1
        a2, h2 = e2
        return a1 * a2, a2 * h1 + h2

    _, h = lax.associative_scan(combine, (a, bu), axis=1)
    c = lax.complex(c_re.astype(f32), c_im.astype(f32))
    y = jnp.einsum('gcp,blgp->blgc', c, h).real + d_skip.astype(f32) * ug
    return y.reshape(bsz, L, SSM_WIDTH)


def hybrid_layer(x, norm_g, w_in, conv_w, w_out_a, a_re, a_im, log_dt, b_re, b_im,
                 c_re, c_im, d_skip, w_glu, b_glu, w_out_b, w_o):
    h = rmsnorm(x, norm_g)
    proj = jnp.einsum('bld,dn->bln', h, w_in)
    v, bg, cg, za, u, zb, ga, gb = jnp.split(proj, SPLITS, axis=-1)
    ya = bg * causal_depthwise_conv(cg * v, conv_w)
    ya = jnp.einsum('blc,cd->bld', ya * jax.nn.silu(za), w_out_a)
    yb = jax.nn.gelu(s5_scan(u, a_re, a_im, log_dt, b_re, b_im, c_re, c_im, d_skip))
    yb = yb * jax.nn.sigmoid(jnp.einsum('blc,ce->ble', yb, w_glu.astype(jnp.float32))
                             + b_glu.astype(jnp.float32))
    yb = yb.astype(x.dtype)
    yb = jnp.einsum('blc,cd->bld', yb * jax.nn.silu(zb), w_out_b)
    m = jax.nn.sigmoid(ga) * ya + jax.nn.sigmoid(gb) * yb
    return x + jnp.einsum('bld,de->ble', m, w_o)


def setup_inputs(seed: int = 0) -> dict:
    key = jax.random.key(seed)
    ks = jax.random.split(key, 20)
    f32 = jnp.float32
    n = lambda k, shape, s: jax.random.normal(k, shape, f32) * s
    res_scale = 1.0 / math.sqrt(2.0 * DEPTH)
    G, P, c = SSM_GROUPS, SSM_STATE, SSM_GROUP
    a_im_base = math.pi * jnp.arange(P, dtype=f32)
    return {
        "x": n(ks[0], (BATCH, SEQ, D_MODEL), 1.0),
        "norm_g": 1.0 + n(ks[1], (DEPTH, D_MODEL), 0.02),
        "w_in": n(ks[2], (DEPTH, D_MODEL, N_IN), D_MODEL ** -0.5),
        "conv_w": n(ks[3], (DEPTH, CONV_K, CONV_WIDTH), CONV_K ** -0.5),
        "w_out_a": n(ks[4], (DEPTH, CONV_WIDTH, D_MODEL), CONV_WIDTH ** -0.5),
        "a_re": -0.5 + n(ks[5], (DEPTH, G, P), 0.01),
        "a_im": a_im_base + n(ks[6], (DEPTH, G, P), 0.01),
        "log_dt": jax.random.uniform(ks[7], (DEPTH, G), f32,
                                     math.log(DT_MIN), math.log(DT_MAX)),
        "b_re": n(ks[8], (DEPTH, G, P, c), (2.0 * c) ** -0.5),
        "b_im": n(ks[9], (DEPTH, G, P, c), (2.0 * c) ** -0.5),
        "c_re": n(ks[10], (DEPTH, G, c, P), (2.0 * P) ** -0.5),
        "c_im": n(ks[11], (DEPTH, G, c, P), (2.0 * P) ** -0.5),
        "d_skip": n(ks[12], (DEPTH, G, c), 1.0),
        "w_glu": n(ks[13], (DEPTH, SSM_WIDTH, SSM_WIDTH), SSM_WIDTH ** -0.5),
        "b_glu": n(ks[14], (DEPTH, SSM_WIDTH), 0.02),
        "w_out_b": n(ks[15], (DEPTH, SSM_WIDTH, D_MODEL), SSM_WIDTH ** -0.5),
        "w_o": n(ks[16], (DEPTH, D_MODEL, D_MODEL), D_MODEL ** -0.5 * res_scale),
        "final_g": 1.0 + n(ks[17], (D_MODEL,), 0.02),
    }


def reference(x, norm_g, w_in, conv_w, w_out_a, a_re, a_im, log_dt, b_re, b_im,
              c_re, c_im, d_skip, w_glu, b_glu, w_out_b, w_o, final_g):
    for i in range(DEPTH):
        x = hybrid_layer(x, norm_g[i], w_in[i], conv_w[i], w_out_a[i], a_re[i], a_im[i],
                         log_dt[i], b_re[i], b_im[i], c_re[i], c_im[i], d_skip[i],
                         w_glu[i], b_glu[i], w_out_b[i], w_o[i])
    return rmsnorm(x, final_g)
```

```python
import contextlib
import math
import numpy as np
import concourse.bass as bass
import concourse.mybir as mybir
from concourse.bass_utils import run_bass_kernel_spmd

F32 = mybir.dt.float32
BF16 = mybir.dt.bfloat16
AF = mybir.ActivationFunctionType
ALU = mybir.AluOpType

ENGS = ("pe", "act", "dve", "pool", "sp")
SAME_ENGINE_SYNC = True
DEPTH = 2
NT = 1024
D = 2048
KC = 16
NIN = 14336
MAGIC = 12582912.0
TWO_PI = 2.0 * math.pi
OFF_V, OFF_BG, OFF_CG, OFF_ZA, OFF_U, OFF_ZB, OFF_GA, OFF_GB = 0, 2048, 4096, 6144, 8192, 9216, 10240, 12288

C_IDENT, C_A0MASK, C_SWAP, C_LS = 0, 128, 256, 384
C_TM, C_TP, C_WM, C_WP, C_SGN, C_EPS = 512, 513, 514, 515, 516, 517
NCONST = 520


class Acc:
    __slots__ = ("space", "lo", "hi")

    def __init__(self, space, lo, hi):
        self.space, self.lo, self.hi = space, lo, hi


class Buf:
    def __init__(self, space, lo, nbytes, ap, esz):
        self.space, self.lo, self.hi, self.ap, self.esz = space, lo, lo + nbytes, ap, esz

    def all(self):
        return Acc(self.space, self.lo, self.hi)

    def rng(self, a, b):
        return Acc(self.space, self.lo + a * self.esz, self.lo + b * self.esz)


class Sched:
    def __init__(self, nc, n_dma_sems=16):
        self.nc = nc
        self.ops = {e: [] for e in ENGS}
        self.cnt = {e: 0 for e in ENGS}
        self.pending = {e: [] for e in ENGS}
        self.wr = {}
        self.rd = {}
        self.n_dma_sems = n_dma_sems
        self.dma_cnt = {}
        self.dma_rr = {e: 0 for e in ENGS}
        self.sem_names = set()
        self.nops = 0
        self.ps_last = {}

    def _deps(self, reads, writes):
        deps = []
        for a in reads:
            for (lo, hi, ev) in self.wr.get(a.space, ()):
                if lo < a.hi and a.lo < hi:
                    deps.append((ev, True))
        for a in writes:
            for (lo, hi, ev) in self.wr.get(a.space, ()):
                if lo < a.hi and a.lo < hi:
                    deps.append((ev, False))
            for (lo, hi, ev) in self.rd.get(a.space, ()):
                if lo < a.hi and a.lo < hi:
                    deps.append((ev, False))
        return deps

    def _record(self, reads, writes, ev):
        for a in writes:
            wl = self.wr.setdefault(a.space, [])
            wl[:] = [w for w in wl if not (a.lo <= w[0] and w[1] <= a.hi)]
            wl.append((a.lo, a.hi, ev))
            rl = self.rd.setdefault(a.space, [])
            rl[:] = [r for r in rl if not (a.lo <= r[0] and r[1] <= a.hi)]
        for a in reads:
            rl = self.rd.setdefault(a.space, [])
            rl[:] = [r for r in rl if not (r[0] == a.lo and r[1] == a.hi and r[2][0] == ev[0])]
            rl.append((a.lo, a.hi, ev))

    def _emit(self, eng, fn, reads, writes, inc=True, dma=False, custom_sem=None):
        self.nops += 1
        deps = self._deps(reads, writes)
        banks = set()
        for a in tuple(reads) + tuple(writes):
            if a.space == "ps":
                for bnk in range(a.lo // 2048, (a.hi - 1) // 2048 + 1):
                    banks.add(bnk)
        for bnk in banks:
            for oe, oev in self.ps_last.get(bnk, {}).items():
                if oe != eng:
                    deps.append((oev, True))
        waits = []
        for ev, is_raw in deps:
            if ev[0] == ("eng", eng) and (eng == "pe" or not SAME_ENGINE_SYNC or not is_raw):
                continue
            waits.append(ev)
        if custom_sem is not None:
            semkey = custom_sem
            ev = [semkey, 1]
            incspec = (semkey, 1)
        elif dma:
            r = self.dma_rr[eng]
            self.dma_rr[eng] = (r + 1) % self.n_dma_sems
            semkey = ("dma", eng, r)
            if self.dma_cnt.get(semkey, 0) > 0:
                waits.append([semkey, self.dma_cnt[semkey]])
            self.dma_cnt[semkey] = self.dma_cnt.get(semkey, 0) + 16
            ev = [semkey, self.dma_cnt[semkey]]
            incspec = (semkey, 16)
        elif inc:
            self.cnt[eng] += 1
            semkey = ("eng", eng)
            ev = [semkey, self.cnt[eng]]
            for pev in self.pending[eng]:
                pev[1] = self.cnt[eng]
            self.pending[eng] = []
            incspec = (semkey, 1)
        else:
            semkey = ("eng", eng)
            ev = [semkey, None]
            self.pending[eng].append(ev)
            incspec = None
        self.sem_names.add(semkey)
        self.ops[eng].append((fn, waits, incspec))
        self._record(reads, writes, ev)
        for bnk in banks:
            self.ps_last.setdefault(bnk, {})[eng] = ev
        return ev

    def op(self, eng, fn, reads=(), writes=(), inc=True, custom_sem=None):
        return self._emit(eng, fn, tuple(reads), tuple(writes), inc=inc, custom_sem=custom_sem)

    def dma(self, eng, out, in_, reads=(), writes=()):
        return self._emit(eng, lambda e: e.dma_start(out=out, in_=in_), tuple(reads), tuple(writes), dma=True)

    def wait_all(self, eng, events):
        self.ops[eng].append((None, list(events), None))

    def replay(self, block, sems):
        engmap = {"pe": block.tensor, "act": block.scalar, "dve": block.vector,
                  "pool": block.gpsimd, "sp": block.sync}
        for e in ENGS:
            assert not self.pending[e], f"pending non-inc'd ops on {e}"

        def make(e):
            oplist = self.ops[e]

            def body(eng):
                waited = {}
                for fn, waits, incspec in oplist:
                    need = {}
                    for semkey, val in waits:
                        assert val is not None
                        if waited.get(semkey, 0) >= val:
                            continue
                        need[semkey] = max(need.get(semkey, 0), val)
                    for semkey, val in need.items():
                        eng.wait_ge(sems[semkey], val)
                        waited[semkey] = val
                    if fn is None:
                        continue
                    ins = fn(eng)
                    if incspec is not None:
                        ins.then_inc(sems[incspec[0]], incspec[1])
            return body

        for e in ENGS:
            if self.ops[e]:
                engmap[e](make(e))


class Arena:
    def __init__(self, tensor, base, nbytes):
        self.t, self.base, self.nbytes, self.off = tensor, base, nbytes, 0

    def reset(self):
        self.off = 0

    def sub(self, off, nbytes):
        a = Arena(self.t, self.base + off, nbytes)
        a.t0 = getattr(self, "t0", 0) + off
        return a

    def alloc(self, shape, dt, parts=128):
        esz = 4 if dt == F32 else 2
        n = 1
        for s in shape:
            n *= s
        nb = (n * esz + 31) // 32 * 32
        assert self.off + nb <= self.nbytes, ("arena overflow", self.off, nb, self.nbytes)
        o4 = (getattr(self, 't0', 0) + self.off) // 4
        v = self.t[0:parts, o4:o4 + nb // 4]
        if dt != F32:
            v = v.bitcast(dt)
        v = v[:, 0:n]
        if len(shape) == 2:
            v = v.rearrange("p (a b) -> p a b", b=shape[1])
        elif len(shape) == 3:
            v = v.rearrange("p (a b c) -> p a b c", b=shape[1], c=shape[2])
        elif len(shape) == 4:
            v = v.rearrange("p (a b c d) -> p a b c d", b=shape[1], c=shape[2], d=shape[3])
        b = Buf("sb", self.base + self.off, n * esz, v, esz)
        self.off += nb
        return b


def build_program(debug=()):
    nc = bass.Bass("TRN2", target_bir_lowering=False)
    dbg_events = []

    def tap(name, buf, shape, dt, layer=0):
        if name not in debug or layer != 0:
            return
        d = nc.dram_tensor("dbg_" + name, [128] + list(shape), dt, kind="ExternalOutput").ap()
        dbg_events.append(S.dma("sp", d, buf.ap, reads=[buf.all()], writes=[Acc("dbg_" + name, 0, 1)]))
    dt_in = lambda name, shape: nc.dram_tensor(name, shape, F32, kind="ExternalInput").ap()
    x_d = dt_in("x", [NT, D])
    normg_d = dt_in("norm_g", [DEPTH, D])
    win_d = dt_in("w_in", [DEPTH, D, NIN])
    convw_d = dt_in("conv_w", [DEPTH, 3, D])
    wa_d = dt_in("w_out_a", [DEPTH, D, D])
    are_d = dt_in("a_re", [DEPTH, 64, 64])
    aim_d = dt_in("a_im", [DEPTH, 64, 64])
    ldt_d = dt_in("log_dt", [DEPTH, 64])
    bre_d = dt_in("b_re", [DEPTH, 64, 64, 16])
    bim_d = dt_in("b_im", [DEPTH, 64, 64, 16])
    cre_d = dt_in("c_re", [DEPTH, 64, 16, 64])
    cim_d = dt_in("c_im", [DEPTH, 64, 16, 64])
    dsk_d = dt_in("d_skip", [DEPTH, 64, 16])
    wglu_d = dt_in("w_glu", [DEPTH, 1024, 1024])
    bglu_d = dt_in("b_glu", [DEPTH, 1024])
    wb_d = dt_in("w_out_b", [DEPTH, 1024, D])
    wo_d = dt_in("w_o", [DEPTH, D, D])
    fg_d = dt_in("final_g", [D])
    const_d = dt_in("consts", [128, NCONST])
    mcol_d = dt_in("maskcol", [128, 1])
    out_d = nc.dram_tensor("out", [NT, D], F32, kind="ExternalOutput").ap()
    cc1in = [nc.dram_tensor(f"cc1in{l}", [64, 128], F32) for l in range(DEPTH)]
    cc1out = [nc.dram_tensor(f"cc1out{l}", [128, 128], F32) for l in range(DEPTH)]
    cc2in = [nc.dram_tensor(f"cc2in{l}", [32, 128], F32) for l in range(DEPTH)]
    cc2out = [nc.dram_tensor(f"cc2out{l}", [64, 128], F32) for l in range(DEPTH)]

    S = Sched(nc)
    es = contextlib.ExitStack()
    with es:
        def raw(name, nbytes):
            return es.enter_context(nc.sbuf_tensor(name, [128, nbytes // 4], F32))
        sb_off = [0]

        def region(name, nbytes):
            t = raw(name, nbytes)
            a = Arena(t, sb_off[0], nbytes)
            sb_off[0] += nbytes
            return a
        RX = region("RX", 65536)
        RH = region("RH", 32768)
        RAO = region("RAO", 32768)
        RM = region("RM", 32768)
        RYB = region("RYB", 16384)
        RWB = region("RWB", 16384)
        RT = region("RT", 8192)
        RS = region("RS", 6144)

        X = RX.alloc([16, 1024], F32)
        H = RH.alloc([16, 1024], BF16)
        AO = RAO.alloc([16, 1024], BF16)
        RAO.reset()
        WU = RAO.alloc([2, 16, 512], BF16)
        RAO.reset()
        XST = [RAO.alloc([2048], F32) for _ in range(2)]
        M = RM.alloc([16, 1024], BF16)
        RM.reset()
        RUT = RM.sub(0, 16384)
        RUB = RM.sub(16384, 16384)
        UT = RUT.alloc([64, 8, 16], BF16)
        RUT.reset()
        UBLK = RUB.alloc([64, 128], BF16)
        GATE = RM.alloc([8, 1024], BF16)
        YB = RYB.alloc([8, 1024], BF16)
        NW = 4
        WB = [RWB.alloc([16, 128], BF16) for _ in range(NW)]
        CONST = RS.alloc([NCONST], F32)
        IDB = RS.alloc([128], BF16)
        LSB = RS.alloc([128], BF16)
        ONB = RS.alloc([128], BF16)
        ONF = RS.alloc([128], F32)
        MCOL = RS.alloc([1], F32)
        NG = RS.alloc([DEPTH, 16], F32)
        FG = RS.alloc([16], F32)
        CW = RS.alloc([DEPTH, 3, 16], F32)
        BGL = RS.alloc([DEPTH, 8], F32)
        TAIL = RS.alloc([16, 2], F32)
        GATE01 = RS.alloc([16, 2], F32)
        BGT = RS.alloc([2], F32)
        TAILP = RS.alloc([16, 2], F32)
        DL = RS.alloc([4, 16], F32)

        PS = []
        for b in range(8):
            t = es.enter_context(nc.psum_tensor(f"ps{b}", [128, 512], F32))
            PS.append(Buf("ps", b * 2048, 2048, t[:, :], 4))

        def psb(b):
            return PS[b].ap.bitcast(BF16)

        ident = CONST.ap[:, C_IDENT:C_IDENT + 128]
        a0mask = CONST.ap[:, C_A0MASK:C_A0MASK + 128]
        swapm = CONST.ap[:, C_SWAP:C_SWAP + 128]
        ccol = lambda c: CONST.ap[:, c:c + 1]

        dq = ["sp"]

        S.dma("sp", CONST.ap, const_d[:, :], writes=[CONST.all()])
        S.dma("sp", MCOL.ap, mcol_d[:, :], writes=[MCOL.all()])
        RT.reset()
        vec_jobs = []
        for l_ in range(DEPTH):
            vec_jobs.append((normg_d[l_].rearrange("(k p) -> k p", p=128), 16, NG.ap[:, l_, :], NG))
            vec_jobs.append((bglu_d[l_].rearrange("(k p) -> k p", p=128), 8, BGL.ap[:, l_, :], BGL))
            for t_ in range(3):
                vec_jobs.append((convw_d[l_, t_].rearrange("(k p) -> k p", p=128), 16, CW.ap[:, l_, t_, :], CW))
        vec_jobs.append((fg_d.rearrange("(k p) -> k p", p=128), 16, FG.ap, FG))
        for vi, (src, nk, dst_ap, dst_buf) in enumerate(vec_jobs):
            stg = RT.alloc([128], F32, parts=16)
            S.dma("sp", stg.ap[0:nk, :], src, writes=[stg.all()])
            bk = vi % 4
            S.op("pe", lambda e, stg=stg, nk=nk, bk=bk: e.transpose(out=PS[bk].ap[:, 0:nk], in_=stg.ap[0:nk, :], identity=CONST.ap[0:nk, 0:nk]),
                 reads=[stg.all(), CONST.all()], writes=[PS[bk].rng(0, 16)])
            S.op("dve", lambda e, dst_ap=dst_ap, nk=nk, bk=bk: e.tensor_copy(out=dst_ap, in_=PS[bk].ap[:, 0:nk]),
                 reads=[PS[bk].rng(0, 16)], writes=[dst_buf.all()])
        S.op("dve", lambda e: e.tensor_copy(out=IDB.ap, in_=ident), reads=[CONST.all()], writes=[IDB.all()])
        S.op("pool", lambda e: e.memset(ONB.ap, 1.0), writes=[ONB.all()])
        S.op("pool", lambda e: e.memset(ONF.ap, 1.0), writes=[ONF.all()])
        S.op("dve", lambda e: e.tensor_copy(out=LSB.ap, in_=CONST.ap[:, C_LS:C_LS + 128]), reads=[CONST.all()], writes=[LSB.all()])

        xv = x_d.rearrange("(t j) d -> j t d", j=8)
        for j in range(8):
            st = XST[j % 2]
            S.dma("sp", st.ap, xv[j], writes=[st.all()])
            for kq in range(4):
                bank = (j * 4 + kq) % 4
                for kk in range(4):
                    k = kq * 4 + kk
                    S.op("pe", lambda e, k=k, kk=kk, bank=bank, st=st: e.transpose(
                        out=PS[bank].ap[:, kk * 128:(kk + 1) * 128], in_=st.ap[:, k * 128:(k + 1) * 128], identity=ident),
                        reads=[st.rng(k * 128, (k + 1) * 128), CONST.all()], writes=[PS[bank].rng(kk * 128, (kk + 1) * 128)],
                        inc=(kk == 3))
                eng = "act" if kq % 2 == 0 else "dve"
                outap = X.ap[:, kq * 4:(kq + 1) * 4, j * 128:(j + 1) * 128]
                inap = PS[bank].ap.rearrange("p (a b) -> p a b", b=128)
                wr = [X.rng(k * 1024 + j * 128, k * 1024 + (j + 1) * 128) for k in range(kq * 4, kq * 4 + 4)]
                if eng == "act":
                    S.op("act", lambda e, o=outap, i=inap: e.copy(out=o, in_=i), reads=[PS[bank].all()], writes=wr)
                else:
                    S.op("dve", lambda e, o=outap, i=inap: e.tensor_copy(out=o, in_=i), reads=[PS[bank].all()], writes=wr)

        tap("x0", X, [16, 1024], F32)
        def rmsnorm(gcol_fn, out_h):
            RT.reset()
            SQ = [RT.alloc([1024], BF16) for _ in range(2)]
            RSTD = RT.alloc([1024], F32)
            for k in range(16):
                sq = SQ[k % 2]
                S.op("act", lambda e, k=k, sq=sq: e.activation(out=sq.ap, in_=X.ap[:, k, :], func=AF.Square),
                     reads=[X.rng(k * 1024, (k + 1) * 1024)], writes=[sq.all()])
                for hf in range(2):
                    S.op("pe", lambda e, k=k, hf=hf, sq=sq: e.matmul(PS[hf].ap, lhsT=ONB.ap, rhs=sq.ap[:, hf * 512:(hf + 1) * 512],
                                                                  start=(k == 0), stop=(k == 15)),
                         reads=[sq.rng(hf * 512, (hf + 1) * 512), ONB.all()], writes=[PS[hf].all()], inc=True)
            for hf in range(2):
                S.op("act", lambda e, hf=hf: e.activation(out=RSTD.ap[:, hf * 512:(hf + 1) * 512], in_=PS[hf].ap, func=AF.Sqrt,
                                                          scale=1.0 / D, bias=ccol(C_EPS)),
                     reads=[PS[hf].all(), CONST.all()], writes=[RSTD.rng(hf * 512, (hf + 1) * 512)])
            S.op("dve", lambda e: e.reciprocal(out=RSTD.ap, in_=RSTD.ap), reads=[RSTD.all()], writes=[RSTD.all()])
            for k in range(16):
                if out_h:
                    S.op("dve", lambda e, k=k: e.scalar_tensor_tensor(out=H.ap[:, k, :], in0=X.ap[:, k, :], scalar=gcol_fn(k),
                                                                      in1=RSTD.ap, op0=ALU.mult, op1=ALU.mult),
                         reads=[X.rng(k * 1024, (k + 1) * 1024), RSTD.all(), NG.all()], writes=[H.rng(k * 1024, (k + 1) * 1024)])
                else:
                    S.op("dve", lambda e, k=k: e.scalar_tensor_tensor(out=X.ap[:, k, :], in0=X.ap[:, k, :], scalar=gcol_fn(k),
                                                                      in1=RSTD.ap, op0=ALU.mult, op1=ALU.mult),
                         reads=[X.rng(k * 1024, (k + 1) * 1024), RSTD.all(), FG.all()], writes=[X.rng(k * 1024, (k + 1) * 1024)])

        wslot = [0]

        def wload(src_ap, nk):
            s = WB[wslot[0] % NW]
            wslot[0] += 1
            S.dma("pool", s.ap[:, 0:nk, :], src_ap.rearrange("(k p) n -> p k n", p=128),
                  writes=[s.rng(0, nk * 128)])
            return s

        def proj(slot, nk, rhs_buf, banks):
            for k in range(nk):
                for hf in range(2):
                    S.op("pe", lambda e, k=k, hf=hf: e.matmul(PS[banks[hf]].ap, lhsT=slot.ap[:, k, :],
                                                              rhs=rhs_buf.ap[:, k, hf * 512:(hf + 1) * 512],
                                                              start=(k == 0), stop=(k == nk - 1)),
                         reads=[slot.rng(k * 128, (k + 1) * 128), rhs_buf.rng(k * 1024 + hf * 512, k * 1024 + (hf + 1) * 512)],
                         writes=[PS[banks[hf]].all()], inc=(k == nk - 1))

        def run_jobs(jobs, pf=3):
            slots = [None] * len(jobs)
            for i in range(min(pf, len(jobs))):
                slots[i] = wload(jobs[i][0], jobs[i][1])
            for i, (src, nk, consume) in enumerate(jobs):
                if i + pf < len(jobs):
                    slots[i + pf] = wload(jobs[i + pf][0], jobs[i + pf][1])
                banks = ((0, 1), (2, 3), (4, 5), (6, 7))[i % 4]
                consume(slots[i], banks)

        for l in range(DEPTH):
            if l == 1:
                tap("x1", X, [16, 1024], F32)
            rmsnorm(lambda k, l=l: NG.ap[:, l, k:k + 1], True)

            tap("h", H, [16, 1024], BF16, l)
            tt = lambda eng, o, a_, b_, op, rd, wr: S.op(eng, lambda e: e.tensor_tensor(out=o, in0=a_, in1=b_, op=op), reads=rd, writes=wr)

            def trig(Yb, Rb, SNb, CSb):
                S.op("dve", lambda e: e.tensor_scalar(out=Rb.ap, in0=Yb.ap, scalar1=MAGIC, scalar2=MAGIC, op0=ALU.add, op1=ALU.subtract),
                     reads=[Yb.all()], writes=[Rb.all()])
                S.op("dve", lambda e: e.tensor_tensor(out=Rb.ap, in0=Yb.ap, in1=Rb.ap, op=ALU.subtract), reads=[Yb.all(), Rb.all()], writes=[Rb.all()])
                S.op("act", lambda e: e.activation(out=SNb.ap, in_=Rb.ap, func=AF.Sin, scale=TWO_PI), reads=[Rb.all()], writes=[SNb.all()])
                S.op("dve", lambda e: e.tensor_scalar(out=Yb.ap, in0=Yb.ap, scalar1=0.25, scalar2=None, op0=ALU.add), reads=[Yb.all()], writes=[Yb.all()])
                S.op("dve", lambda e: e.tensor_scalar(out=Rb.ap, in0=Yb.ap, scalar1=MAGIC, scalar2=MAGIC, op0=ALU.add, op1=ALU.subtract),
                     reads=[Yb.all(), SNb.all()], writes=[Rb.all()])
                S.op("dve", lambda e: e.tensor_tensor(out=Rb.ap, in0=Yb.ap, in1=Rb.ap, op=ALU.subtract), reads=[Yb.all(), Rb.all()], writes=[Rb.all()])
                S.op("act", lambda e: e.activation(out=CSb.ap, in_=Rb.ap, func=AF.Sin, scale=TWO_PI), reads=[Rb.all()], writes=[CSb.all()])

            RAO.reset()
            ZALL = RAO.alloc([64, 2, 64], BF16)
            RAO.reset()
            WUQ = [RAO.alloc([16, 256], BF16) for _ in range(2)]
            QTA = RAO.alloc([64, 16], F32)
            QTB = RAO.alloc([64, 16], F32)
            LTA = RAO.alloc([64, 8], F32)
            LTB = RAO.alloc([64, 8], F32)
            T2G = RAO.alloc([8, 8, 16], BF16)
            SLAB = []
            for b in range(16):
                sa = RYB.sub(b * 1024, 1024)
                SLAB.append(tuple(sa.alloc([4, 16], F32) for _ in range(4)))
            A = RWB
            A.reset()
            sm = lambda: A.alloc([64], F32)
            DCOL, A1R, A1I, HV1, HV2, HEND = sm(), sm(), sm(), sm(), sm(), sm()
            HENDT = A.alloc([128], F32, parts=64)
            XIK = A.alloc([64], F32)
            XRK = A.alloc([64], F32)
            keep = A.off
            NAT = A.alloc([2, 128], F32, parts=64)
            DNAT = A.alloc([128], F32, parts=64)
            MLD = [(A.alloc([128], F32, parts=64), A.alloc([128], F32, parts=64)) for _ in range(2)]
            ART, AIT, LDT, DTT, XR, XI = sm(), sm(), sm(), sm(), sm(), sm()
            Yp, Rp, SNp, CSp, MGp, MGI = sm(), sm(), sm(), sm(), sm(), sm()
            LBR, LBI, LIR, LII, BR, BI = sm(), sm(), sm(), sm(), sm(), sm()
            E1, E2, E3, E4 = sm(), sm(), sm(), sm()
            id64 = CONST.ap[0:64, 0:64]
            for hh in range(2):
                S.dma("sp", NAT.ap[:, 0, hh * 64:(hh + 1) * 64], are_d[l], writes=[NAT.rng(hh * 64, (hh + 1) * 64)])
                S.dma("sp", NAT.ap[:, 1, hh * 64:(hh + 1) * 64], aim_d[l], writes=[NAT.rng(128 + hh * 64, 128 + (hh + 1) * 64)])
            for j in range(8):
                S.dma("sp", DNAT.ap[:, j * 16:(j + 1) * 16], dsk_d[l], writes=[DNAT.rng(j * 16, (j + 1) * 16)])
            S.dma("sp", LDT.ap, ldt_d[l].partition_broadcast(128), writes=[LDT.all()])
            for i, (src, dst) in enumerate(((NAT.ap[:, 0, :], ART), (NAT.ap[:, 1, :], AIT), (DNAT.ap, DCOL))):
                rd = NAT.all() if i < 2 else DNAT.all()
                S.op("pe", lambda e, src=src: e.transpose(out=PS[6].ap[:, 0:64], in_=src, identity=id64),
                     reads=[rd, CONST.all()], writes=[PS[6].rng(0, 64)])
                S.op("dve", lambda e, dst=dst: e.tensor_copy(out=dst.ap, in_=PS[6].ap[:, 0:64]), reads=[PS[6].rng(0, 64)], writes=[dst.all()])
            S.op("act", lambda e: e.activation(out=DTT.ap, in_=LDT.ap, func=AF.Exp), reads=[LDT.all()], writes=[DTT.all()])
            tt("dve", XR.ap, ART.ap, DTT.ap, ALU.mult, [ART.all(), DTT.all()], [XR.all()])
            tt("dve", XI.ap, AIT.ap, DTT.ap, ALU.mult, [AIT.all(), DTT.all()], [XI.all()])
            S.op("dve", lambda e: e.tensor_scalar(out=Yp.ap, in0=XI.ap, scalar1=1.0 / TWO_PI, scalar2=None, op0=ALU.mult), reads=[XI.all()], writes=[Yp.all()])
            trig(Yp, Rp, SNp, CSp)
            S.op("act", lambda e: e.activation(out=MGp.ap, in_=XR.ap, func=AF.Exp), reads=[XR.all()], writes=[MGp.all()])
            S.op("act", lambda e: e.activation(out=MGI.ap, in_=XR.ap, func=AF.Exp, scale=-1.0), reads=[XR.all()], writes=[MGI.all()])
            tt("dve", LBR.ap, CSp.ap, MGp.ap, ALU.mult, [CSp.all(), MGp.all()], [LBR.all()])
            tt("dve", LBI.ap, SNp.ap, MGp.ap, ALU.mult, [SNp.all(), MGp.all()], [LBI.all()])
            tt("dve", LIR.ap, CSp.ap, MGI.ap, ALU.mult, [CSp.all(), MGI.all()], [LIR.all()])
            S.op("dve", lambda e: e.scalar_tensor_tensor(out=LII.ap, in0=SNp.ap, scalar=-1.0, in1=MGI.ap, op0=ALU.mult, op1=ALU.mult),
                 reads=[SNp.all(), MGI.all()], writes=[LII.all()])
            S.op("dve", lambda e: e.tensor_scalar(out=E1.ap, in0=LBR.ap, scalar1=-1.0, scalar2=None, op0=ALU.add), reads=[LBR.all()], writes=[E1.all()])
            tt("dve", E2.ap, ART.ap, ART.ap, ALU.mult, [ART.all()], [E2.all()])
            tt("dve", E3.ap, AIT.ap, AIT.ap, ALU.mult, [AIT.all()], [E3.all()])
            tt("dve", E2.ap, E2.ap, E3.ap, ALU.add, [E2.all(), E3.all()], [E2.all()])
            S.op("dve", lambda e: e.reciprocal(out=E2.ap, in_=E2.ap), reads=[E2.all()], writes=[E2.all()])
            tt("dve", E3.ap, E1.ap, ART.ap, ALU.mult, [E1.all(), ART.all()], [E3.all()])
            tt("dve", E4.ap, LBI.ap, AIT.ap, ALU.mult, [LBI.all(), AIT.all()], [E4.all()])
            tt("dve", E3.ap, E3.ap, E4.ap, ALU.add, [E3.all(), E4.all()], [E3.all()])
            tt("dve", BR.ap, E3.ap, E2.ap, ALU.mult, [E3.all(), E2.all()], [BR.all()])
            tt("dve", E3.ap, LBI.ap, ART.ap, ALU.mult, [LBI.all(), ART.all()], [E3.all()])
            tt("dve", E4.ap, E1.ap, AIT.ap, ALU.mult, [E1.all(), AIT.all()], [E4.all()])
            tt("dve", E3.ap, E3.ap, E4.ap, ALU.subtract, [E3.all(), E4.all()], [E3.all()])
            tt("dve", BI.ap, E3.ap, E2.ap, ALU.mult, [E3.all(), E2.all()], [BI.all()])

            def cmul_small(ore, oim, are_, aim_, bre_, bim_, rd, wr):
                tt("dve", E3.ap, are_, bre_, ALU.mult, rd, [E3.all()])
                tt("dve", E4.ap, aim_, bim_, ALU.mult, rd, [E4.all()])
                tt("dve", E1.ap, are_, bim_, ALU.mult, rd, [E1.all()])
                tt("dve", E2.ap, aim_, bre_, ALU.mult, rd, [E2.all()])
                tt("dve", ore, E3.ap, E4.ap, ALU.subtract, [E3.all(), E4.all()], wr)
                tt("dve", oim, E1.ap, E2.ap, ALU.add, [E1.all(), E2.all()], wr)
            S.op("dve", lambda e: e.tensor_copy(out=QTA.ap[:, :, 7], in_=BR.ap), reads=[BR.all()], writes=[QTA.all()])
            S.op("dve", lambda e: e.tensor_copy(out=QTB.ap[:, :, 7], in_=BI.ap), reads=[BI.all()], writes=[QTB.all()])
            for j in range(6, -1, -1):
                cmul_small(QTA.ap[:, :, j], QTB.ap[:, :, j], QTA.ap[:, :, j + 1], QTB.ap[:, :, j + 1], LBR.ap, LBI.ap,
                           [QTA.all(), QTB.all(), LBR.all(), LBI.all()], [QTA.all(), QTB.all()])
            cmul_small(QTA.ap[:, :, 8], QTB.ap[:, :, 8], BR.ap, BI.ap, LIR.ap, LII.ap, [BR.all(), BI.all(), LIR.all(), LII.all()], [QTA.all(), QTB.all()])
            for j in range(1, 8):
                cmul_small(QTA.ap[:, :, 8 + j], QTB.ap[:, :, 8 + j], QTA.ap[:, :, 7 + j], QTB.ap[:, :, 7 + j], LIR.ap, LII.ap,
                           [QTA.all(), QTB.all(), LIR.all(), LII.all()], [QTA.all(), QTB.all()])
            S.op("dve", lambda e: e.tensor_copy(out=LTA.ap[:, :, 0], in_=LBR.ap), reads=[LBR.all()], writes=[LTA.all()])
            S.op("dve", lambda e: e.tensor_copy(out=LTB.ap[:, :, 0], in_=LBI.ap), reads=[LBI.all()], writes=[LTB.all()])
            for i in range(1, 8):
                cmul_small(LTA.ap[:, :, i], LTB.ap[:, :, i], LTA.ap[:, :, i - 1], LTB.ap[:, :, i - 1], LBR.ap, LBI.ap,
                           [LTA.all(), LTB.all(), LBR.all(), LBI.all()], [LTA.all(), LTB.all()])
            S.op("dve", lambda e: e.tensor_copy(out=A1R.ap, in_=LTA.ap[:, :, 7]), reads=[LTA.all()], writes=[A1R.all()])
            S.op("dve", lambda e: e.tensor_copy(out=A1I.ap, in_=LTB.ap[:, :, 7]), reads=[LTB.all()], writes=[A1I.all()])
            for _ in range(7):
                tt("dve", E1.ap, A1R.ap, A1R.ap, ALU.mult, [A1R.all()], [E1.all()])
                tt("dve", E3.ap, A1I.ap, A1I.ap, ALU.mult, [A1I.all()], [E3.all()])
                tt("dve", E4.ap, A1R.ap, A1I.ap, ALU.mult, [A1R.all(), A1I.all()], [E4.all()])
                tt("dve", A1R.ap, E1.ap, E3.ap, ALU.subtract, [E1.all(), E3.all()], [A1R.all()])
                S.op("dve", lambda e: e.tensor_scalar(out=A1I.ap, in0=E4.ap, scalar1=2.0, scalar2=None, op0=ALU.mult), reads=[E4.all()], writes=[A1I.all()])
            sgn = ccol(C_SGN)
            S.op("dve", lambda e: e.tensor_scalar(out=A1I.ap, in0=A1I.ap, scalar1=sgn, scalar2=None, op0=ALU.mult), reads=[A1I.all(), CONST.all()], writes=[A1I.all()])
            S.op("dve", lambda e: e.tensor_scalar(out=QTB.ap, in0=QTB.ap, scalar1=sgn, scalar2=None, op0=ALU.mult), reads=[QTB.all(), CONST.all()], writes=[QTB.all()])
            S.op("dve", lambda e: e.tensor_scalar(out=LTA.ap, in0=LTA.ap, scalar1=sgn, scalar2=-1.0, op0=ALU.mult, op1=ALU.mult),
                 reads=[LTA.all(), CONST.all()], writes=[LTA.all()])
            S.op("dve", lambda e: e.tensor_scalar(out=LTB.ap, in0=LTB.ap, scalar1=-1.0, scalar2=None, op0=ALU.mult), reads=[LTB.all()], writes=[LTB.all()])
            S.op("dve", lambda e: e.tensor_copy(out=XIK.ap, in_=XI.ap), reads=[XI.all()], writes=[XIK.all()])
            S.op("dve", lambda e: e.tensor_copy(out=XRK.ap, in_=XR.ap), reads=[XR.all()], writes=[XRK.all()])

            for nq in range(4):
                wq_ = WUQ[nq % 2]
                S.dma("pool", wq_.ap, win_d[l, :, OFF_U + nq * 256:OFF_U + (nq + 1) * 256].rearrange("(k p) n -> p k n", p=128),
                      writes=[wq_.all()])
                for j in range(8):
                    bank = (nq * 8 + j) % 4
                    for k in range(16):
                        S.op("pe", lambda e, j=j, k=k, bank=bank, wq_=wq_: e.matmul(
                            PS[bank].ap[:, 0:256], lhsT=H.ap[:, k, j * 128:(j + 1) * 128], rhs=wq_.ap[:, k, :],
                            start=(k == 0), stop=(k == 15)),
                            reads=[H.rng(k * 1024 + j * 128, k * 1024 + (j + 1) * 128), wq_.rng(k * 256, (k + 1) * 256)],
                            writes=[PS[bank].rng(0, 256)], inc=(k == 15))
                    outap = UT.ap[:, nq * 16:(nq + 1) * 16, j, :]
                    inap = PS[bank].ap[:, 0:256].rearrange("p (g c) -> p g c", c=16)
                    S.op("act", lambda e, o=outap, i=inap: e.copy(out=o, in_=i), reads=[PS[bank].rng(0, 256)],
                         writes=[UT.rng(nq * 16 * 128, (nq + 1) * 16 * 128)])
            tap("ut", UT, [64, 8, 16], BF16, l)
            for gq in range(16):
                bank = 4 + gq % 2
                for gl in range(4):
                    g = gq * 4 + gl
                    S.op("pe", lambda e, g=g, gl=gl, bank=bank: e.transpose(
                        out=psb(bank)[:, gl * 128:(gl + 1) * 128], in_=UT.ap[:, g].rearrange("p j c -> p (j c)"), identity=IDB.ap),
                        reads=[UT.rng(g * 128, (g + 1) * 128), IDB.all()], writes=[PS[bank].rng(gl * 64, (gl + 1) * 64)], inc=(gl == 3))
                outap = UBLK.ap[:, gq * 4:(gq + 1) * 4, :].rearrange("p g t -> p (g t)")
                inap = psb(bank)[:, 0:512]
                S.op("act", lambda e, o=outap, i=inap: e.copy(out=o, in_=i), reads=[PS[bank].all()], writes=[UBLK.rng(gq * 512, (gq + 1) * 512)])

            for b in range(16):
                g0 = b * 4
                BT1, BT2, CT1, CT2 = SLAB[b]
                S.dma("sp", BT1.ap[0:64], bre_d[l, g0:g0 + 4].rearrange("g p c -> p g c"), writes=[BT1.all()])
                S.dma("sp", BT1.ap[64:128], bim_d[l, g0:g0 + 4].rearrange("g p c -> p g c"), writes=[BT1.all()])
                S.dma("sp", BT2.ap[0:64], bim_d[l, g0:g0 + 4].rearrange("g p c -> p g c"), writes=[BT2.all()])
                S.dma("sp", BT2.ap[64:128], bre_d[l, g0:g0 + 4].rearrange("g p c -> p g c"), writes=[BT2.all()])
            ARN = [RUT, RT, RWB.sub(keep, RWB.nbytes - keep)]
            for a_ in ARN:
                a_.reset()

            def palloc(shape, dt, parts=128):
                for a_ in ARN:
                    esz = 4 if dt == F32 else 2
                    n = 1
                    for s_ in shape:
                        n *= s_
                    if a_.off + (n * esz + 31) // 32 * 32 <= a_.nbytes:
                        return a_.alloc(shape, dt, parts)
                raise AssertionError("pass arena overflow")

            def run_pipelined(genf, n, depth=2):
                active = []
                nxt = 0
                while nxt < n or active:
                    if nxt < n and len(active) < depth:
                        active.append(genf(nxt))
                        nxt += 1
                    for g_ in list(active):
                        try:
                            next(g_)
                        except StopIteration:
                            active.remove(g_)

            def tables(g0, bank, cw, ct, neg_im, DG, Yv, Rv, SNv, CSv, WRE, WIM):
                i64b = CONST.ap[0:64, 0:64].unsqueeze(1).broadcast_to([64, 4, 64])
                tt("dve", DG.ap[:, 0], i64b, XIK.ap[0:64, g0:g0 + 4].unsqueeze(2).broadcast_to([64, 4, 64]), ALU.mult, [CONST.all(), XIK.all()], [DG.rng(0, 256)])
                tt("pool", DG.ap[:, 1], i64b, XRK.ap[0:64, g0:g0 + 4].unsqueeze(2).broadcast_to([64, 4, 64]), ALU.mult, [CONST.all(), XRK.all()], [DG.rng(256, 512)])
                S.op("pe", lambda e: e.matmul(PS[bank].ap, lhsT=ONF.ap[0:64, :], rhs=DG.ap.rearrange("p a g q -> p (a g q)"), start=True, stop=True),
                     reads=[DG.all(), ONF.all()], writes=[PS[bank].all()])
                aimv = PS[bank].ap[:, 0:256].rearrange("p (g q) -> p g q", q=64)
                arev = PS[bank].ap[:, 256:512].rearrange("p (g q) -> p g q", q=64)
                S.op("dve", lambda e: e.tensor_scalar(out=Yv.ap, in0=aimv, scalar1=ccol(cw), scalar2=None, op0=ALU.mult),
                     reads=[PS[bank].all(), CONST.all()], writes=[Yv.all()])
                MGv = WRE
                S.op("act", lambda e: e.activation(out=MGv.ap, in_=arev, func=AF.Exp, scale=ccol(ct)),
                     reads=[PS[bank].all(), CONST.all()], writes=[MGv.all()])
                trig(Yv, Rv, SNv, CSv)
                if neg_im:
                    S.op("dve", lambda e: e.scalar_tensor_tensor(out=WIM.ap, in0=SNv.ap, scalar=-1.0, in1=MGv.ap, op0=ALU.mult, op1=ALU.mult),
                         reads=[SNv.all(), MGv.all()], writes=[WIM.all()])
                else:
                    tt("dve", WIM.ap, SNv.ap, MGv.ap, ALU.mult, [SNv.all(), MGv.all()], [WIM.all()])
                tt("dve", WRE.ap, CSv.ap, MGv.ap, ALU.mult, [CSv.all(), MGv.all()], [WRE.all()])

            def cmul(dre, dim_, drd, Z1, Z2, wre, wim, xre, xim, xrd):
                tt("dve", Z1.ap, wre.ap, xre, ALU.mult, [wre.all()] + xrd, [Z1.all()])
                tt("dve", Z2.ap, wim.ap, xim, ALU.mult, [wim.all()] + xrd, [Z2.all()])
                tt("pool", dre, Z1.ap, Z2.ap, ALU.subtract, [Z1.all(), Z2.all()], drd)
                tt("dve", Z1.ap, wre.ap, xim, ALU.mult, [wre.all()] + xrd, [Z1.all()])
                tt("dve", Z2.ap, wim.ap, xre, ALU.mult, [wim.all()] + xrd, [Z2.all()])
                tt("pool", dim_, Z1.ap, Z2.ap, ALU.add, [Z1.all(), Z2.all()], drd)

            def emit_ct(b):
                g0 = b * 4
                M1, M2 = MLD[b % 2]
                CT1, CT2 = SLAB[b][2], SLAB[b][3]
                cv_re = cre_d[l, g0:g0 + 4].rearrange("g c p -> (g c) p")
                cv_im = cim_d[l, g0:g0 + 4].rearrange("g c p -> (g c) p")
                S.dma("sp", M1.ap[:, 0:64], cv_re, writes=[M1.rng(0, 64)])
                S.dma("sp", M1.ap[:, 64:128], cv_im, writes=[M1.rng(64, 128)])
                S.dma("sp", M2.ap[:, 0:64], cv_im, writes=[M2.rng(0, 64)])
                S.dma("sp", M2.ap[:, 64:128], cv_re, writes=[M2.rng(64, 128)])
                bk = 4 + b % 2
                for (mld, ct, off) in ((M1, CT1, 0), (M2, CT2, 64)):
                    S.op("pe", lambda e, mld=mld, off=off, bk=bk: e.transpose(out=PS[bk].ap[:, off:off + 64], in_=mld.ap, identity=id64),
                         reads=[mld.all(), CONST.all()], writes=[PS[bk].rng(off, off + 64)])
                    S.op("act", lambda e, ct=ct, off=off, bk=bk: e.copy(out=ct.ap.rearrange("p g c -> p (g c)"), in_=PS[bk].ap[:, off:off + 64]),
                         reads=[PS[bk].rng(off, off + 64)], writes=[ct.all()])

            Yv, Rv, SNv, CSv = (palloc([4, 64], F32) for _ in range(4))
            FS = []
            NFS = 3
            fT1, fT2, fDG = palloc([4, 8, 16], F32), palloc([4, 8, 16], F32), palloc([2, 4, 64], F32, parts=64)
            for s_ in range(NFS):
                FS.append(dict(T1=fT1, T2=fT2, BNAT=palloc([4, 128], BF16), BBR=palloc([4, 128], BF16),
                               Z1=palloc([4, 64], F32), Z2=palloc([4, 64], F32), DG=fDG,
                               WRE=palloc([4, 64], F32), WIM=palloc([4, 64], F32)))

            def front_batch(b):
                emit_ct(b)
                g0 = b * 4
                f = FS[b % NFS]
                T1, T2, BNAT, BBR, Z1, Z2, DG, WRE, WIM = (f[k] for k in ("T1", "T2", "BNAT", "BBR", "Z1", "Z2", "DG", "WRE", "WIM"))
                BT1, BT2 = SLAB[b][0], SLAB[b][1]
                bt1b = BT1.ap.unsqueeze(2).broadcast_to([128, 4, 8, 16])
                bt2b = BT2.ap.unsqueeze(2).broadcast_to([128, 4, 8, 16])
                qa = QTA.ap[:, g0:g0 + 4, 0:8].unsqueeze(3).broadcast_to([128, 4, 8, 16])
                qb = QTB.ap[:, g0:g0 + 4, 0:8].unsqueeze(3).broadcast_to([128, 4, 8, 16])
                tt("dve", T1.ap, bt1b, qa, ALU.mult, [BT1.all(), QTA.all()], [T1.all()])
                tt("pool", T2.ap, bt2b, qb, ALU.mult, [BT2.all(), QTB.all()], [T2.all()])
                tt("dve", BNAT.ap.rearrange("p g (j c) -> p g j c", c=16), T1.ap, T2.ap, ALU.add, [T1.all(), T2.all()], [BNAT.all()])
                yield
                bk = 4 + b % 2
                for gl in range(4):
                    S.op("pe", lambda e, gl=gl: e.transpose(out=psb(bk)[:, gl * 128:(gl + 1) * 128], in_=BNAT.ap[:, gl, :], identity=IDB.ap),
                         reads=[BNAT.rng(gl * 128, (gl + 1) * 128), IDB.all()], writes=[PS[bk].rng(gl * 64, (gl + 1) * 64)], inc=(gl == 3))
                S.op("act", lambda e: e.copy(out=BBR.ap.rearrange("p g t -> p (g t)"), in_=psb(bk)[:, 0:512]), reads=[PS[bk].all()], writes=[BBR.all()])
                tables(g0, b % 2, C_WM, C_TM, True, DG, Yv, Rv, SNv, CSv, WRE, WIM)
                yield
                sb_ = 2 + b % 2
                for gl in range(4):
                    g = g0 + gl
                    S.op("pe", lambda e, gl=gl, g=g: e.matmul(PS[sb_].ap[:, gl * 128:(gl + 1) * 128], lhsT=UBLK.ap[:, g, :], rhs=BBR.ap[:, gl, :],
                                                              start=True, stop=True),
                         reads=[UBLK.rng(g * 128, (g + 1) * 128), BBR.rng(gl * 128, (gl + 1) * 128)],
                         writes=[PS[sb_].rng(gl * 128, (gl + 1) * 128)], inc=(gl == 3))
                yield
                sv = PS[sb_].ap.rearrange("p (g r q) -> p g r q", r=2, q=64)
                zw = [ZALL.rng(g0 * 128, (g0 + 4) * 128)]
                cmul(ZALL.ap[:, g0:g0 + 4, 0, :], ZALL.ap[:, g0:g0 + 4, 1, :], zw, Z1, Z2, WRE, WIM, sv[:, :, 0, :], sv[:, :, 1, :], [PS[sb_].all()])
                yield
                for gl in range(4):
                    g = g0 + gl
                    S.op("pe", lambda e, g=g: e.matmul(PS[7].ap[:, g:g + 1], lhsT=ZALL.ap[:, g].rearrange("p r q -> p (r q)"), rhs=ONB.ap[:, 0:1],
                                                       start=True, stop=True),
                         reads=[ZALL.rng(g * 128, (g + 1) * 128), ONB.all()], writes=[PS[7].rng(g, g + 1)], inc=(gl == 3))
            run_pipelined(front_batch, 16, depth=NFS)
            S.op("dve", lambda e: e.tensor_copy(out=HV1.ap, in_=PS[7].ap[:, 0:64]), reads=[PS[7].rng(0, 64)], writes=[HV1.all()])
            S.op("pe", lambda e: e.matmul(PS[6].ap[:, 0:64], lhsT=swapm, rhs=HV1.ap, start=True, stop=True),
                 reads=[HV1.all(), CONST.all()], writes=[PS[6].rng(0, 64)])
            tt("dve", HV2.ap, PS[6].ap[:, 0:64], A1I.ap, ALU.mult, [PS[6].rng(0, 64), A1I.all()], [HV2.all()])
            tt("dve", HV1.ap, HV1.ap, A1R.ap, ALU.mult, [HV1.all(), A1R.all()], [HV1.all()])
            tt("dve", HEND.ap, HV1.ap, HV2.ap, ALU.add, [HV1.all(), HV2.all()], [HEND.all()])
            S.op("pe", lambda e: e.transpose(out=PS[6].ap[0:64, 128:256], in_=HEND.ap, identity=ident),
                 reads=[HEND.all(), CONST.all()], writes=[PS[6].rng(128, 256)])
            S.op("dve", lambda e: e.tensor_copy(out=HENDT.ap, in_=PS[6].ap[0:64, 128:256]), reads=[PS[6].rng(128, 256)], writes=[HENDT.all()])
            S.dma("sp", cc1in[l].ap()[:, :], HENDT.ap, reads=[HENDT.all()], writes=[Acc("cc1in%d" % l, 0, 1)])
            S.op("pool", lambda e, l=l: e.collective_compute("AllGather", ALU.bypass, replica_groups=[[0, 1], [2, 3], [4, 5], [6, 7]],
                                                             ins=[cc1in[l].ap().opt()], outs=[cc1out[l].ap().opt()]),
                 reads=[Acc("cc1in%d" % l, 0, 1)], writes=[Acc("cc1out%d" % l, 0, 1)], custom_sem=("cc1", l))

            for a_ in ARN:
                a_.reset()
            Yv, Rv, SNv, CSv = (palloc([4, 64], F32) for _ in range(4))
            T1 = palloc([4, 8, 16], F32)
            T2 = palloc([4, 8, 16], F32)
            XG = palloc([4, 128], F32)
            CBR = palloc([4, 128], F32)
            DG = palloc([2, 4, 64], F32, parts=64)
            PF = palloc([4, 2, 64], F32)
            Z1 = palloc([4, 64], F32)
            Z2 = palloc([4, 64], F32)
            HT = palloc([4, 2, 64], BF16)
            HP = palloc([4, 128], BF16)
            BS = []
            for s_ in range(2):
                BS.append(dict(CBRb=palloc([4, 128], BF16), A0R=palloc([4, 128], BF16), WRE=palloc([4, 64], F32), WIM=palloc([4, 64], F32),
                               HINB=palloc([4, 2, 64], F32)))

            def back_batch(b):
                g0 = b * 4
                f = BS[b % 2]
                CBRb, A0R, WRE, WIM, HINB = (f[k] for k in ("CBRb", "A0R", "WRE", "WIM", "HINB"))
                BT1, BT2, CT1, CT2 = SLAB[b]
                S.dma("sp", HINB.ap.rearrange("p g r q -> p (g r q)"), cc1out[l].ap()[g0:g0 + 4, :].rearrange("g n -> (g n)").partition_broadcast(128),
                      reads=[Acc("cc1out%d" % l, 0, 1)], writes=[HINB.all()])
                bt1b = BT1.ap.unsqueeze(2).broadcast_to([128, 4, 8, 16])
                bt2b = BT2.ap.unsqueeze(2).broadcast_to([128, 4, 8, 16])
                qa2 = QTA.ap[:, g0:g0 + 4, 8:16].unsqueeze(3).broadcast_to([128, 4, 8, 16])
                qb2 = QTB.ap[:, g0:g0 + 4, 8:16].unsqueeze(3).broadcast_to([128, 4, 8, 16])
                tt("dve", T1.ap, bt1b, qa2, ALU.mult, [BT1.all(), QTA.all()], [T1.all()])
                tt("pool", T2.ap, bt2b, qb2, ALU.mult, [BT2.all(), QTB.all()], [T2.all()])
                tt("dve", XG.ap.rearrange("p g (j c) -> p g j c", c=16), T1.ap, T2.ap, ALU.add, [T1.all(), T2.all()], [XG.all()])
                ct1b = CT1.ap.unsqueeze(2).broadcast_to([128, 4, 8, 16])
                ct2b = CT2.ap.unsqueeze(2).broadcast_to([128, 4, 8, 16])
                la = LTA.ap[:, g0:g0 + 4, :].unsqueeze(3).broadcast_to([128, 4, 8, 16])
                lb = LTB.ap[:, g0:g0 + 4, :].unsqueeze(3).broadcast_to([128, 4, 8, 16])
                tt("dve", T1.ap, ct1b, la, ALU.mult, [CT1.all(), LTA.all(), XG.all()], [T1.all()])
                tt("pool", T2.ap, ct2b, lb, ALU.mult, [CT2.all(), LTB.all(), XG.all()], [T2.all()])
                tt("dve", CBR.ap.rearrange("p g (j c) -> p g j c", c=16), T1.ap, T2.ap, ALU.add, [T1.all(), T2.all()], [CBR.all()])
                S.op("act", lambda e: e.copy(out=CBRb.ap, in_=CBR.ap), reads=[CBR.all()], writes=[CBRb.all()])
                yield
                ab = 2 + b % 2
                for gl in range(4):
                    S.op("pe", lambda e, gl=gl: e.matmul(PS[ab].ap[:, gl * 128:(gl + 1) * 128], lhsT=XG.ap[:, gl, :], rhs=CBR.ap[:, gl, :],
                                                         start=True, stop=True),
                         reads=[XG.rng(gl * 128, (gl + 1) * 128), CBR.rng(gl * 128, (gl + 1) * 128)],
                         writes=[PS[ab].rng(gl * 128, (gl + 1) * 128)], inc=(gl == 3))
                tables(g0, b % 2, C_WP, C_TP, False, DG, Yv, Rv, SNv, CSv, WRE, WIM)
                yield
                t3v = T1.ap.rearrange("p g j c -> p g (j c)")
                tt("dve", t3v, PS[ab].ap.rearrange("p (g n) -> p g n", n=128), a0mask.unsqueeze(1).broadcast_to([128, 4, 128]),
                   ALU.mult, [PS[ab].all(), CONST.all(), CBR.all()], [T1.all()])
                for gl in range(4):
                    g = g0 + gl
                    S.op("dve", lambda e, gl=gl, g=g: e.scalar_tensor_tensor(out=A0R.ap[:, gl, :], in0=ident, scalar=DCOL.ap[:, g:g + 1],
                                                                             in1=t3v[:, gl, :], op0=ALU.mult, op1=ALU.add),
                         reads=[CONST.all(), DCOL.all(), T1.all()], writes=[A0R.rng(gl * 128, (gl + 1) * 128)])
                S.op("pe", lambda e: e.matmul(PS[4].ap, lhsT=LSB.ap, rhs=ZALL.ap[:, g0:g0 + 4].rearrange("p g r q -> p (g r q)"), start=True, stop=True),
                     reads=[ZALL.rng(g0 * 128, (g0 + 4) * 128), LSB.all()], writes=[PS[4].all()])
                yield
                pfv = PF.ap
                S.op("dve", lambda e: e.scalar_tensor_tensor(out=PF.ap.rearrange("p g r q -> p (g r q)"), in0=HINB.ap.rearrange("p g r q -> p (g r q)"),
                                                             scalar=MCOL.ap[:, 0:1], in1=PS[4].ap, op0=ALU.mult, op1=ALU.add),
                     reads=[HINB.all(), MCOL.all(), PS[4].all()], writes=[PF.all()])
                cmul(HT.ap[:, :, 0, :], HT.ap[:, :, 1, :], [HT.all()], Z1, Z2, WRE, WIM, pfv[:, :, 0, :], pfv[:, :, 1, :], [PF.all()])
                yield
                for gl in range(4):
                    S.op("pe", lambda e, gl=gl: e.transpose(out=psb(5)[:, gl * 128:(gl + 1) * 128], in_=HT.ap[:, gl].rearrange("p r q -> p (r q)"), identity=IDB.ap),
                         reads=[HT.rng(gl * 128, (gl + 1) * 128), IDB.all()], writes=[PS[5].rng(gl * 64, (gl + 1) * 64)], inc=(gl == 3))
                S.op("act", lambda e: e.copy(out=HP.ap.rearrange("p g t -> p (g t)"), in_=psb(5)[:, 0:512]), reads=[PS[5].all()], writes=[HP.all()])
                yield
                for gl in range(4):
                    g = g0 + gl
                    S.op("pe", lambda e, gl=gl: e.matmul(PS[6].ap[:, gl * 128:(gl + 1) * 128], lhsT=HP.ap[:, gl, :], rhs=CBRb.ap[:, gl, :],
                                                         start=True, stop=False),
                         reads=[HP.rng(gl * 128, (gl + 1) * 128), CBRb.rng(gl * 128, (gl + 1) * 128)],
                         writes=[PS[6].rng(gl * 128, (gl + 1) * 128)], inc=False)
                    S.op("pe", lambda e, gl=gl, g=g: e.matmul(PS[6].ap[:, gl * 128:(gl + 1) * 128], lhsT=UBLK.ap[:, g, :], rhs=A0R.ap[:, gl, :],
                                                              start=False, stop=True),
                         reads=[UBLK.rng(g * 128, (g + 1) * 128), A0R.rng(gl * 128, (gl + 1) * 128)],
                         writes=[PS[6].rng(gl * 128, (gl + 1) * 128)], inc=True)
                hb = b % 2
                S.op("act", lambda e: e.activation(out=T2G.ap[:, :, hb * 4:(hb + 1) * 4, :],
                                                   in_=PS[6].ap.rearrange("p (g i c) -> p i g c", g=4, i=8), func=AF.Gelu_apprx_tanh),
                     reads=[PS[6].all()], writes=[T2G.all()])
                if hb == 1:
                    yield
                    q = b // 2
                    for i in range(8):
                        S.op("pe", lambda e, i=i: e.transpose(out=psb(7)[:, i * 128:(i + 1) * 128], in_=T2G.ap[:, i].rearrange("p g c -> p (g c)"), identity=IDB.ap),
                             reads=[T2G.rng(i * 128, (i + 1) * 128), IDB.all()], writes=[PS[7].rng(i * 64, (i + 1) * 64)], inc=(i == 7))
                    S.op("dve", lambda e: e.tensor_copy(out=YB.ap[:, q, :], in_=psb(7)), reads=[PS[7].all()], writes=[YB.rng(q * 1024, (q + 1) * 1024)])
            run_pipelined(back_batch, 16)

            tap("yb", YB, [8, 1024], BF16, l)
            tap("zall", ZALL, [64, 2, 64], BF16, l)
            RUT.reset()
            VT = RUT.alloc([1024], F32)
            CV = RUT.alloc([1024], F32)
            ACC = RUT.alloc([1024], F32)
            SZ = RUT.alloc([1024], F32)
            jobs = []
            for c in range(16):
                def mk_v(c):
                    def f(slot, banks):
                        proj(slot, 16, H, banks)
                        for hf in range(2):
                            S.op("act", lambda e, hf=hf: e.copy(out=VT.ap[:, hf * 512:(hf + 1) * 512], in_=PS[banks[hf]].ap),
                                 reads=[PS[banks[hf]].all()], writes=[VT.rng(hf * 512, (hf + 1) * 512)])
                    return f

                def mk_cg(c):
                    def f(slot, banks):
                        proj(slot, 16, H, banks)
                        for hf in range(2):
                            S.op("dve", lambda e, hf=hf: e.tensor_tensor(out=CV.ap[:, hf * 512:(hf + 1) * 512], in0=PS[banks[hf]].ap,
                                                                         in1=VT.ap[:, hf * 512:(hf + 1) * 512], op=ALU.mult),
                                 reads=[PS[banks[hf]].all(), VT.rng(hf * 512, (hf + 1) * 512)], writes=[CV.rng(hf * 512, (hf + 1) * 512)])
                        w = lambda t, l=l: CW.ap[:, l, t, c:c + 1]
                        cv3 = CV.ap.rearrange("p (j t) -> p j t", t=128)
                        ac3 = ACC.ap.rearrange("p (j t) -> p j t", t=128)
                        S.op("dve", lambda e: e.tensor_scalar(out=ACC.ap, in0=CV.ap, scalar1=w(2), scalar2=None, op0=ALU.mult),
                             reads=[CV.all(), CW.all()], writes=[ACC.all()])
                        for (o, i, t) in ((ac3[:, 1:8, :], cv3[:, 0:7, :], 1), (ac3[:, 0, 1:128], cv3[:, 7, 0:127], 1),
                                          (ac3[:, 2:8, :], cv3[:, 0:6, :], 0), (ac3[:, 0:2, 1:128], cv3[:, 6:8, 0:127], 0)):
                            S.op("dve", lambda e, o=o, i=i, t=t: e.scalar_tensor_tensor(out=o, in0=i, scalar=w(t), in1=o, op0=ALU.mult, op1=ALU.add),
                                 reads=[CV.all(), ACC.all(), CW.all()], writes=[ACC.all()])
                        S.op("act", lambda e: e.copy(out=TAIL.ap[:, c, :], in_=cv3[:, 6:8, 127]), reads=[CV.all()], writes=[TAIL.rng(c * 2, c * 2 + 2)])
                    return f

                def mk_bg(c):
                    def f(slot, banks):
                        proj(slot, 16, H, banks)
                        S.op("act", lambda e: e.copy(out=BGT.ap, in_=PS[banks[0]].ap.rearrange("p (j t) -> p j t", t=128)[:, 0:2, 0]),
                             reads=[PS[banks[0]].all()], writes=[BGT.all()])
                        for hf in range(2):
                            S.op("dve", lambda e, hf=hf: e.tensor_tensor(out=VT.ap[:, hf * 512:(hf + 1) * 512], in0=PS[banks[hf]].ap,
                                                                         in1=ACC.ap[:, hf * 512:(hf + 1) * 512], op=ALU.mult),
                                 reads=[PS[banks[hf]].all(), ACC.rng(hf * 512, (hf + 1) * 512)], writes=[VT.rng(hf * 512, (hf + 1) * 512)])
                    return f

                def mk_za(c):
                    def f(slot, banks):
                        proj(slot, 16, H, banks)
                        for hf in range(2):
                            S.op("act", lambda e, hf=hf: e.activation(out=SZ.ap[:, hf * 512:(hf + 1) * 512], in_=PS[banks[hf]].ap, func=AF.Silu),
                                 reads=[PS[banks[hf]].all()], writes=[SZ.rng(hf * 512, (hf + 1) * 512)])
                        S.op("dve", lambda e: e.tensor_tensor(out=GATE01.ap[:, c, :], in0=BGT.ap, in1=SZ.ap.rearrange("p (j t) -> p j t", t=128)[:, 0:2, 0], op=ALU.mult),
                             reads=[BGT.all(), SZ.all()], writes=[GATE01.rng(c * 2, c * 2 + 2)])
                        S.op("dve", lambda e: e.tensor_tensor(out=AO.ap[:, c, :], in0=VT.ap, in1=SZ.ap, op=ALU.mult),
                             reads=[VT.all(), SZ.all()], writes=[AO.rng(c * 1024, (c + 1) * 1024)])
                    return f
                jobs.append((win_d[l, :, OFF_V + c * 128:OFF_V + (c + 1) * 128], 16, mk_v(c)))
                jobs.append((win_d[l, :, OFF_CG + c * 128:OFF_CG + (c + 1) * 128], 16, mk_cg(c)))
                jobs.append((win_d[l, :, OFF_BG + c * 128:OFF_BG + (c + 1) * 128], 16, mk_bg(c)))
                jobs.append((win_d[l, :, OFF_ZA + c * 128:OFF_ZA + (c + 1) * 128], 16, mk_za(c)))
            run_jobs(jobs)

            S.dma("sp", cc2in[l].ap()[:, :].rearrange("r (a j) -> (r a) j", j=32), TAIL.ap.rearrange("p c t -> p (c t)"),
                  reads=[TAIL.all()], writes=[Acc("cc2in%d" % l, 0, 1)])
            S.op("pool", lambda e, l=l: e.collective_compute("AllGather", ALU.bypass, replica_groups=[[0, 1], [2, 3], [4, 5], [6, 7]],
                                                             ins=[cc2in[l].ap().opt()], outs=[cc2out[l].ap().opt()]),
                 reads=[Acc("cc2in%d" % l, 0, 1)], writes=[Acc("cc2out%d" % l, 0, 1)], custom_sem=("cc2", l))
            S.dma("sp", TAILP.ap.rearrange("p c t -> p (c t)"), cc2out[l].ap()[0:32, :].rearrange("r (a j) -> (r a) j", j=32),
                  reads=[Acc("cc2out%d" % l, 0, 1)], writes=[TAILP.all()])
            S.op("dve", lambda e: e.tensor_scalar(out=TAILP.ap, in0=TAILP.ap, scalar1=MCOL.ap[:, 0:1], scalar2=None, op0=ALU.mult),
                 reads=[TAILP.all(), MCOL.all()], writes=[TAILP.all()])
            tts = lambda o, a_, b_, op, rd, wr: S.op("dve", lambda e: e.tensor_tensor(out=o, in0=a_, in1=b_, op=op), reads=rd, writes=wr)
            cm1, cm2 = TAILP.ap[:, :, 1], TAILP.ap[:, :, 0]
            w0, w1 = CW.ap[:, l, 0, :], CW.ap[:, l, 1, :]
            tts(DL.ap[:, 0, :], cm1, w1, ALU.mult, [TAILP.all(), CW.all()], [DL.rng(0, 16)])
            tts(DL.ap[:, 1, :], cm2, w0, ALU.mult, [TAILP.all(), CW.all()], [DL.rng(16, 32)])
            tts(DL.ap[:, 0, :], DL.ap[:, 0, :], DL.ap[:, 1, :], ALU.add, [DL.rng(0, 32)], [DL.rng(0, 16)])
            tts(DL.ap[:, 0, :], DL.ap[:, 0, :], GATE01.ap[:, :, 0], ALU.mult, [DL.rng(0, 16), GATE01.all()], [DL.rng(0, 16)])
            tts(DL.ap[:, 2, :], cm1, w0, ALU.mult, [TAILP.all(), CW.all()], [DL.rng(32, 48)])
            tts(DL.ap[:, 2, :], DL.ap[:, 2, :], GATE01.ap[:, :, 1], ALU.mult, [DL.rng(32, 48), GATE01.all()], [DL.rng(32, 48)])
            tts(AO.ap[:, :, 0], AO.ap[:, :, 0], DL.ap[:, 0, :], ALU.add, [AO.all(), DL.rng(0, 16)], [AO.all()])
            tts(AO.ap[:, :, 128], AO.ap[:, :, 128], DL.ap[:, 2, :], ALU.add, [AO.all(), DL.rng(32, 48)], [AO.all()])

            tap("ao", AO, [16, 1024], BF16, l)
            RT.reset()
            TA = RT.alloc([1024], F32)
            TB = RT.alloc([1024], F32)
            jobs = []
            for eo in range(8):
                def mk_glu(eo):
                    def f(slot, banks):
                        proj(slot, 8, YB, banks)
                        for hf in range(2):
                            S.op("act", lambda e, hf=hf, l=l: e.activation(out=TA.ap[:, hf * 512:(hf + 1) * 512], in_=PS[banks[hf]].ap, func=AF.Sigmoid,
                                                                           bias=BGL.ap[:, l, eo:eo + 1]),
                                 reads=[PS[banks[hf]].all(), BGL.all()], writes=[TA.rng(hf * 512, (hf + 1) * 512)])
                    return f

                def mk_zb(eo):
                    def f(slot, banks):
                        proj(slot, 16, H, banks)
                        for hf in range(2):
                            S.op("act", lambda e, hf=hf: e.activation(out=TB.ap[:, hf * 512:(hf + 1) * 512], in_=PS[banks[hf]].ap, func=AF.Silu),
                                 reads=[PS[banks[hf]].all()], writes=[TB.rng(hf * 512, (hf + 1) * 512)])
                        S.op("dve", lambda e: e.tensor_tensor(out=GATE.ap[:, eo, :], in0=TA.ap, in1=TB.ap, op=ALU.mult),
                             reads=[TA.all(), TB.all()], writes=[GATE.rng(eo * 1024, (eo + 1) * 1024)])
                        if eo == 7:
                            for e2 in range(8):
                                S.op("dve", lambda e, e2=e2: e.tensor_tensor(out=YB.ap[:, e2, :], in0=YB.ap[:, e2, :], in1=GATE.ap[:, e2, :], op=ALU.mult),
                                     reads=[YB.rng(e2 * 1024, (e2 + 1) * 1024), GATE.rng(e2 * 1024, (e2 + 1) * 1024)], writes=[YB.rng(e2 * 1024, (e2 + 1) * 1024)])
                            tap("yb2", YB, [8, 1024], BF16, l)
                    return f
                jobs.append((wglu_d[l, :, eo * 128:(eo + 1) * 128], 8, mk_glu(eo)))
                jobs.append((win_d[l, :, OFF_ZB + eo * 128:OFF_ZB + (eo + 1) * 128], 16, mk_zb(eo)))
            for c in range(16):
                def mk_g(c, dst):
                    def f(slot, banks):
                        proj(slot, 16, H, banks)
                        for hf in range(2):
                            S.op("act", lambda e, hf=hf: e.activation(out=dst.ap[:, hf * 512:(hf + 1) * 512], in_=PS[banks[hf]].ap, func=AF.Sigmoid),
                                 reads=[PS[banks[hf]].all()], writes=[dst.rng(hf * 512, (hf + 1) * 512)])
                    return f

                def mk_wa(c):
                    def f(slot, banks):
                        proj(slot, 16, AO, banks)
                        for hf in range(2):
                            S.op("dve", lambda e, hf=hf: e.tensor_tensor(out=TA.ap[:, hf * 512:(hf + 1) * 512], in0=PS[banks[hf]].ap,
                                                                         in1=TA.ap[:, hf * 512:(hf + 1) * 512], op=ALU.mult),
                                 reads=[PS[banks[hf]].all(), TA.rng(hf * 512, (hf + 1) * 512)], writes=[TA.rng(hf * 512, (hf + 1) * 512)])
                    return f

                def mk_wb(c):
                    def f(slot, banks):
                        proj(slot, 8, YB, banks)
                        for hf in range(2):
                            S.op("dve", lambda e, hf=hf: e.tensor_tensor(out=TB.ap[:, hf * 512:(hf + 1) * 512], in0=PS[banks[hf]].ap,
                                                                         in1=TB.ap[:, hf * 512:(hf + 1) * 512], op=ALU.mult),
                                 reads=[PS[banks[hf]].all(), TB.rng(hf * 512, (hf + 1) * 512)], writes=[TB.rng(hf * 512, (hf + 1) * 512)])
                        S.op("dve", lambda e: e.tensor_tensor(out=M.ap[:, c, :], in0=TA.ap, in1=TB.ap, op=ALU.add),
                             reads=[TA.all(), TB.all()], writes=[M.rng(c * 1024, (c + 1) * 1024)])
                    return f
                jobs.append((win_d[l, :, OFF_GA + c * 128:OFF_GA + (c + 1) * 128], 16, mk_g(c, TA)))
                jobs.append((wa_d[l, :, c * 128:(c + 1) * 128], 16, mk_wa(c)))
                jobs.append((win_d[l, :, OFF_GB + c * 128:OFF_GB + (c + 1) * 128], 16, mk_g(c, TB)))
                jobs.append((wb_d[l, :, c * 128:(c + 1) * 128], 8, mk_wb(c)))
            for c in range(16):
                def mk_wo(c):
                    def f(slot, banks):
                        proj(slot, 16, M, banks)
                        for hf in range(2):
                            S.op("dve", lambda e, hf=hf: e.tensor_tensor(out=X.ap[:, c, hf * 512:(hf + 1) * 512], in0=X.ap[:, c, hf * 512:(hf + 1) * 512],
                                                                         in1=PS[banks[hf]].ap, op=ALU.add),
                                 reads=[PS[banks[hf]].all(), X.rng(c * 1024 + hf * 512, c * 1024 + (hf + 1) * 512)],
                                 writes=[X.rng(c * 1024 + hf * 512, c * 1024 + (hf + 1) * 512)])
                    return f
                jobs.append((wo_d[l, :, c * 128:(c + 1) * 128], 16, mk_wo(c)))
            run_jobs(jobs)

        tap("xf", X, [16, 1024], F32)
        rmsnorm(lambda k: FG.ap[:, k:k + 1], False)
        RAO.reset()
        OST = [RAO.alloc([2048], F32) for _ in range(2)]
        ov = out_d.rearrange("(t j) d -> j t d", j=8)
        out_events = []
        for j in range(8):
            st = OST[j % 2]
            for kq in range(4):
                bank = (j * 4 + kq) % 4
                for kk in range(4):
                    k = kq * 4 + kk
                    S.op("pe", lambda e, k=k, kk=kk, bank=bank, j=j: e.transpose(
                        out=PS[bank].ap[:, kk * 128:(kk + 1) * 128], in_=X.ap[:, k, j * 128:(j + 1) * 128], identity=ident),
                        reads=[X.rng(k * 1024 + j * 128, k * 1024 + (j + 1) * 128), CONST.all()],
                        writes=[PS[bank].rng(kk * 128, (kk + 1) * 128)], inc=(kk == 3))
                eng = "act" if kq % 2 == 0 else "dve"
                o = st.ap[:, kq * 512:(kq + 1) * 512]
                if eng == "act":
                    S.op("act", lambda e, o=o, bank=bank: e.copy(out=o, in_=PS[bank].ap), reads=[PS[bank].all()], writes=[st.rng(kq * 512, (kq + 1) * 512)])
                else:
                    S.op("dve", lambda e, o=o, bank=bank: e.tensor_copy(out=o, in_=PS[bank].ap), reads=[PS[bank].all()], writes=[st.rng(kq * 512, (kq + 1) * 512)])
            out_events.append(S.dma("sp", ov[j], st.ap, reads=[st.all()], writes=[Acc("out", j, j + 1)]))
        S.wait_all("sp", out_events + dbg_events)

        sems = {k: es.enter_context(nc.semaphore("s_" + "_".join(map(str, k)))) for k in sorted(S.sem_names, key=str)}
        es.enter_context(nc.allow_non_contiguous_dma(reason="small strided parameter loads"))
        block = es.enter_context(nc.Block())
        S.replay(block, sems)
    return nc, S


_CACHE = {}


def _consts():
    c = np.zeros((128, NCONST), np.float32)
    c[:, C_IDENT:C_IDENT + 128] = np.eye(128, dtype=np.float32)
    jj = np.arange(128) // 16
    c[:, C_A0MASK:C_A0MASK + 128] = (jj[None, :] >= jj[:, None]).astype(np.float32)
    sw = np.zeros((128, 128), np.float32)
    for p in range(64):
        sw[p, 64 + p] = 1.0
        sw[64 + p, p] = 1.0
    c[:, C_SWAP:C_SWAP + 128] = sw
    t = np.arange(128, dtype=np.float64)
    c[:, C_TM] = -8.0 * (t + 1)
    c[:, C_TP] = 8.0 * t
    c[:, C_WM] = (t + 1) * 8.0 / (2 * np.pi)
    c[:, C_WP] = t * 8.0 / (2 * np.pi)
    c[:64, C_SGN] = -1.0
    c[64:, C_SGN] = 1.0
    c[:, C_EPS] = 1e-6
    ii = np.arange(128)
    c[:, C_LS:C_LS + 128] = (ii[:, None] < ii[None, :]).astype(np.float32)
    return c


def kernel(_debug=(), **inputs):
    key = ("nc", tuple(sorted(_debug)))
    if key not in _CACHE:
        _CACHE[key] = build_program(debug=set(_debug))[0]
    nc = _CACHE[key]
    x = np.ascontiguousarray(inputs["x"], dtype=np.float32)
    consts = _consts()
    shared = {k: np.ascontiguousarray(inputs[k], dtype=np.float32) for k in
              ("norm_g", "w_in", "conv_w", "w_out_a", "a_re", "a_im", "log_dt", "b_re", "b_im", "c_re", "c_im",
               "d_skip", "w_glu", "b_glu", "w_out_b", "w_o", "final_g")}
    in_maps = []
    for c in range(8):
        b, half = c // 2, c % 2
        m = dict(shared)
        m["x"] = np.ascontiguousarray(x[b, half * NT:(half + 1) * NT, :])
        m["consts"] = consts
        m["maskcol"] = np.full((128, 1), float(half), np.float32)
        in_maps.append(m)
    res = run_bass_kernel_spmd(nc, in_maps, core_ids=list(range(8)))
    out = np.empty((4, 2048, 2048), np.float32)
    for c in range(8):
        b, half = c // 2, c % 2
        out[b, half * NT:(half + 1) * NT, :] = res.results[c]["out"]
    if _debug:
        return out, res.results
    return out
```

```python
import contextlib
import math
import numpy as np
import concourse.bass as bass
import concourse.mybir as mybir
from concourse.bass_utils import run_bass_kernel_spmd

F32 = mybir.dt.float32
BF16 = mybir.dt.bfloat16
AF = mybir.ActivationFunctionType
ALU = mybir.AluOpType

ENGS = ("pe", "act", "dve", "pool", "sp")
SAME_ENGINE_SYNC = True
DEPTH = 2
NT = 1024
D = 2048
KC = 16
NIN = 14336
MAGIC = 12582912.0
TWO_PI = 2.0 * math.pi
OFF_V, OFF_BG, OFF_CG, OFF_ZA, OFF_U, OFF_ZB, OFF_GA, OFF_GB = 0, 2048, 4096, 6144, 8192, 9216, 10240, 12288

C_IDENT, C_A0MASK, C_SWAP, C_LS = 0, 128, 256, 384
C_TM, C_TP, C_WM, C_WP, C_SGN, C_EPS = 512, 513, 514, 515, 516, 517
NCONST = 520


class Acc:
    __slots__ = ("space", "lo", "hi")

    def __init__(self, space, lo, hi):
        self.space, self.lo, self.hi = space, lo, hi


class Buf:
    def __init__(self, space, lo, nbytes, ap, esz):
        self.space, self.lo, self.hi, self.ap, self.esz = space, lo, lo + nbytes, ap, esz

    def all(self):
        return Acc(self.space, self.lo, self.hi)

    def rng(self, a, b):
        return Acc(self.space, self.lo + a * self.esz, self.lo + b * self.esz)


class Sched:
    def __init__(self, nc, n_dma_sems=16):
        self.nc = nc
        self.ops = {e: [] for e in ENGS}
        self.cnt = {e: 0 for e in ENGS}
        self.pending = {e: [] for e in ENGS}
        self.wr = {}
        self.rd = {}
        self.n_dma_sems = n_dma_sems
        self.dma_cnt = {}
        self.dma_rr = {e: 0 for e in ENGS}
        self.sem_names = set()
        self.nops = 0
        self.ps_last = {}

    def _deps(self, reads, writes):
        deps = []
        for a in reads:
            for (lo, hi, ev) in self.wr.get(a.space, ()):
                if lo < a.hi and a.lo < hi:
                    deps.append((ev, True))
        for a in writes:
            for (lo, hi, ev) in self.wr.get(a.space, ()):
                if lo < a.hi and a.lo < hi:
                    deps.append((ev, False))
            for (lo, hi, ev) in self.rd.get(a.space, ()):
                if lo < a.hi and a.lo < hi:
                    deps.append((ev, False))
        return deps

    def _record(self, reads, writes, ev):
        for a in writes:
            wl = self.wr.setdefault(a.space, [])
            wl[:] = [w for w in wl if not (a.lo <= w[0] and w[1] <= a.hi)]
            wl.append((a.lo, a.hi, ev))
            rl = self.rd.setdefault(a.space, [])
            rl[:] = [r for r in rl if not (a.lo <= r[0] and r[1] <= a.hi)]
        for a in reads:
            rl = self.rd.setdefault(a.space, [])
            rl[:] = [r for r in rl if not (r[0] == a.lo and r[1] == a.hi and r[2][0] == ev[0])]
            rl.append((a.lo, a.hi, ev))

    def _emit(self, eng, fn, reads, writes, inc=True, dma=False, custom_sem=None):
        self.nops += 1
        deps = self._deps(reads, writes)
        banks = set()
        for a in tuple(reads) + tuple(writes):
            if a.space == "ps":
                for bnk in range(a.lo // 2048, (a.hi - 1) // 2048 + 1):
                    banks.add(bnk)
        for bnk in banks:
            for oe, oev in self.ps_last.get(bnk, {}).items():
                if oe != eng:
                    deps.append((oev, True))
        waits = []
        for ev, is_raw in deps:
            if ev[0] == ("eng", eng) and (eng == "pe" or not SAME_ENGINE_SYNC or not is_raw):
                continue
            waits.append(ev)
        if custom_sem is not None:
            semkey = custom_sem
            ev = [semkey, 1]
            incspec = (semkey, 1)
        elif dma:
            r = self.dma_rr[eng]
            self.dma_rr[eng] = (r + 1) % self.n_dma_sems
            semkey = ("dma", eng, r)
            if self.dma_cnt.get(semkey, 0) > 0:
                waits.append([semkey, self.dma_cnt[semkey]])
            self.dma_cnt[semkey] = self.dma_cnt.get(semkey, 0) + 16
            ev = [semkey, self.dma_cnt[semkey]]
            incspec = (semkey, 16)
        elif inc:
            self.cnt[eng] += 1
            semkey = ("eng", eng)
            ev = [semkey, self.cnt[eng]]
            for pev in self.pending[eng]:
                pev[1] = self.cnt[eng]
            self.pending[eng] = []
            incspec = (semkey, 1)
        else:
            semkey = ("eng", eng)
            ev = [semkey, None]
            self.pending[eng].append(ev)
            incspec = None
        self.sem_names.add(semkey)
        self.ops[eng].append((fn, waits, incspec))
        self._record(reads, writes, ev)
        for bnk in banks:
            self.ps_last.setdefault(bnk, {})[eng] = ev
        return ev

    def op(self, eng, fn, reads=(), writes=(), inc=True, custom_sem=None):
        return self._emit(eng, fn, tuple(reads), tuple(writes), inc=inc, custom_sem=custom_sem)

    def dma(self, eng, out, in_, reads=(), writes=()):
        return self._emit(eng, lambda e: e.dma_start(out=out, in_=in_), tuple(reads), tuple(writes), dma=True)

    def wait_all(self, eng, events):
        self.ops[eng].append((None, list(events), None))

    def replay(self, block, sems):
        engmap = {"pe": block.tensor, "act": block.scalar, "dve": block.vector,
                  "pool": block.gpsimd, "sp": block.sync}
        for e in ENGS:
            assert not self.pending[e], f"pending non-inc'd ops on {e}"

        def make(e):
            oplist = self.ops[e]

            def body(eng):
                waited = {}
                for fn, waits, incspec in oplist:
                    need = {}
                    for semkey, val in waits:
                        assert val is not None
                        if waited.get(semkey, 0) >= val:
                            continue
                        need[semkey] = max(need.get(semkey, 0), val)
                    for semkey, val in need.items():
                        eng.wait_ge(sems[semkey], val)
                        waited[semkey] = val
                    if fn is None:
                        continue
                    ins = fn(eng)
                    if incspec is not None:
                        ins.then_inc(sems[incspec[0]], incspec[1])
            return body

        for e in ENGS:
            if self.ops[e]:
                engmap[e](make(e))


class Arena:
    def __init__(self, tensor, base, nbytes):
        self.t, self.base, self.nbytes, self.off = tensor, base, nbytes, 0

    def reset(self):
        self.off = 0

    def sub(self, off, nbytes):
        a = Arena(self.t, self.base + off, nbytes)
        a.t0 = getattr(self, "t0", 0) + off
        return a

    def alloc(self, shape, dt, parts=128):
        esz = 4 if dt == F32 else 2
        n = 1
        for s in shape:
            n *= s
        nb = (n * esz + 31) // 32 * 32
        assert self.off + nb <= self.nbytes, ("arena overflow", self.off, nb, self.nbytes)
        o4 = (getattr(self, 't0', 0) + self.off) // 4
        v = self.t[0:parts, o4:o4 + nb // 4]
        if dt != F32:
            v = v.bitcast(dt)
        v = v[:, 0:n]
        if len(shape) == 2:
            v = v.rearrange("p (a b) -> p a b", b=shape[1])
        elif len(shape) == 3:
            v = v.rearrange("p (a b c) -> p a b c", b=shape[1], c=shape[2])
        elif len(shape) == 4:
            v = v.rearrange("p (a b c d) -> p a b c d", b=shape[1], c=shape[2], d=shape[3])
        b = Buf("sb", self.base + self.off, n * esz, v, esz)
        self.off += nb
        return b


def build_program(debug=()):
    nc = bass.Bass("TRN2", target_bir_lowering=False)
    dbg_events = []

    def tap(name, buf, shape, dt, layer=0):
        if name not in debug or layer != 0:
            return
        d = nc.dram_tensor("dbg_" + name, [128] + list(shape), dt, kind="ExternalOutput").ap()
        dbg_events.append(S.dma("sp", d, buf.ap, reads=[buf.all()], writes=[Acc("dbg_" + name, 0, 1)]))
    dt_in = lambda name, shape: nc.dram_tensor(name, shape, F32, kind="ExternalInput").ap()
    x_d = dt_in("x", [NT, D])
    normg_d = dt_in("norm_g", [DEPTH, D])
    win_d = dt_in("w_in", [DEPTH, D, NIN])
    convw_d = dt_in("conv_w", [DEPTH, 3, D])
    wa_d = dt_in("w_out_a", [DEPTH, D, D])
    are_d = dt_in("a_re", [DEPTH, 64, 64])
    aim_d = dt_in("a_im", [DEPTH, 64, 64])
    ldt_d = dt_in("log_dt", [DEPTH, 64])
    bre_d = dt_in("b_re", [DEPTH, 64, 64, 16])
    bim_d = dt_in("b_im", [DEPTH, 64, 64, 16])
    cre_d = dt_in("c_re", [DEPTH, 64, 16, 64])
    cim_d = dt_in("c_im", [DEPTH, 64, 16, 64])
    dsk_d = dt_in("d_skip", [DEPTH, 64, 16])
    wglu_d = dt_in("w_glu", [DEPTH, 1024, 1024])
    bglu_d = dt_in("b_glu", [DEPTH, 1024])
    wb_d = dt_in("w_out_b", [DEPTH, 1024, D])
    wo_d = dt_in("w_o", [DEPTH, D, D])
    fg_d = dt_in("final_g", [D])
    const_d = dt_in("consts", [128, NCONST])
    mcol_d = dt_in("maskcol", [128, 1])
    out_d = nc.dram_tensor("out", [NT, D], F32, kind="ExternalOutput").ap()
    cc1in = [nc.dram_tensor(f"cc1in{l}", [64, 128], F32) for l in range(DEPTH)]
    cc1out = [nc.dram_tensor(f"cc1out{l}", [128, 128], F32) for l in range(DEPTH)]
    cc2in = [nc.dram_tensor(f"cc2in{l}", [32, 128], F32) for l in range(DEPTH)]
    cc2out = [nc.dram_tensor(f"cc2out{l}", [64, 128], F32) for l in range(DEPTH)]

    S = Sched(nc)
    es = contextlib.ExitStack()
    with es:
        def raw(name, nbytes):
            return es.enter_context(nc.sbuf_tensor(name, [128, nbytes // 4], F32))
        sb_off = [0]

        def region(name, nbytes):
            t = raw(name, nbytes)
            a = Arena(t, sb_off[0], nbytes)
            sb_off[0] += nbytes
            return a
        RX = region("RX", 65536)
        RH = region("RH", 32768)
        RAO = region("RAO", 32768)
        RM = region("RM", 32768)
        RYB = region("RYB", 16384)
        RWB = region("RWB", 16384)
        RT = region("RT", 8192)
        RS = region("RS", 6144)

        X = RX.alloc([16, 1024], F32)
        H = RH.alloc([16, 1024], BF16)
        AO = RAO.alloc([16, 1024], BF16)
        RAO.reset()
        WU = RAO.alloc([2, 16, 512], BF16)
        RAO.reset()
        XST = [RAO.alloc([2048], F32) for _ in range(2)]
        M = RM.alloc([16, 1024], BF16)
        RM.reset()
        RUT = RM.sub(0, 16384)
        RUB = RM.sub(16384, 16384)
        UT = RUT.alloc([64, 8, 16], BF16)
        RUT.reset()
        UBLK = RUB.alloc([64, 128], BF16)
        GATE = RM.alloc([8, 1024], BF16)
        YB = RYB.alloc([8, 1024], BF16)
        NW = 4
        WB = [RWB.alloc([16, 128], BF16) for _ in range(NW)]
        CONST = RS.alloc([NCONST], F32)
        IDB = RS.alloc([128], BF16)
        LSB = RS.alloc([128], BF16)
        ONB = RS.alloc([128], BF16)
        ONF = RS.alloc([128], F32)
        MCOL = RS.alloc([1], F32)
        NG = RS.alloc([DEPTH, 16], F32)
        FG = RS.alloc([16], F32)
        CW = RS.alloc([DEPTH, 3, 16], F32)
        BGL = RS.alloc([DEPTH, 8], F32)
        TAIL = RS.alloc([16, 2], F32)
        GATE01 = RS.alloc([16, 2], F32)
        BGT = RS.alloc([2], F32)
        TAILP = RS.alloc([16, 2], F32)
        DL = RS.alloc([4, 16], F32)

        PS = []
        for b in range(8):
            t = es.enter_context(nc.psum_tensor(f"ps{b}", [128, 512], F32))
            PS.append(Buf("ps", b * 2048, 2048, t[:, :], 4))

        def psb(b):
            return PS[b].ap.bitcast(BF16)

        ident = CONST.ap[:, C_IDENT:C_IDENT + 128]
        a0mask = CONST.ap[:, C_A0MASK:C_A0MASK + 128]
        swapm = CONST.ap[:, C_SWAP:C_SWAP + 128]
        ccol = lambda c: CONST.ap[:, c:c + 1]

        dq = ["sp"]

        S.dma("sp", CONST.ap, const_d[:, :], writes=[CONST.all()])
        S.dma("sp", MCOL.ap, mcol_d[:, :], writes=[MCOL.all()])
        RT.reset()
        vec_jobs = []
        for l_ in range(DEPTH):
            vec_jobs.append((normg_d[l_].rearrange("(k p) -> k p", p=128), 16, NG.ap[:, l_, :], NG))
            vec_jobs.append((bglu_d[l_].rearrange("(k p) -> k p", p=128), 8, BGL.ap[:, l_, :], BGL))
            for t_ in range(3):
                vec_jobs.append((convw_d[l_, t_].rearrange("(k p) -> k p", p=128), 16, CW.ap[:, l_, t_, :], CW))
        vec_jobs.append((fg_d.rearrange("(k p) -> k p", p=128), 16, FG.ap, FG))
        for vi, (src, nk, dst_ap, dst_buf) in enumerate(vec_jobs):
            stg = RT.alloc([128], F32, parts=16)
            S.dma("sp", stg.ap[0:nk, :], src, writes=[stg.all()])
            bk = vi % 4
            S.op("pe", lambda e, stg=stg, nk=nk, bk=bk: e.transpose(out=PS[bk].ap[:, 0:nk], in_=stg.ap[0:nk, :], identity=CONST.ap[0:nk, 0:nk]),
                 reads=[stg.all(), CONST.all()], writes=[PS[bk].rng(0, 16)])
            S.op("dve", lambda e, dst_ap=dst_ap, nk=nk, bk=bk: e.tensor_copy(out=dst_ap, in_=PS[bk].ap[:, 0:nk]),
                 reads=[PS[bk].rng(0, 16)], writes=[dst_buf.all()])
        S.op("dve", lambda e: e.tensor_copy(out=IDB.ap, in_=ident), reads=[CONST.all()], writes=[IDB.all()])
        S.op("pool", lambda e: e.memset(ONB.ap, 1.0), writes=[ONB.all()])
        S.op("pool", lambda e: e.memset(ONF.ap, 1.0), writes=[ONF.all()])
        S.op("dve", lambda e: e.tensor_copy(out=LSB.ap, in_=CONST.ap[:, C_LS:C_LS + 128]), reads=[CONST.all()], writes=[LSB.all()])

        xv = x_d.rearrange("(t j) d -> j t d", j=8)
        for j in range(8):
            st = XST[j % 2]
            S.dma("sp", st.ap, xv[j], writes=[st.all()])
            for kq in range(4):
                bank = (j * 4 + kq) % 4
                for kk in range(4):
                    k = kq * 4 + kk
                    S.op("pe", lambda e, k=k, kk=kk, bank=bank, st=st: e.transpose(
                        out=PS[bank].ap[:, kk * 128:(kk + 1) * 128], in_=st.ap[:, k * 128:(k + 1) * 128], identity=ident),
                        reads=[st.rng(k * 128, (k + 1) * 128), CONST.all()], writes=[PS[bank].rng(kk * 128, (kk + 1) * 128)],
                        inc=(kk == 3))
                eng = "act" if kq % 2 == 0 else "dve"
                outap = X.ap[:, kq * 4:(kq + 1) * 4, j * 128:(j + 1) * 128]
                inap = PS[bank].ap.rearrange("p (a b) -> p a b", b=128)
                wr = [X.rng(k * 1024 + j * 128, k * 1024 + (j + 1) * 128) for k in range(kq * 4, kq * 4 + 4)]
                if eng == "act":
                    S.op("act", lambda e, o=outap, i=inap: e.copy(out=o, in_=i), reads=[PS[bank].all()], writes=wr)
                else:
                    S.op("dve", lambda e, o=outap, i=inap: e.tensor_copy(out=o, in_=i), reads=[PS[bank].all()], writes=wr)

        tap("x0", X, [16, 1024], F32)
        def rmsnorm(gcol_fn, out_h):
            RT.reset()
            SQ = [RT.alloc([1024], BF16) for _ in range(2)]
            RSTD = RT.alloc([1024], F32)
            for k in range(16):
                sq = SQ[k % 2]
                S.op("act", lambda e, k=k, sq=sq: e.activation(out=sq.ap, in_=X.ap[:, k, :], func=AF.Square),
                     reads=[X.rng(k * 1024, (k + 1) * 1024)], writes=[sq.all()])
                for hf in range(2):
                    S.op("pe", lambda e, k=k, hf=hf, sq=sq: e.matmul(PS[hf].ap, lhsT=ONB.ap, rhs=sq.ap[:, hf * 512:(hf + 1) * 512],
                                                                  start=(k == 0), stop=(k == 15)),
                         reads=[sq.rng(hf * 512, (hf + 1) * 512), ONB.all()], writes=[PS[hf].all()], inc=True)
            for hf in range(2):
                S.op("act", lambda e, hf=hf: e.activation(out=RSTD.ap[:, hf * 512:(hf + 1) * 512], in_=PS[hf].ap, func=AF.Sqrt,
                                                          scale=1.0 / D, bias=ccol(C_EPS)),
                     reads=[PS[hf].all(), CONST.all()], writes=[RSTD.rng(hf * 512, (hf + 1) * 512)])
            S.op("dve", lambda e: e.reciprocal(out=RSTD.ap, in_=RSTD.ap), reads=[RSTD.all()], writes=[RSTD.all()])
            for k in range(16):
                if out_h:
                    S.op("dve", lambda e, k=k: e.scalar_tensor_tensor(out=H.ap[:, k, :], in0=X.ap[:, k, :], scalar=gcol_fn(k),
                                                                      in1=RSTD.ap, op0=ALU.mult, op1=ALU.mult),
                         reads=[X.rng(k * 1024, (k + 1) * 1024), RSTD.all(), NG.all()], writes=[H.rng(k * 1024, (k + 1) * 1024)])
                else:
                    S.op("dve", lambda e, k=k: e.scalar_tensor_tensor(out=X.ap[:, k, :], in0=X.ap[:, k, :], scalar=gcol_fn(k),
                                                                      in1=RSTD.ap, op0=ALU.mult, op1=ALU.mult),
                         reads=[X.rng(k * 1024, (k + 1) * 1024), RSTD.all(), FG.all()], writes=[X.rng(k * 1024, (k + 1) * 1024)])

        wslot = [0]

        def wload(src_ap, nk):
            s = WB[wslot[0] % NW]
            wslot[0] += 1
            S.dma("pool", s.ap[:, 0:nk, :], src_ap.rearrange("(k p) n -> p k n", p=128),
                  writes=[s.rng(0, nk * 128)])
            return s

        def proj(slot, nk, rhs_buf, banks):
            for k in range(nk):
                for hf in range(2):
                    S.op("pe", lambda e, k=k, hf=hf: e.matmul(PS[banks[hf]].ap, lhsT=slot.ap[:, k, :],
                                                              rhs=rhs_buf.ap[:, k, hf * 512:(hf + 1) * 512],
                                                              start=(k == 0), stop=(k == nk - 1)),
                         reads=[slot.rng(k * 128, (k + 1) * 128), rhs_buf.rng(k * 1024 + hf * 512, k * 1024 + (hf + 1) * 512)],
                         writes=[PS[banks[hf]].all()], inc=(k == nk - 1))

        def run_jobs(jobs, pf=3):
            slots = [None] * len(jobs)
            for i in range(min(pf, len(jobs))):
                slots[i] = wload(jobs[i][0], jobs[i][1])
            for i, (src, nk, consume) in enumerate(jobs):
                if i + pf < len(jobs):
                    slots[i + pf] = wload(jobs[i + pf][0], jobs[i + pf][1])
                banks = ((0, 1), (2, 3), (4, 5), (6, 7))[i % 4]
                consume(slots[i], banks)

        for l in range(DEPTH):
            if l == 1:
                tap("x1", X, [16, 1024], F32)
            rmsnorm(lambda k, l=l: NG.ap[:, l, k:k + 1], True)

            tap("h", H, [16, 1024], BF16, l)
            tt = lambda eng, o, a_, b_, op, rd, wr: S.op(eng, lambda e: e.tensor_tensor(out=o, in0=a_, in1=b_, op=op), reads=rd, writes=wr)

            def trig(Yb, Rb, SNb, CSb):
                S.op("dve", lambda e: e.tensor_scalar(out=Rb.ap, in0=Yb.ap, scalar1=MAGIC, scalar2=MAGIC, op0=ALU.add, op1=ALU.subtract),
                     reads=[Yb.all()], writes=[Rb.all()])
                S.op("dve", lambda e: e.tensor_tensor(out=Rb.ap, in0=Yb.ap, in1=Rb.ap, op=ALU.subtract), reads=[Yb.all(), Rb.all()], writes=[Rb.all()])
                S.op("act", lambda e: e.activation(out=SNb.ap, in_=Rb.ap, func=AF.Sin, scale=TWO_PI), reads=[Rb.all()], writes=[SNb.all()])
                S.op("dve", lambda e: e.tensor_scalar(out=Yb.ap, in0=Yb.ap, scalar1=0.25, scalar2=None, op0=ALU.add), reads=[Yb.all()], writes=[Yb.all()])
                S.op("dve", lambda e: e.tensor_scalar(out=Rb.ap, in0=Yb.ap, scalar1=MAGIC, scalar2=MAGIC, op0=ALU.add, op1=ALU.subtract),
                     reads=[Yb.all(), SNb.all()], writes=[Rb.all()])
                S.op("dve", lambda e: e.tensor_tensor(out=Rb.ap, in0=Yb.ap, in1=Rb.ap, op=ALU.subtract), reads=[Yb.all(), Rb.all()], writes=[Rb.all()])
                S.op("act", lambda e: e.activation(out=CSb.ap, in_=Rb.ap, func=AF.Sin, scale=TWO_PI), reads=[Rb.all()], writes=[CSb.all()])

            RAO.reset()
            ZALL = RAO.alloc([64, 2, 64], BF16)
            RAO.reset()
            WUQ = [RAO.alloc([16, 256], BF16) for _ in range(2)]
            QTA = RAO.alloc([64, 16], F32)
            QTB = RAO.alloc([64, 16], F32)
            LTA = RAO.alloc([64, 8], F32)
            LTB = RAO.alloc([64, 8], F32)
            T2G = RAO.alloc([8, 8, 16], BF16)
            SLAB = []
            for b in range(16):
                sa = RYB.sub(b * 1024, 1024)
                SLAB.append(tuple(sa.alloc([4, 16], F32) for _ in range(4)))
            A = RWB
            A.reset()
            sm = lambda: A.alloc([64], F32)
            DCOL, A1R, A1I, HV1, HV2, HEND = sm(), sm(), sm(), sm(), sm(), sm()
            HENDT = A.alloc([128], F32, parts=64)
            XIK = A.alloc([64], F32)
            XRK = A.alloc([64], F32)
            keep = A.off
            NAT = A.alloc([2, 128], F32, parts=64)
            DNAT = A.alloc([128], F32, parts=64)
            MLD = [(A.alloc([128], F32, parts=64), A.alloc([128], F32, parts=64)) for _ in range(2)]
            ART, AIT, LDT, DTT, XR, XI = sm(), sm(), sm(), sm(), sm(), sm()
            Yp, Rp, SNp, CSp, MGp, MGI = sm(), sm(), sm(), sm(), sm(), sm()
            LBR, LBI, LIR, LII, BR, BI = sm(), sm(), sm(), sm(), sm(), sm()
            E1, E2, E3, E4 = sm(), sm(), sm(), sm()
            id64 = CONST.ap[0:64, 0:64]
            for hh in range(2):
                S.dma("sp", NAT.ap[:, 0, hh * 64:(hh + 1) * 64], are_d[l], writes=[NAT.rng(hh * 64, (hh + 1) * 64)])
                S.dma("sp", NAT.ap[:, 1, hh * 64:(hh + 1) * 64], aim_d[l], writes=[NAT.rng(128 + hh * 64, 128 + (hh + 1) * 64)])
            for j in range(8):
                S.dma("sp", DNAT.ap[:, j * 16:(j + 1) * 16], dsk_d[l], writes=[DNAT.rng(j * 16, (j + 1) * 16)])
            S.dma("sp", LDT.ap, ldt_d[l].partition_broadcast(128), writes=[LDT.all()])
            for i, (src, dst) in enumerate(((NAT.ap[:, 0, :], ART), (NAT.ap[:, 1, :], AIT), (DNAT.ap, DCOL))):
                rd = NAT.all() if i < 2 else DNAT.all()
                S.op("pe", lambda e, src=src: e.transpose(out=PS[6].ap[:, 0:64], in_=src, identity=id64),
                     reads=[rd, CONST.all()], writes=[PS[6].rng(0, 64)])
                S.op("dve", lambda e, dst=dst: e.tensor_copy(out=dst.ap, in_=PS[6].ap[:, 0:64]), reads=[PS[6].rng(0, 64)], writes=[dst.all()])
            S.op("act", lambda e: e.activation(out=DTT.ap, in_=LDT.ap, func=AF.Exp), reads=[LDT.all()], writes=[DTT.all()])
            tt("dve", XR.ap, ART.ap, DTT.ap, ALU.mult, [ART.all(), DTT.all()], [XR.all()])
            tt("dve", XI.ap, AIT.ap, DTT.ap, ALU.mult, [AIT.all(), DTT.all()], [XI.all()])
            S.op("dve", lambda e: e.tensor_scalar(out=Yp.ap, in0=XI.ap, scalar1=1.0 / TWO_PI, scalar2=None, op0=ALU.mult), reads=[XI.all()], writes=[Yp.all()])
            trig(Yp, Rp, SNp, CSp)
            S.op("act", lambda e: e.activation(out=MGp.ap, in_=XR.ap, func=AF.Exp), reads=[XR.all()], writes=[MGp.all()])
            S.op("act", lambda e: e.activation(out=MGI.ap, in_=XR.ap, func=AF.Exp, scale=-1.0), reads=[XR.all()], writes=[MGI.all()])
            tt("dve", LBR.ap, CSp.ap, MGp.ap, ALU.mult, [CSp.all(), MGp.all()], [LBR.all()])
            tt("dve", LBI.ap, SNp.ap, MGp.ap, ALU.mult, [SNp.all(), MGp.all()], [LBI.all()])
            tt("dve", LIR.ap, CSp.ap, MGI.ap, ALU.mult, [CSp.all(), MGI.all()], [LIR.all()])
            S.op("dve", lambda e: e.scalar_tensor_tensor(out=LII.ap, in0=SNp.ap, scalar=-1.0, in1=MGI.ap, op0=ALU.mult, op1=ALU.mult),
                 reads=[SNp.all(), MGI.all()], writes=[LII.all()])
            S.op("dve", lambda e: e.tensor_scalar(out=E1.ap, in0=LBR.ap, scalar1=-1.0, scalar2=None, op0=ALU.add), reads=[LBR.all()], writes=[E1.all()])
            tt("dve", E2.ap, ART.ap, ART.ap, ALU.mult, [ART.all()], [E2.all()])
            tt("dve", E3.ap, AIT.ap, AIT.ap, ALU.mult, [AIT.all()], [E3.all()])
            tt("dve", E2.ap, E2.ap, E3.ap, ALU.add, [E2.all(), E3.all()], [E2.all()])
            S.op("dve", lambda e: e.reciprocal(out=E2.ap, in_=E2.ap), reads=[E2.all()], writes=[E2.all()])
            tt("dve", E3.ap, E1.ap, ART.ap, ALU.mult, [E1.all(), ART.all()], [E3.all()])
            tt("dve", E4.ap, LBI.ap, AIT.ap, ALU.mult, [LBI.all(), AIT.all()], [E4.all()])
            tt("dve", E3.ap, E3.ap, E4.ap, ALU.add, [E3.all(), E4.all()], [E3.all()])
            tt("dve", BR.ap, E3.ap, E2.ap, ALU.mult, [E3.all(), E2.all()], [BR.all()])
            tt("dve", E3.ap, LBI.ap, ART.ap, ALU.mult, [LBI.all(), ART.all()], [E3.all()])
            tt("dve", E4.ap, E1.ap, AIT.ap, ALU.mult, [E1.all(), AIT.all()], [E4.all()])
            tt("dve", E3.ap, E3.ap, E4.ap, ALU.subtract, [E3.all(), E4.all()], [E3.all()])
            tt("dve", BI.ap, E3.ap, E2.ap, ALU.mult, [E3.all(), E2.all()], [BI.all()])

            def cmul_small(ore, oim, are_, aim_, bre_, bim_, rd, wr):
                tt("dve", E3.ap, are_, bre_, ALU.mult, rd, [E3.all()])
                tt("dve", E4.ap, aim_, bim_, ALU.mult, rd, [E4.all()])
                tt("dve", E1.ap, are_, bim_, ALU.mult, rd, [E1.all()])
                tt("dve", E2.ap, aim_, bre_, ALU.mult, rd, [E2.all()])
                tt("dve", ore, E3.ap, E4.ap, ALU.subtract, [E3.all(), E4.all()], wr)
                tt("dve", oim, E1.ap, E2.ap, ALU.add, [E1.all(), E2.all()], wr)
            S.op("dve", lambda e: e.tensor_copy(out=QTA.ap[:, :, 7], in_=BR.ap), reads=[BR.all()], writes=[QTA.all()])
            S.op("dve", lambda e: e.tensor_copy(out=QTB.ap[:, :, 7], in_=BI.ap), reads=[BI.all()], writes=[QTB.all()])
            for j in range(6, -1, -1):
                cmul_small(QTA.ap[:, :, j], QTB.ap[:, :, j], QTA.ap[:, :, j + 1], QTB.ap[:, :, j + 1], LBR.ap, LBI.ap,
                           [QTA.all(), QTB.all(), LBR.all(), LBI.all()], [QTA.all(), QTB.all()])
            cmul_small(QTA.ap[:, :, 8], QTB.ap[:, :, 8], BR.ap, BI.ap, LIR.ap, LII.ap, [BR.all(), BI.all(), LIR.all(), LII.all()], [QTA.all(), QTB.all()])
            for j in range(1, 8):
                cmul_small(QTA.ap[:, :, 8 + j], QTB.ap[:, :, 8 + j], QTA.ap[:, :, 7 + j], QTB.ap[:, :, 7 + j], LIR.ap, LII.ap,
                           [QTA.all(), QTB.all(), LIR.all(), LII.all()], [QTA.all(), QTB.all()])
            S.op("dve", lambda e: e.tensor_copy(out=LTA.ap[:, :, 0], in_=LBR.ap), reads=[LBR.all()], writes=[LTA.all()])
            S.op("dve", lambda e: e.tensor_copy(out=LTB.ap[:, :, 0], in_=LBI.ap), reads=[LBI.all()], writes=[LTB.all()])
            for i in range(1, 8):
                cmul_small(LTA.ap[:, :, i], LTB.ap[:, :, i], LTA.ap[:, :, i - 1], LTB.ap[:, :, i - 1], LBR.ap, LBI.ap,
                           [LTA.all(), LTB.all(), LBR.all(), LBI.all()], [LTA.all(), LTB.all()])
            S.op("dve", lambda e: e.tensor_copy(out=A1R.ap, in_=LTA.ap[:, :, 7]), reads=[LTA.all()], writes=[A1R.all()])
            S.op("dve", lambda e: e.tensor_copy(out=A1I.ap, in_=LTB.ap[:, :, 7]), reads=[LTB.all()], writes=[A1I.all()])
            for _ in range(7):
                tt("dve", E1.ap, A1R.ap, A1R.ap, ALU.mult, [A1R.all()], [E1.all()])
                tt("dve", E3.ap, A1I.ap, A1I.ap, ALU.mult, [A1I.all()], [E3.all()])
                tt("dve", E4.ap, A1R.ap, A1I.ap, ALU.mult, [A1R.all(), A1I.all()], [E4.all()])
                tt("dve", A1R.ap, E1.ap, E3.ap, ALU.subtract, [E1.all(), E3.all()], [A1R.all()])
                S.op("dve", lambda e: e.tensor_scalar(out=A1I.ap, in0=E4.ap, scalar1=2.0, scalar2=None, op0=ALU.mult), reads=[E4.all()], writes=[A1I.all()])
            sgn = ccol(C_SGN)
            S.op("dve", lambda e: e.tensor_scalar(out=A1I.ap, in0=A1I.ap, scalar1=sgn, scalar2=None, op0=ALU.mult), reads=[A1I.all(), CONST.all()], writes=[A1I.all()])
            S.op("dve", lambda e: e.tensor_scalar(out=QTB.ap, in0=QTB.ap, scalar1=sgn, scalar2=None, op0=ALU.mult), reads=[QTB.all(), CONST.all()], writes=[QTB.all()])
            S.op("dve", lambda e: e.tensor_scalar(out=LTA.ap, in0=LTA.ap, scalar1=sgn, scalar2=-1.0, op0=ALU.mult, op1=ALU.mult),
                 reads=[LTA.all(), CONST.all()], writes=[LTA.all()])
            S.op("dve", lambda e: e.tensor_scalar(out=LTB.ap, in0=LTB.ap, scalar1=-1.0, scalar2=None, op0=ALU.mult), reads=[LTB.all()], writes=[LTB.all()])
            S.op("dve", lambda e: e.tensor_copy(out=XIK.ap, in_=XI.ap), reads=[XI.all()], writes=[XIK.all()])
            S.op("dve", lambda e: e.tensor_copy(out=XRK.ap, in_=XR.ap), reads=[XR.all()], writes=[XRK.all()])

            for nq in range(4):
                wq_ = WUQ[nq % 2]
                S.dma("pool", wq_.ap, win_d[l, :, OFF_U + nq * 256:OFF_U + (nq + 1) * 256].rearrange("(k p) n -> p k n", p=128),
                      writes=[wq_.all()])
                for j in range(8):
                    bank = (nq * 8 + j) % 4
                    for k in range(16):
                        S.op("pe", lambda e, j=j, k=k, bank=bank, wq_=wq_: e.matmul(
                            PS[bank].ap[:, 0:256], lhsT=H.ap[:, k, j * 128:(j + 1) * 128], rhs=wq_.ap[:, k, :],
                            start=(k == 0), stop=(k == 15)),
                            reads=[H.rng(k * 1024 + j * 128, k * 1024 + (j + 1) * 128), wq_.rng(k * 256, (k + 1) * 256)],
                            writes=[PS[bank].rng(0, 256)], inc=(k == 15))
                    outap = UT.ap[:, nq * 16:(nq + 1) * 16, j, :]
                    inap = PS[bank].ap[:, 0:256].rearrange("p (g c) -> p g c", c=16)
                    S.op("act", lambda e, o=outap, i=inap: e.copy(out=o, in_=i), reads=[PS[bank].rng(0, 256)],
                         writes=[UT.rng(nq * 16 * 128, (nq + 1) * 16 * 128)])
            tap("ut", UT, [64, 8, 16], BF16, l)
            for gq in range(16):
                bank = 4 + gq % 2
                for gl in range(4):
                    g = gq * 4 + gl
                    S.op("pe", lambda e, g=g, gl=gl, bank=bank: e.transpose(
                        out=psb(bank)[:, gl * 128:(gl + 1) * 128], in_=UT.ap[:, g].rearrange("p j c -> p (j c)"), identity=IDB.ap),
                        reads=[UT.rng(g * 128, (g + 1) * 128), IDB.all()], writes=[PS[bank].rng(gl * 64, (gl + 1) * 64)], inc=(gl == 3))
                outap = UBLK.ap[:, gq * 4:(gq + 1) * 4, :].rearrange("p g t -> p (g t)")
                inap = psb(bank)[:, 0:512]
                S.op("act", lambda e, o=outap, i=inap: e.copy(out=o, in_=i), reads=[PS[bank].all()], writes=[UBLK.rng(gq * 512, (gq + 1) * 512)])

            for b in range(16):
                g0 = b * 4
                BT1, BT2, CT1, CT2 = SLAB[b]
                S.dma("sp", BT1.ap[0:64], bre_d[l, g0:g0 + 4].rearrange("g p c -> p g c"), writes=[BT1.all()])
                S.dma("sp", BT1.ap[64:128], bim_d[l, g0:g0 + 4].rearrange("g p c -> p g c"), writes=[BT1.all()])
                S.dma("sp", BT2.ap[0:64], bim_d[l, g0:g0 + 4].rearrange("g p c -> p g c"), writes=[BT2.all()])
                S.dma("sp", BT2.ap[64:128], bre_d[l, g0:g0 + 4].rearrange("g p c -> p g c"), writes=[BT2.all()])
            ARN = [RUT, RT, RWB.sub(keep, RWB.nbytes - keep)]
            for a_ in ARN:
                a_.reset()

            def palloc(shape, dt, parts=128):
                for a_ in ARN:
                    esz = 4 if dt == F32 else 2
                    n = 1
                    for s_ in shape:
                        n *= s_
                    if a_.off + (n * esz + 31) // 32 * 32 <= a_.nbytes:
                        return a_.alloc(shape, dt, parts)
                raise AssertionError("pass arena overflow")

            def view(buf, shape, dt):
                esz = 4 if dt == F32 else 2
                n = 1
                for s_ in shape:
                    n *= s_
                assert n * esz <= buf.hi - buf.lo
                v = buf.ap
                while len(v.shape) > 2:
                    v = v.rearrange("p a b -> p (a b)") if len(v.shape) == 3 else v.rearrange("p a b c -> p (a b c)")
                if dt != F32 and buf.esz == 4:
                    v = v.bitcast(dt)
                v = v[:, 0:n]
                if len(shape) == 2:
                    v = v.rearrange("p (a b) -> p a b", b=shape[1])
                elif len(shape) == 3:
                    v = v.rearrange("p (a b c) -> p a b c", b=shape[1], c=shape[2])
                return Buf("sb", buf.lo, n * esz, v, esz)

            def run_pipelined(genf, n, depth=2):
                active = []
                nxt = 0
                while nxt < n or active:
                    if nxt < n and len(active) < depth:
                        active.append(genf(nxt))
                        nxt += 1
                    for g_ in list(active):
                        try:
                            next(g_)
                        except StopIteration:
                            active.remove(g_)

            def g_trig(Yb, Rb, SNb, CSb, Y2b, R2b):
                S.op("dve", lambda e: e.tensor_scalar(out=Rb.ap, in0=Yb.ap, scalar1=MAGIC, scalar2=MAGIC, op0=ALU.add, op1=ALU.subtract),
                     reads=[Yb.all()], writes=[Rb.all()])
                yield
                S.op("dve", lambda e: e.tensor_scalar(out=Y2b.ap, in0=Yb.ap, scalar1=0.25, scalar2=None, op0=ALU.add), reads=[Yb.all()], writes=[Y2b.all()])
                yield
                S.op("dve", lambda e: e.tensor_tensor(out=Rb.ap, in0=Yb.ap, in1=Rb.ap, op=ALU.subtract), reads=[Yb.all(), Rb.all()], writes=[Rb.all()])
                yield
                S.op("dve", lambda e: e.tensor_scalar(out=R2b.ap, in0=Y2b.ap, scalar1=MAGIC, scalar2=MAGIC, op0=ALU.add, op1=ALU.subtract),
                     reads=[Y2b.all()], writes=[R2b.all()])
                yield
                S.op("act", lambda e: e.activation(out=SNb.ap, in_=Rb.ap, func=AF.Sin, scale=TWO_PI), reads=[Rb.all()], writes=[SNb.all()])
                yield
                S.op("dve", lambda e: e.tensor_tensor(out=R2b.ap, in0=Y2b.ap, in1=R2b.ap, op=ALU.subtract), reads=[Y2b.all(), R2b.all()], writes=[R2b.all()])
                yield
                S.op("act", lambda e: e.activation(out=CSb.ap, in_=R2b.ap, func=AF.Sin, scale=TWO_PI), reads=[R2b.all()], writes=[CSb.all()])
                yield

            def g_tables(g0, bank, cw, ct, neg_im, DG, Yv, Rv, SNv, CSv, Y2v, R2v, WRE, WIM):
                i64b = CONST.ap[0:64, 0:64].unsqueeze(1).broadcast_to([64, 4, 64])
                tt("dve", DG.ap[:, 0], i64b, XIK.ap[0:64, g0:g0 + 4].unsqueeze(2).broadcast_to([64, 4, 64]), ALU.mult, [CONST.all(), XIK.all()], [DG.rng(0, 256)])
                tt("pool", DG.ap[:, 1], i64b, XRK.ap[0:64, g0:g0 + 4].unsqueeze(2).broadcast_to([64, 4, 64]), ALU.mult, [CONST.all(), XRK.all()], [DG.rng(256, 512)])
                yield
                S.op("pe", lambda e: e.matmul(PS[bank].ap, lhsT=ONF.ap[0:64, :], rhs=DG.ap.rearrange("p a g q -> p (a g q)"), start=True, stop=True),
                     reads=[DG.all(), ONF.all()], writes=[PS[bank].all()])
                yield
                aimv = PS[bank].ap[:, 0:256].rearrange("p (g q) -> p g q", q=64)
                arev = PS[bank].ap[:, 256:512].rearrange("p (g q) -> p g q", q=64)
                S.op("dve", lambda e: e.tensor_scalar(out=Yv.ap, in0=aimv, scalar1=ccol(cw), scalar2=None, op0=ALU.mult),
                     reads=[PS[bank].all(), CONST.all()], writes=[Yv.all()])
                MGv = WRE
                S.op("act", lambda e: e.activation(out=MGv.ap, in_=arev, func=AF.Exp, scale=ccol(ct)),
                     reads=[PS[bank].all(), CONST.all()], writes=[MGv.all()])
                yield
                yield from g_trig(Yv, Rv, SNv, CSv, Y2v, R2v)
                if neg_im:
                    S.op("dve", lambda e: e.scalar_tensor_tensor(out=WIM.ap, in0=SNv.ap, scalar=-1.0, in1=MGv.ap, op0=ALU.mult, op1=ALU.mult),
                         reads=[SNv.all(), MGv.all()], writes=[WIM.all()])
                else:
                    tt("dve", WIM.ap, SNv.ap, MGv.ap, ALU.mult, [SNv.all(), MGv.all()], [WIM.all()])
                yield
                tt("dve", WRE.ap, CSv.ap, MGv.ap, ALU.mult, [CSv.all(), MGv.all()], [WRE.all()])
                yield

            def g_cmul(dre, dim_, drd, Z, wre, wim, xre, xim, xrd):
                tt("dve", Z[0].ap, wre.ap, xre, ALU.mult, [wre.all()] + xrd, [Z[0].all()])
                yield
                tt("dve", Z[1].ap, wim.ap, xim, ALU.mult, [wim.all()] + xrd, [Z[1].all()])
                yield
                tt("dve", Z[2].ap, wre.ap, xim, ALU.mult, [wre.all()] + xrd, [Z[2].all()])
                yield
                tt("dve", Z[3].ap, wim.ap, xre, ALU.mult, [wim.all()] + xrd, [Z[3].all()])
                yield
                tt("pool", dre, Z[0].ap, Z[1].ap, ALU.subtract, [Z[0].all(), Z[1].all()], drd)
                yield
                tt("pool", dim_, Z[2].ap, Z[3].ap, ALU.add, [Z[2].all(), Z[3].all()], drd)
                yield

            def emit_ct(b):
                g0 = b * 4
                M1, M2 = MLD[b % 2]
                CT1, CT2 = SLAB[b][2], SLAB[b][3]
                cv_re = cre_d[l, g0:g0 + 4].rearrange("g c p -> (g c) p")
                cv_im = cim_d[l, g0:g0 + 4].rearrange("g c p -> (g c) p")
                S.dma("sp", M1.ap[:, 0:64], cv_re, writes=[M1.rng(0, 64)])
                S.dma("sp", M1.ap[:, 64:128], cv_im, writes=[M1.rng(64, 128)])
                S.dma("sp", M2.ap[:, 0:64], cv_im, writes=[M2.rng(0, 64)])
                S.dma("sp", M2.ap[:, 64:128], cv_re, writes=[M2.rng(64, 128)])
                for (mld, ct, off) in ((M1, CT1, 0), (M2, CT2, 64)):
                    S.op("pe", lambda e, mld=mld, off=off: e.transpose(out=PS[6].ap[:, off:off + 64], in_=mld.ap, identity=id64),
                         reads=[mld.all(), CONST.all()], writes=[PS[6].rng(off, off + 64)])
                    S.op("act", lambda e, ct=ct, off=off: e.copy(out=ct.ap.rearrange("p g c -> p (g c)"), in_=PS[6].ap[:, off:off + 64]),
                         reads=[PS[6].rng(off, off + 64)], writes=[ct.all()])

            FS = []
            for s_ in range(2):
                tr = [palloc([4, 64], F32) for _ in range(6)]
                FS.append(dict(T1=palloc([4, 8, 16], F32), T2=palloc([4, 8, 16], F32), BNAT=palloc([4, 128], BF16), BBR=palloc([4, 128], BF16),
                               DG=palloc([2, 4, 64], F32, parts=64), TR=tr, WRE=palloc([4, 64], F32), WIM=palloc([4, 64], F32),
                               Z=[tr[0], tr[1], tr[4], tr[5]]))

            def front_batch(b):
                g0 = b * 4
                s_ = b % 2
                f = FS[s_]
                T1, T2, BNAT, BBR, DG, TR, WRE, WIM, Z = (f[k] for k in ("T1", "T2", "BNAT", "BBR", "DG", "TR", "WRE", "WIM", "Z"))
                emit_ct(b)
                yield
                BT1, BT2 = SLAB[b][0], SLAB[b][1]
                bt1b = BT1.ap.unsqueeze(2).broadcast_to([128, 4, 8, 16])
                bt2b = BT2.ap.unsqueeze(2).broadcast_to([128, 4, 8, 16])
                qa = QTA.ap[:, g0:g0 + 4, 0:8].unsqueeze(3).broadcast_to([128, 4, 8, 16])
                qb = QTB.ap[:, g0:g0 + 4, 0:8].unsqueeze(3).broadcast_to([128, 4, 8, 16])
                tt("dve", T1.ap, bt1b, qa, ALU.mult, [BT1.all(), QTA.all()], [T1.all()])
                tt("pool", T2.ap, bt2b, qb, ALU.mult, [BT2.all(), QTB.all()], [T2.all()])
                yield
                tg = g_tables(g0, s_, C_WM, C_TM, True, DG, TR[0], TR[1], TR[2], TR[3], TR[4], TR[5], WRE, WIM)
                next(tg)
                yield
                tt("dve", BNAT.ap.rearrange("p g (j c) -> p g j c", c=16), T1.ap, T2.ap, ALU.add, [T1.all(), T2.all()], [BNAT.all()])
                yield
                next(tg)
                yield
                bk = 4 + s_
                for gl in range(4):
                    S.op("pe", lambda e, gl=gl: e.transpose(out=psb(bk)[:, gl * 128:(gl + 1) * 128], in_=BNAT.ap[:, gl, :], identity=IDB.ap),
                         reads=[BNAT.rng(gl * 128, (gl + 1) * 128), IDB.all()], writes=[PS[bk].rng(gl * 64, (gl + 1) * 64)], inc=(gl == 3))
                yield
                next(tg)
                yield
                S.op("act", lambda e: e.copy(out=BBR.ap.rearrange("p g t -> p (g t)"), in_=psb(bk)[:, 0:512]), reads=[PS[bk].all()], writes=[BBR.all()])
                yield
                sb_ = 2 + s_
                for gl in range(4):
                    g = g0 + gl
                    S.op("pe", lambda e, gl=gl, g=g: e.matmul(PS[sb_].ap[:, gl * 128:(gl + 1) * 128], lhsT=UBLK.ap[:, g, :], rhs=BBR.ap[:, gl, :],
                                                              start=True, stop=True),
                         reads=[UBLK.rng(g * 128, (g + 1) * 128), BBR.rng(gl * 128, (gl + 1) * 128)],
                         writes=[PS[sb_].rng(gl * 128, (gl + 1) * 128)], inc=(gl == 3))
                yield
                for _ in tg:
                    yield
                sv = PS[sb_].ap.rearrange("p (g r q) -> p g r q", r=2, q=64)
                zw = [ZALL.rng(g0 * 128, (g0 + 4) * 128)]
                yield from g_cmul(ZALL.ap[:, g0:g0 + 4, 0, :], ZALL.ap[:, g0:g0 + 4, 1, :], zw, Z, WRE, WIM, sv[:, :, 0, :], sv[:, :, 1, :], [PS[sb_].all()])
                for gl in range(4):
                    g = g0 + gl
                    S.op("pe", lambda e, g=g: e.matmul(PS[7].ap[:, g:g + 1], lhsT=ZALL.ap[:, g].rearrange("p r q -> p (r q)"), rhs=ONB.ap[:, 0:1],
                                                       start=True, stop=True),
                         reads=[ZALL.rng(g * 128, (g + 1) * 128), ONB.all()], writes=[PS[7].rng(g, g + 1)], inc=(gl == 3))
            run_pipelined(front_batch, 16, depth=2)
            S.op("dve", lambda e: e.tensor_copy(out=HV1.ap, in_=PS[7].ap[:, 0:64]), reads=[PS[7].rng(0, 64)], writes=[HV1.all()])
            S.op("pe", lambda e: e.matmul(PS[6].ap[:, 0:64], lhsT=swapm, rhs=HV1.ap, start=True, stop=True),
                 reads=[HV1.all(), CONST.all()], writes=[PS[6].rng(0, 64)])
            tt("dve", HV2.ap, PS[6].ap[:, 0:64], A1I.ap, ALU.mult, [PS[6].rng(0, 64), A1I.all()], [HV2.all()])
            tt("dve", HV1.ap, HV1.ap, A1R.ap, ALU.mult, [HV1.all(), A1R.all()], [HV1.all()])
            tt("dve", HEND.ap, HV1.ap, HV2.ap, ALU.add, [HV1.all(), HV2.all()], [HEND.all()])
            S.op("pe", lambda e: e.transpose(out=PS[6].ap[0:64, 128:256], in_=HEND.ap, identity=ident),
                 reads=[HEND.all(), CONST.all()], writes=[PS[6].rng(128, 256)])
            S.op("dve", lambda e: e.tensor_copy(out=HENDT.ap, in_=PS[6].ap[0:64, 128:256]), reads=[PS[6].rng(128, 256)], writes=[HENDT.all()])
            S.dma("sp", cc1in[l].ap()[:, :], HENDT.ap, reads=[HENDT.all()], writes=[Acc("cc1in%d" % l, 0, 1)])
            S.op("pool", lambda e, l=l: e.collective_compute("AllGather", ALU.bypass, replica_groups=[[0, 1], [2, 3], [4, 5], [6, 7]],
                                                             ins=[cc1in[l].ap().opt()], outs=[cc1out[l].ap().opt()]),
                 reads=[Acc("cc1in%d" % l, 0, 1)], writes=[Acc("cc1out%d" % l, 0, 1)], custom_sem=("cc1", l))

            for a_ in ARN:
                a_.reset()
            BS = []
            for s_ in range(2):
                tr = [palloc([4, 64], F32) for _ in range(6)]
                BS.append(dict(A=palloc([4, 8, 16], F32), B=palloc([4, 8, 16], F32), CBR=palloc([4, 128], F32),
                               DG=palloc([2, 4, 64], F32, parts=64), TR=tr, WRE=palloc([4, 64], F32), WIM=palloc([4, 64], F32),
                               CBRb=palloc([4, 128], BF16), A0R=palloc([4, 128], BF16),
                               Z=[tr[0], tr[1], tr[4], tr[5]], HT=view(tr[2], [4, 2, 64], BF16), HP=view(tr[3], [4, 128], BF16)))

            def back_batch(b):
                g0 = b * 4
                s_ = b % 2
                f = BS[s_]
                A_, B_, CBR, DG, TR, WRE, WIM, CBRb, A0R, Z, HT, HP = (f[k] for k in ("A", "B", "CBR", "DG", "TR", "WRE", "WIM", "CBRb", "A0R", "Z", "HT", "HP"))
                BT1, BT2, CT1, CT2 = SLAB[b]
                bt1b = BT1.ap.unsqueeze(2).broadcast_to([128, 4, 8, 16])
                bt2b = BT2.ap.unsqueeze(2).broadcast_to([128, 4, 8, 16])
                qa2 = QTA.ap[:, g0:g0 + 4, 8:16].unsqueeze(3).broadcast_to([128, 4, 8, 16])
                qb2 = QTB.ap[:, g0:g0 + 4, 8:16].unsqueeze(3).broadcast_to([128, 4, 8, 16])
                tt("dve", A_.ap, bt1b, qa2, ALU.mult, [BT1.all(), QTA.all()], [A_.all()])
                tt("pool", B_.ap, bt2b, qb2, ALU.mult, [BT2.all(), QTB.all()], [B_.all()])
                yield
                tg = g_tables(g0, s_, C_WP, C_TP, False, DG, TR[0], TR[1], TR[2], TR[3], TR[4], TR[5], WRE, WIM)
                next(tg)
                yield
                XG = view(A_, [4, 128], F32)
                tt("dve", A_.ap, A_.ap, B_.ap, ALU.add, [A_.all(), B_.all()], [A_.all()])
                yield
                next(tg)
                yield
                ct1b = CT1.ap.unsqueeze(2).broadcast_to([128, 4, 8, 16])
                ct2b = CT2.ap.unsqueeze(2).broadcast_to([128, 4, 8, 16])
                la = LTA.ap[:, g0:g0 + 4, :].unsqueeze(3).broadcast_to([128, 4, 8, 16])
                lb = LTB.ap[:, g0:g0 + 4, :].unsqueeze(3).broadcast_to([128, 4, 8, 16])
                cbr4 = CBR.ap.rearrange("p g (j c) -> p g j c", c=16)
                tt("dve", cbr4, ct1b, la, ALU.mult, [CT1.all(), LTA.all()], [CBR.all()])
                tt("pool", B_.ap, ct2b, lb, ALU.mult, [CT2.all(), LTB.all()], [B_.all()])
                yield
                next(tg)
                yield
                tt("dve", cbr4, cbr4, B_.ap, ALU.add, [CBR.all(), B_.all()], [CBR.all()])
                yield
                next(tg)
                yield
                S.op("act", lambda e: e.copy(out=CBRb.ap, in_=CBR.ap), reads=[CBR.all()], writes=[CBRb.all()])
                ab = 2 + s_
                for gl in range(4):
                    S.op("pe", lambda e, gl=gl: e.matmul(PS[ab].ap[:, gl * 128:(gl + 1) * 128], lhsT=XG.ap[:, gl, :], rhs=CBR.ap[:, gl, :],
                                                         start=True, stop=True),
                         reads=[XG.rng(gl * 128, (gl + 1) * 128), CBR.rng(gl * 128, (gl + 1) * 128)],
                         writes=[PS[ab].rng(gl * 128, (gl + 1) * 128)], inc=(gl == 3))
                yield
                HINB = view(B_, [4, 2, 64], F32)
                S.dma("sp", HINB.ap.rearrange("p g r q -> p (g r q)"), cc1out[l].ap()[g0:g0 + 4, :].rearrange("g n -> (g n)").partition_broadcast(128),
                      reads=[Acc("cc1out%d" % l, 0, 1)], writes=[HINB.all()])
                pb = 4 + s_
                S.op("pe", lambda e: e.matmul(PS[pb].ap, lhsT=LSB.ap, rhs=ZALL.ap[:, g0:g0 + 4].rearrange("p g r q -> p (g r q)"), start=True, stop=True),
                     reads=[ZALL.rng(g0 * 128, (g0 + 4) * 128), LSB.all()], writes=[PS[pb].all()])
                yield
                next(tg)
                yield
                t3v = XG.ap
                tt("dve", t3v, PS[ab].ap.rearrange("p (g n) -> p g n", n=128), a0mask.unsqueeze(1).broadcast_to([128, 4, 128]),
                   ALU.mult, [PS[ab].all(), CONST.all()], [A_.all()])
                yield
                for gl in range(4):
                    g = g0 + gl
                    S.op("dve", lambda e, gl=gl, g=g: e.scalar_tensor_tensor(out=A0R.ap[:, gl, :], in0=ident, scalar=DCOL.ap[:, g:g + 1],
                                                                             in1=t3v[:, gl, :], op0=ALU.mult, op1=ALU.add),
                         reads=[CONST.all(), DCOL.all(), A_.all()], writes=[A0R.rng(gl * 128, (gl + 1) * 128)])
                    yield
                    try:
                        next(tg)
                    except StopIteration:
                        pass
                    yield
                for _ in tg:
                    yield
                PF = view(A_, [4, 2, 64], F32)
                S.op("dve", lambda e: e.scalar_tensor_tensor(out=PF.ap.rearrange("p g r q -> p (g r q)"), in0=HINB.ap.rearrange("p g r q -> p (g r q)"),
                                                             scalar=MCOL.ap[:, 0:1], in1=PS[pb].ap, op0=ALU.mult, op1=ALU.add),
                     reads=[HINB.all(), MCOL.all(), PS[pb].all()], writes=[PF.all()])
                yield
                pfv = PF.ap
                yield from g_cmul(HT.ap[:, :, 0, :], HT.ap[:, :, 1, :], [HT.all()], Z, WRE, WIM, pfv[:, :, 0, :], pfv[:, :, 1, :], [PF.all()])
                for gl in range(4):
                    S.op("pe", lambda e, gl=gl: e.transpose(out=psb(pb)[:, gl * 128:(gl + 1) * 128], in_=HT.ap[:, gl].rearrange("p r q -> p (r q)"), identity=IDB.ap),
                         reads=[HT.rng(gl * 128, (gl + 1) * 128), IDB.all()], writes=[PS[pb].rng(gl * 64, (gl + 1) * 64)], inc=(gl == 3))
                yield
                S.op("act", lambda e: e.copy(out=HP.ap.rearrange("p g t -> p (g t)"), in_=psb(pb)[:, 0:512]), reads=[PS[pb].all()], writes=[HP.all()])
                yield
                for gl in range(4):
                    g = g0 + gl
                    S.op("pe", lambda e, gl=gl: e.matmul(PS[6].ap[:, gl * 128:(gl + 1) * 128], lhsT=HP.ap[:, gl, :], rhs=CBRb.ap[:, gl, :],
                                                         start=True, stop=False),
                         reads=[HP.rng(gl * 128, (gl + 1) * 128), CBRb.rng(gl * 128, (gl + 1) * 128)],
                         writes=[PS[6].rng(gl * 128, (gl + 1) * 128)], inc=False)
                    S.op("pe", lambda e, gl=gl, g=g: e.matmul(PS[6].ap[:, gl * 128:(gl + 1) * 128], lhsT=UBLK.ap[:, g, :], rhs=A0R.ap[:, gl, :],
                                                              start=False, stop=True),
                         reads=[UBLK.rng(g * 128, (g + 1) * 128), A0R.rng(gl * 128, (gl + 1) * 128)],
                         writes=[PS[6].rng(gl * 128, (gl + 1) * 128)], inc=True)
                yield
                hb = b % 2
                S.op("act", lambda e: e.activation(out=T2G.ap[:, :, hb * 4:(hb + 1) * 4, :],
                                                   in_=PS[6].ap.rearrange("p (g i c) -> p i g c", g=4, i=8), func=AF.Gelu_apprx_tanh),
                     reads=[PS[6].all()], writes=[T2G.all()])
                if hb == 1:
                    yield
                    q = b // 2
                    for i in range(8):
                        S.op("pe", lambda e, i=i: e.transpose(out=psb(7)[:, i * 128:(i + 1) * 128], in_=T2G.ap[:, i].rearrange("p g c -> p (g c)"), identity=IDB.ap),
                             reads=[T2G.rng(i * 128, (i + 1) * 128), IDB.all()], writes=[PS[7].rng(i * 64, (i + 1) * 64)], inc=(i == 7))
                    yield
                    S.op("dve", lambda e: e.tensor_copy(out=YB.ap[:, q, :], in_=psb(7)), reads=[PS[7].all()], writes=[YB.rng(q * 1024, (q + 1) * 1024)])
            run_pipelined(back_batch, 16, depth=2)

            tap("yb", YB, [8, 1024], BF16, l)
            tap("zall", ZALL, [64, 2, 64], BF16, l)
            RUT.reset()
            VT = RUT.alloc([1024], F32)
            CV = RUT.alloc([1024], F32)
            ACC = RUT.alloc([1024], F32)
            SZ = RUT.alloc([1024], F32)
            jobs = []
            for c in range(16):
                def mk_v(c):
                    def f(slot, banks):
                        proj(slot, 16, H, banks)
                        for hf in range(2):
                            S.op("act", lambda e, hf=hf: e.copy(out=VT.ap[:, hf * 512:(hf + 1) * 512], in_=PS[banks[hf]].ap),
                                 reads=[PS[banks[hf]].all()], writes=[VT.rng(hf * 512, (hf + 1) * 512)])
                    return f

                def mk_cg(c):
                    def f(slot, banks):
                        proj(slot, 16, H, banks)
                        for hf in range(2):
                            S.op("dve", lambda e, hf=hf: e.tensor_tensor(out=CV.ap[:, hf * 512:(hf + 1) * 512], in0=PS[banks[hf]].ap,
                                                                         in1=VT.ap[:, hf * 512:(hf + 1) * 512], op=ALU.mult),
                                 reads=[PS[banks[hf]].all(), VT.rng(hf * 512, (hf + 1) * 512)], writes=[CV.rng(hf * 512, (hf + 1) * 512)])
                        w = lambda t, l=l: CW.ap[:, l, t, c:c + 1]
                        cv3 = CV.ap.rearrange("p (j t) -> p j t", t=128)
                        ac3 = ACC.ap.rearrange("p (j t) -> p j t", t=128)
                        S.op("dve", lambda e: e.tensor_scalar(out=ACC.ap, in0=CV.ap, scalar1=w(2), scalar2=None, op0=ALU.mult),
                             reads=[CV.all(), CW.all()], writes=[ACC.all()])
                        for (o, i, t) in ((ac3[:, 1:8, :], cv3[:, 0:7, :], 1), (ac3[:, 0, 1:128], cv3[:, 7, 0:127], 1),
                                          (ac3[:, 2:8, :], cv3[:, 0:6, :], 0), (ac3[:, 0:2, 1:128], cv3[:, 6:8, 0:127], 0)):
                            S.op("dve", lambda e, o=o, i=i, t=t: e.scalar_tensor_tensor(out=o, in0=i, scalar=w(t), in1=o, op0=ALU.mult, op1=ALU.add),
                                 reads=[CV.all(), ACC.all(), CW.all()], writes=[ACC.all()])
                        S.op("act", lambda e: e.copy(out=TAIL.ap[:, c, :], in_=cv3[:, 6:8, 127]), reads=[CV.all()], writes=[TAIL.rng(c * 2, c * 2 + 2)])
                    return f

                def mk_bg(c):
                    def f(slot, banks):
                        proj(slot, 16, H, banks)
                        S.op("act", lambda e: e.copy(out=BGT.ap, in_=PS[banks[0]].ap.rearrange("p (j t) -> p j t", t=128)[:, 0:2, 0]),
                             reads=[PS[banks[0]].all()], writes=[BGT.all()])
                        for hf in range(2):
                            S.op("dve", lambda e, hf=hf: e.tensor_tensor(out=VT.ap[:, hf * 512:(hf + 1) * 512], in0=PS[banks[hf]].ap,
                                                                         in1=ACC.ap[:, hf * 512:(hf + 1) * 512], op=ALU.mult),
                                 reads=[PS[banks[hf]].all(), ACC.rng(hf * 512, (hf + 1) * 512)], writes=[VT.rng(hf * 512, (hf + 1) * 512)])
                    return f

                def mk_za(c):
                    def f(slot, banks):
                        proj(slot, 16, H, banks)
                        for hf in range(2):
                            S.op("act", lambda e, hf=hf: e.activation(out=SZ.ap[:, hf * 512:(hf + 1) * 512], in_=PS[banks[hf]].ap, func=AF.Silu),
                                 reads=[PS[banks[hf]].all()], writes=[SZ.rng(hf * 512, (hf + 1) * 512)])
                        S.op("dve", lambda e: e.tensor_tensor(out=GATE01.ap[:, c, :], in0=BGT.ap, in1=SZ.ap.rearrange("p (j t) -> p j t", t=128)[:, 0:2, 0], op=ALU.mult),
                             reads=[BGT.all(), SZ.all()], writes=[GATE01.rng(c * 2, c * 2 + 2)])
                        S.op("dve", lambda e: e.tensor_tensor(out=AO.ap[:, c, :], in0=VT.ap, in1=SZ.ap, op=ALU.mult),
                             reads=[VT.all(), SZ.all()], writes=[AO.rng(c * 1024, (c + 1) * 1024)])
                    return f
                jobs.append((win_d[l, :, OFF_V + c * 128:OFF_V + (c + 1) * 128], 16, mk_v(c)))
                jobs.append((win_d[l, :, OFF_CG + c * 128:OFF_CG + (c + 1) * 128], 16, mk_cg(c)))
                jobs.append((win_d[l, :, OFF_BG + c * 128:OFF_BG + (c + 1) * 128], 16, mk_bg(c)))
                jobs.append((win_d[l, :, OFF_ZA + c * 128:OFF_ZA + (c + 1) * 128], 16, mk_za(c)))
            run_jobs(jobs)

            S.dma("sp", cc2in[l].ap()[:, :].rearrange("r (a j) -> (r a) j", j=32), TAIL.ap.rearrange("p c t -> p (c t)"),
                  reads=[TAIL.all()], writes=[Acc("cc2in%d" % l, 0, 1)])
            S.op("pool", lambda e, l=l: e.collective_compute("AllGather", ALU.bypass, replica_groups=[[0, 1], [2, 3], [4, 5], [6, 7]],
                                                             ins=[cc2in[l].ap().opt()], outs=[cc2out[l].ap().opt()]),
                 reads=[Acc("cc2in%d" % l, 0, 1)], writes=[Acc("cc2out%d" % l, 0, 1)], custom_sem=("cc2", l))
            S.dma("sp", TAILP.ap.rearrange("p c t -> p (c t)"), cc2out[l].ap()[0:32, :].rearrange("r (a j) -> (r a) j", j=32),
                  reads=[Acc("cc2out%d" % l, 0, 1)], writes=[TAILP.all()])
            S.op("dve", lambda e: e.tensor_scalar(out=TAILP.ap, in0=TAILP.ap, scalar1=MCOL.ap[:, 0:1], scalar2=None, op0=ALU.mult),
                 reads=[TAILP.all(), MCOL.all()], writes=[TAILP.all()])
            tts = lambda o, a_, b_, op, rd, wr: S.op("dve", lambda e: e.tensor_tensor(out=o, in0=a_, in1=b_, op=op), reads=rd, writes=wr)
            cm1, cm2 = TAILP.ap[:, :, 1], TAILP.ap[:, :, 0]
            w0, w1 = CW.ap[:, l, 0, :], CW.ap[:, l, 1, :]
            tts(DL.ap[:, 0, :], cm1, w1, ALU.mult, [TAILP.all(), CW.all()], [DL.rng(0, 16)])
            tts(DL.ap[:, 1, :], cm2, w0, ALU.mult, [TAILP.all(), CW.all()], [DL.rng(16, 32)])
            tts(DL.ap[:, 0, :], DL.ap[:, 0, :], DL.ap[:, 1, :], ALU.add, [DL.rng(0, 32)], [DL.rng(0, 16)])
            tts(DL.ap[:, 0, :], DL.ap[:, 0, :], GATE01.ap[:, :, 0], ALU.mult, [DL.rng(0, 16), GATE01.all()], [DL.rng(0, 16)])
            tts(DL.ap[:, 2, :], cm1, w0, ALU.mult, [TAILP.all(), CW.all()], [DL.rng(32, 48)])
            tts(DL.ap[:, 2, :], DL.ap[:, 2, :], GATE01.ap[:, :, 1], ALU.mult, [DL.rng(32, 48), GATE01.all()], [DL.rng(32, 48)])
            tts(AO.ap[:, :, 0], AO.ap[:, :, 0], DL.ap[:, 0, :], ALU.add, [AO.all(), DL.rng(0, 16)], [AO.all()])
            tts(AO.ap[:, :, 128], AO.ap[:, :, 128], DL.ap[:, 2, :], ALU.add, [AO.all(), DL.rng(32, 48)], [AO.all()])

            tap("ao", AO, [16, 1024], BF16, l)
            RT.reset()
            TA = RT.alloc([1024], F32)
            TB = RT.alloc([1024], F32)
            jobs = []
            for eo in range(8):
                def mk_glu(eo):
                    def f(slot, banks):
                        proj(slot, 8, YB, banks)
                        for hf in range(2):
                            S.op("act", lambda e, hf=hf, l=l: e.activation(out=TA.ap[:, hf * 512:(hf + 1) * 512], in_=PS[banks[hf]].ap, func=AF.Sigmoid,
                                                                           bias=BGL.ap[:, l, eo:eo + 1]),
                                 reads=[PS[banks[hf]].all(), BGL.all()], writes=[TA.rng(hf * 512, (hf + 1) * 512)])
                    return f

                def mk_zb(eo):
                    def f(slot, banks):
                        proj(slot, 16, H, banks)
                        for hf in range(2):
                            S.op("act", lambda e, hf=hf: e.activation(out=TB.ap[:, hf * 512:(hf + 1) * 512], in_=PS[banks[hf]].ap, func=AF.Silu),
                                 reads=[PS[banks[hf]].all()], writes=[TB.rng(hf * 512, (hf + 1) * 512)])
                        S.op("dve", lambda e: e.tensor_tensor(out=GATE.ap[:, eo, :], in0=TA.ap, in1=TB.ap, op=ALU.mult),
                             reads=[TA.all(), TB.all()], writes=[GATE.rng(eo * 1024, (eo + 1) * 1024)])
                        if eo == 7:
                            for e2 in range(8):
                                S.op("dve", lambda e, e2=e2: e.tensor_tensor(out=YB.ap[:, e2, :], in0=YB.ap[:, e2, :], in1=GATE.ap[:, e2, :], op=ALU.mult),
                                     reads=[YB.rng(e2 * 1024, (e2 + 1) * 1024), GATE.rng(e2 * 1024, (e2 + 1) * 1024)], writes=[YB.rng(e2 * 1024, (e2 + 1) * 1024)])
                            tap("yb2", YB, [8, 1024], BF16, l)
                    return f
                jobs.append((wglu_d[l, :, eo * 128:(eo + 1) * 128], 8, mk_glu(eo)))
                jobs.append((win_d[l, :, OFF_ZB + eo * 128:OFF_ZB + (eo + 1) * 128], 16, mk_zb(eo)))
            for c in range(16):
                def mk_g(c, dst):
                    def f(slot, banks):
                        proj(slot, 16, H, banks)
                        for hf in range(2):
                            S.op("act", lambda e, hf=hf: e.activation(out=dst.ap[:, hf * 512:(hf + 1) * 512], in_=PS[banks[hf]].ap, func=AF.Sigmoid),
                                 reads=[PS[banks[hf]].all()], writes=[dst.rng(hf * 512, (hf + 1) * 512)])
                    return f

                def mk_wa(c):
                    def f(slot, banks):
                        proj(slot, 16, AO, banks)
                        for hf in range(2):
                            S.op("dve", lambda e, hf=hf: e.tensor_tensor(out=TA.ap[:, hf * 512:(hf + 1) * 512], in0=PS[banks[hf]].ap,
                                                                         in1=TA.ap[:, hf * 512:(hf + 1) * 512], op=ALU.mult),
                                 reads=[PS[banks[hf]].all(), TA.rng(hf * 512, (hf + 1) * 512)], writes=[TA.rng(hf * 512, (hf + 1) * 512)])
                    return f

                def mk_wb(c):
                    def f(slot, banks):
                        proj(slot, 8, YB, banks)
                        for hf in range(2):
                            S.op("dve", lambda e, hf=hf: e.tensor_tensor(out=TB.ap[:, hf * 512:(hf + 1) * 512], in0=PS[banks[hf]].ap,
                                                                         in1=TB.ap[:, hf * 512:(hf + 1) * 512], op=ALU.mult),
                                 reads=[PS[banks[hf]].all(), TB.rng(hf * 512, (hf + 1) * 512)], writes=[TB.rng(hf * 512, (hf + 1) * 512)])
                        S.op("dve", lambda e: e.tensor_tensor(out=M.ap[:, c, :], in0=TA.ap, in1=TB.ap, op=ALU.add),
                             reads=[TA.all(), TB.all()], writes=[M.rng(c * 1024, (c + 1) * 1024)])
                    return f
                jobs.append((win_d[l, :, OFF_GA + c * 128:OFF_GA + (c + 1) * 128], 16, mk_g(c, TA)))
                jobs.append((wa_d[l, :, c * 128:(c + 1) * 128], 16, mk_wa(c)))
                jobs.append((win_d[l, :, OFF_GB + c * 128:OFF_GB + (c + 1) * 128], 16, mk_g(c, TB)))
                jobs.append((wb_d[l, :, c * 128:(c + 1) * 128], 8, mk_wb(c)))
            for c in range(16):
                def mk_wo(c):
                    def f(slot, banks):
                        proj(slot, 16, M, banks)
                        for hf in range(2):
                            S.op("dve", lambda e, hf=hf: e.tensor_tensor(out=X.ap[:, c, hf * 512:(hf + 1) * 512], in0=X.ap[:, c, hf * 512:(hf + 1) * 512],
                                                                         in1=PS[banks[hf]].ap, op=ALU.add),
                                 reads=[PS[banks[hf]].all(), X.rng(c * 1024 + hf * 512, c * 1024 + (hf + 1) * 512)],
                                 writes=[X.rng(c * 1024 + hf * 512, c * 1024 + (hf + 1) * 512)])
                    return f
                jobs.append((wo_d[l, :, c * 128:(c + 1) * 128], 16, mk_wo(c)))
            run_jobs(jobs)

        tap("xf", X, [16, 1024], F32)
        rmsnorm(lambda k: FG.ap[:, k:k + 1], False)
        RAO.reset()
        OST = [RAO.alloc([2048], F32) for _ in range(2)]
        ov = out_d.rearrange("(t j) d -> j t d", j=8)
        out_events = []
        for j in range(8):
            st = OST[j % 2]
            for kq in range(4):
                bank = (j * 4 + kq) % 4
                for kk in range(4):
                    k = kq * 4 + kk
                    S.op("pe", lambda e, k=k, kk=kk, bank=bank, j=j: e.transpose(
                        out=PS[bank].ap[:, kk * 128:(kk + 1) * 128], in_=X.ap[:, k, j * 128:(j + 1) * 128], identity=ident),
                        reads=[X.rng(k * 1024 + j * 128, k * 1024 + (j + 1) * 128), CONST.all()],
                        writes=[PS[bank].rng(kk * 128, (kk + 1) * 128)], inc=(kk == 3))
                eng = "act" if kq % 2 == 0 else "dve"
                o = st.ap[:, kq * 512:(kq + 1) * 512]
                if eng == "act":
                    S.op("act", lambda e, o=o, bank=bank: e.copy(out=o, in_=PS[bank].ap), reads=[PS[bank].all()], writes=[st.rng(kq * 512, (kq + 1) * 512)])
                else:
                    S.op("dve", lambda e, o=o, bank=bank: e.tensor_copy(out=o, in_=PS[bank].ap), reads=[PS[bank].all()], writes=[st.rng(kq * 512, (kq + 1) * 512)])
            out_events.append(S.dma("sp", ov[j], st.ap, reads=[st.all()], writes=[Acc("out", j, j + 1)]))
        S.wait_all("sp", out_events + dbg_events)

        sems = {k: es.enter_context(nc.semaphore("s_" + "_".join(map(str, k)))) for k in sorted(S.sem_names, key=str)}
        es.enter_context(nc.allow_non_contiguous_dma(reason="small strided parameter loads"))
        block = es.enter_context(nc.Block())
        S.replay(block, sems)
    return nc, S


_CACHE = {}


def _consts():
    c = np.zeros((128, NCONST), np.float32)
    c[:, C_IDENT:C_IDENT + 128] = np.eye(128, dtype=np.float32)
    jj = np.arange(128) // 16
    c[:, C_A0MASK:C_A0MASK + 128] = (jj[None, :] >= jj[:, None]).astype(np.float32)
    sw = np.zeros((128, 128), np.float32)
    for p in range(64):
        sw[p, 64 + p] = 1.0
        sw[64 + p, p] = 1.0
    c[:, C_SWAP:C_SWAP + 128] = sw
    t = np.arange(128, dtype=np.float64)
    c[:, C_TM] = -8.0 * (t + 1)
    c[:, C_TP] = 8.0 * t
    c[:, C_WM] = (t + 1) * 8.0 / (2 * np.pi)
    c[:, C_WP] = t * 8.0 / (2 * np.pi)
    c[:64, C_SGN] = -1.0
    c[64:, C_SGN] = 1.0
    c[:, C_EPS] = 1e-6
    ii = np.arange(128)
    c[:, C_LS:C_LS + 128] = (ii[:, None] < ii[None, :]).astype(np.float32)
    return c


def kernel(_debug=(), **inputs):
    key = ("nc", tuple(sorted(_debug)))
    if key not in _CACHE:
        _CACHE[key] = build_program(debug=set(_debug))[0]
    nc = _CACHE[key]
    x = np.ascontiguousarray(inputs["x"], dtype=np.float32)
    consts = _consts()
    shared = {k: np.ascontiguousarray(inputs[k], dtype=np.float32) for k in
              ("norm_g", "w_in", "conv_w", "w_out_a", "a_re", "a_im", "log_dt", "b_re", "b_im", "c_re", "c_im",
               "d_skip", "w_glu", "b_glu", "w_out_b", "w_o", "final_g")}
    in_maps = []
    for c in range(8):
        b, half = c // 2, c % 2
        m = dict(shared)
        m["x"] = np.ascontiguousarray(x[b, half * NT:(half + 1) * NT, :])
        m["consts"] = consts
        m["maskcol"] = np.full((128, 1), float(half), np.float32)
        in_maps.append(m)
    res = run_bass_kernel_spmd(nc, in_maps, core_ids=list(range(8)))
    out = np.empty((4, 2048, 2048), np.float32)
    for c in range(8):
        b, half = c // 2, c % 2
        out[b, half * NT:(half + 1) * NT, :] = res.results[c]["out"]
    if _debug:
        return out, res.results
    return out
```

```python
import contextlib
import math
import numpy as np
import concourse.bass as bass
import concourse.mybir as mybir
from concourse.bass_utils import run_bass_kernel_spmd

F32 = mybir.dt.float32
BF16 = mybir.dt.bfloat16
AF = mybir.ActivationFunctionType
ALU = mybir.AluOpType

ENGS = ("pe", "act", "dve", "pool", "sp")
SAME_ENGINE_SYNC = True
DEPTH = 2
NT = 1024
D = 2048
KC = 16
NIN = 14336
MAGIC = 12582912.0
TWO_PI = 2.0 * math.pi
OFF_V, OFF_BG, OFF_CG, OFF_ZA, OFF_U, OFF_ZB, OFF_GA, OFF_GB = 0, 2048, 4096, 6144, 8192, 9216, 10240, 12288

C_IDENT, C_A0MASK, C_SWAP, C_LS = 0, 128, 256, 384
C_TM, C_TP, C_WM, C_WP, C_SGN, C_EPS, C_Q = 512, 513, 514, 515, 516, 517, 518
NCONST = 520


class Acc:
    __slots__ = ("space", "lo", "hi")

    def __init__(self, space, lo, hi):
        self.space, self.lo, self.hi = space, lo, hi


class Buf:
    def __init__(self, space, lo, nbytes, ap, esz):
        self.space, self.lo, self.hi, self.ap, self.esz = space, lo, lo + nbytes, ap, esz

    def all(self):
        return Acc(self.space, self.lo, self.hi)

    def rng(self, a, b):
        return Acc(self.space, self.lo + a * self.esz, self.lo + b * self.esz)


class Sched:
    def __init__(self, nc, n_dma_sems=16):
        self.nc = nc
        self.ops = {e: [] for e in ENGS}
        self.cnt = {e: 0 for e in ENGS}
        self.pending = {e: [] for e in ENGS}
        self.wr = {}
        self.rd = {}
        self.n_dma_sems = n_dma_sems
        self.dma_cnt = {}
        self.dma_rr = {e: 0 for e in ENGS}
        self.sem_names = set()
        self.nops = 0
        self.ps_last = {}

    def _deps(self, reads, writes):
        deps = []
        for a in reads:
            for (lo, hi, ev) in self.wr.get(a.space, ()):
                if lo < a.hi and a.lo < hi:
                    deps.append((ev, True))
        for a in writes:
            for (lo, hi, ev) in self.wr.get(a.space, ()):
                if lo < a.hi and a.lo < hi:
                    deps.append((ev, False))
            for (lo, hi, ev) in self.rd.get(a.space, ()):
                if lo < a.hi and a.lo < hi:
                    deps.append((ev, False))
        return deps

    def _record(self, reads, writes, ev):
        for a in writes:
            wl = self.wr.setdefault(a.space, [])
            wl[:] = [w for w in wl if not (a.lo <= w[0] and w[1] <= a.hi)]
            wl.append((a.lo, a.hi, ev))
            rl = self.rd.setdefault(a.space, [])
            rl[:] = [r for r in rl if not (a.lo <= r[0] and r[1] <= a.hi)]
        for a in reads:
            rl = self.rd.setdefault(a.space, [])
            rl[:] = [r for r in rl if not (r[0] == a.lo and r[1] == a.hi and r[2][0] == ev[0])]
            rl.append((a.lo, a.hi, ev))

    def _emit(self, eng, fn, reads, writes, inc=True, dma=False, custom_sem=None):
        self.nops += 1
        deps = self._deps(reads, writes)
        banks = set()
        for a in tuple(reads) + tuple(writes):
            if a.space == "ps":
                for bnk in range(a.lo // 2048, (a.hi - 1) // 2048 + 1):
                    banks.add(bnk)
        for bnk in banks:
            for oe, oev in self.ps_last.get(bnk, {}).items():
                if oe != eng:
                    deps.append((oev, True))
        waits = []
        for ev, is_raw in deps:
            if ev[0] == ("eng", eng) and (eng == "pe" or not SAME_ENGINE_SYNC or not is_raw):
                continue
            waits.append(ev)
        if custom_sem is not None:
            semkey = custom_sem
            ev = [semkey, 1]
            incspec = (semkey, 1)
        elif dma:
            r = self.dma_rr[eng]
            self.dma_rr[eng] = (r + 1) % self.n_dma_sems
            semkey = ("dma", eng, r)
            if self.dma_cnt.get(semkey, 0) > 0:
                waits.append([semkey, self.dma_cnt[semkey]])
            self.dma_cnt[semkey] = self.dma_cnt.get(semkey, 0) + 16
            ev = [semkey, self.dma_cnt[semkey]]
            incspec = (semkey, 16)
        elif inc:
            self.cnt[eng] += 1
            semkey = ("eng", eng)
            ev = [semkey, self.cnt[eng]]
            for pev in self.pending[eng]:
                pev[1] = self.cnt[eng]
            self.pending[eng] = []
            incspec = (semkey, 1)
        else:
            semkey = ("eng", eng)
            ev = [semkey, None]
            self.pending[eng].append(ev)
            incspec = None
        self.sem_names.add(semkey)
        self.ops[eng].append((fn, waits, incspec))
        self._record(reads, writes, ev)
        for bnk in banks:
            self.ps_last.setdefault(bnk, {})[eng] = ev
        return ev

    def op(self, eng, fn, reads=(), writes=(), inc=True, custom_sem=None):
        return self._emit(eng, fn, tuple(reads), tuple(writes), inc=inc, custom_sem=custom_sem)

    def dma(self, eng, out, in_, reads=(), writes=()):
        return self._emit(eng, lambda e: e.dma_start(out=out, in_=in_), tuple(reads), tuple(writes), dma=True)

    def wait_all(self, eng, events):
        self.ops[eng].append((None, list(events), None))

    def replay(self, block, sems):
        engmap = {"pe": block.tensor, "act": block.scalar, "dve": block.vector,
                  "pool": block.gpsimd, "sp": block.sync}
        for e in ENGS:
            assert not self.pending[e], f"pending non-inc'd ops on {e}"

        def make(e):
            oplist = self.ops[e]

            def body(eng):
                waited = {}
                for fn, waits, incspec in oplist:
                    need = {}
                    for semkey, val in waits:
                        assert val is not None
                        if waited.get(semkey, 0) >= val:
                            continue
                        need[semkey] = max(need.get(semkey, 0), val)
                    for semkey, val in need.items():
                        eng.wait_ge(sems[semkey], val)
                        waited[semkey] = val
                    if fn is None:
                        continue
                    ins = fn(eng)
                    if incspec is not None:
                        ins.then_inc(sems[incspec[0]], incspec[1])
            return body

        for e in ENGS:
            if self.ops[e]:
                engmap[e](make(e))


class Arena:
    def __init__(self, tensor, base, nbytes):
        self.t, self.base, self.nbytes, self.off = tensor, base, nbytes, 0

    def reset(self):
        self.off = 0

    def sub(self, off, nbytes):
        a = Arena(self.t, self.base + off, nbytes)
        a.t0 = getattr(self, "t0", 0) + off
        return a

    def alloc(self, shape, dt, parts=128):
        esz = 4 if dt == F32 else 2
        n = 1
        for s in shape:
            n *= s
        nb = (n * esz + 31) // 32 * 32
        assert self.off + nb <= self.nbytes, ("arena overflow", self.off, nb, self.nbytes)
        o4 = (getattr(self, 't0', 0) + self.off) // 4
        v = self.t[0:parts, o4:o4 + nb // 4]
        if dt != F32:
            v = v.bitcast(dt)
        v = v[:, 0:n]
        if len(shape) == 2:
            v = v.rearrange("p (a b) -> p a b", b=shape[1])
        elif len(shape) == 3:
            v = v.rearrange("p (a b c) -> p a b c", b=shape[1], c=shape[2])
        elif len(shape) == 4:
            v = v.rearrange("p (a b c d) -> p a b c d", b=shape[1], c=shape[2], d=shape[3])
        b = Buf("sb", self.base + self.off, n * esz, v, esz)
        self.off += nb
        return b


def build_program(debug=()):
    nc = bass.Bass("TRN2", target_bir_lowering=False)
    dbg_events = []

    def tap(name, buf, shape, dt, layer=0):
        if name not in debug or layer != 0:
            return
        d = nc.dram_tensor("dbg_" + name, [128] + list(shape), dt, kind="ExternalOutput").ap()
        dbg_events.append(S.dma("sp", d, buf.ap, reads=[buf.all()], writes=[Acc("dbg_" + name, 0, 1)]))
    dt_in = lambda name, shape: nc.dram_tensor(name, shape, F32, kind="ExternalInput").ap()
    x_d = dt_in("x", [NT, D])
    normg_d = dt_in("norm_g", [DEPTH, D])
    win_d = dt_in("w_in", [DEPTH, D, NIN])
    convw_d = dt_in("conv_w", [DEPTH, 3, D])
    wa_d = dt_in("w_out_a", [DEPTH, D, D])
    are_d = dt_in("a_re", [DEPTH, 64, 64])
    aim_d = dt_in("a_im", [DEPTH, 64, 64])
    ldt_d = dt_in("log_dt", [DEPTH, 64])
    bre_d = dt_in("b_re", [DEPTH, 64, 64, 16])
    bim_d = dt_in("b_im", [DEPTH, 64, 64, 16])
    cre_d = dt_in("c_re", [DEPTH, 64, 16, 64])
    cim_d = dt_in("c_im", [DEPTH, 64, 16, 64])
    dsk_d = dt_in("d_skip", [DEPTH, 64, 16])
    wglu_d = dt_in("w_glu", [DEPTH, 1024, 1024])
    bglu_d = dt_in("b_glu", [DEPTH, 1024])
    wb_d = dt_in("w_out_b", [DEPTH, 1024, D])
    wo_d = dt_in("w_o", [DEPTH, D, D])
    fg_d = dt_in("final_g", [D])
    const_d = dt_in("consts", [128, NCONST])
    mcol_d = dt_in("maskcol", [128, 1])
    out_d = nc.dram_tensor("out", [NT, D], F32, kind="ExternalOutput").ap()
    cc1in = [nc.dram_tensor(f"cc1in{l}", [64, 128], F32) for l in range(DEPTH)]
    cc1out = [nc.dram_tensor(f"cc1out{l}", [128, 128], F32) for l in range(DEPTH)]
    cc2in = [nc.dram_tensor(f"cc2in{l}", [32, 128], F32) for l in range(DEPTH)]
    cc2out = [nc.dram_tensor(f"cc2out{l}", [64, 128], F32) for l in range(DEPTH)]

    S = Sched(nc)
    es = contextlib.ExitStack()
    with es:
        def raw(name, nbytes):
            return es.enter_context(nc.sbuf_tensor(name, [128, nbytes // 4], F32))
        sb_off = [0]

        def region(name, nbytes):
            t = raw(name, nbytes)
            a = Arena(t, sb_off[0], nbytes)
            sb_off[0] += nbytes
            return a
        RX = region("RX", 65536)
        RH = region("RH", 32768)
        RAO = region("RAO", 32768)
        RM = region("RM", 32768)
        RYB = region("RYB", 16384)
        RWB = region("RWB", 16384)
        RT = region("RT", 8192)
        RS = region("RS", 6144)

        X = RX.alloc([16, 1024], F32)
        H = RH.alloc([16, 1024], BF16)
        AO = RAO.alloc([16, 1024], BF16)
        RAO.reset()
        WU = RAO.alloc([2, 16, 512], BF16)
        RAO.reset()
        XST = [RAO.alloc([2048], F32) for _ in range(2)]
        M = RM.alloc([16, 1024], BF16)
        RM.reset()
        RUT = RM.sub(0, 16384)
        RUB = RM.sub(16384, 16384)
        UT = RUT.alloc([64, 8, 16], BF16)
        RUT.reset()
        UBLK = RUB.alloc([64, 128], BF16)
        GATE = RM.alloc([8, 1024], BF16)
        YB = RYB.alloc([8, 1024], BF16)
        NW = 4
        WB = [RWB.alloc([16, 128], BF16) for _ in range(NW)]
        CONST = RS.alloc([NCONST], F32)
        IDB = RS.alloc([128], BF16)
        LSB = RS.alloc([128], BF16)
        ONB = RS.alloc([128], BF16)
        ONF = RS.alloc([128], F32)
        MCOL = RS.alloc([1], F32)
        NG = RS.alloc([DEPTH, 16], F32)
        FG = RS.alloc([16], F32)
        CW = RS.alloc([DEPTH, 3, 16], F32)
        BGL = RS.alloc([DEPTH, 8], F32)
        TAIL = RS.alloc([16, 2], F32)
        GATE01 = RS.alloc([16, 2], F32)
        BGT = RS.alloc([2], F32)
        TAILP = RS.alloc([16, 2], F32)
        DL = RS.alloc([4, 16], F32)

        PS = []
        for b in range(8):
            t = es.enter_context(nc.psum_tensor(f"ps{b}", [128, 512], F32))
            PS.append(Buf("ps", b * 2048, 2048, t[:, :], 4))

        def psb(b):
            return PS[b].ap.bitcast(BF16)

        ident = CONST.ap[:, C_IDENT:C_IDENT + 128]
        a0mask = CONST.ap[:, C_A0MASK:C_A0MASK + 128]
        swapm = CONST.ap[:, C_SWAP:C_SWAP + 128]
        ccol = lambda c: CONST.ap[:, c:c + 1]

        dq = ["sp"]

        S.dma("sp", CONST.ap, const_d[:, :], writes=[CONST.all()])
        S.dma("sp", MCOL.ap, mcol_d[:, :], writes=[MCOL.all()])
        RT.reset()
        vec_jobs = []
        for l_ in range(DEPTH):
            vec_jobs.append((normg_d[l_].rearrange("(k p) -> k p", p=128), 16, NG.ap[:, l_, :], NG))
            vec_jobs.append((bglu_d[l_].rearrange("(k p) -> k p", p=128), 8, BGL.ap[:, l_, :], BGL))
            for t_ in range(3):
                vec_jobs.append((convw_d[l_, t_].rearrange("(k p) -> k p", p=128), 16, CW.ap[:, l_, t_, :], CW))
        vec_jobs.append((fg_d.rearrange("(k p) -> k p", p=128), 16, FG.ap, FG))
        for vi, (src, nk, dst_ap, dst_buf) in enumerate(vec_jobs):
            stg = RT.alloc([128], F32, parts=16)
            S.dma("sp", stg.ap[0:nk, :], src, writes=[stg.all()])
            bk = vi % 4
            S.op("pe", lambda e, stg=stg, nk=nk, bk=bk: e.transpose(out=PS[bk].ap[:, 0:nk], in_=stg.ap[0:nk, :], identity=CONST.ap[0:nk, 0:nk]),
                 reads=[stg.all(), CONST.all()], writes=[PS[bk].rng(0, 16)])
            S.op("dve", lambda e, dst_ap=dst_ap, nk=nk, bk=bk: e.tensor_copy(out=dst_ap, in_=PS[bk].ap[:, 0:nk]),
                 reads=[PS[bk].rng(0, 16)], writes=[dst_buf.all()])
        S.op("dve", lambda e: e.tensor_copy(out=IDB.ap, in_=ident), reads=[CONST.all()], writes=[IDB.all()])
        S.op("pool", lambda e: e.memset(ONB.ap, 1.0), writes=[ONB.all()])
        S.op("pool", lambda e: e.memset(ONF.ap, 1.0), writes=[ONF.all()])
        S.op("dve", lambda e: e.tensor_copy(out=LSB.ap, in_=CONST.ap[:, C_LS:C_LS + 128]), reads=[CONST.all()], writes=[LSB.all()])

        xv = x_d.rearrange("(t j) d -> j t d", j=8)
        for j in range(8):
            st = XST[j % 2]
            S.dma("sp", st.ap, xv[j], writes=[st.all()])
            for kq in range(4):
                bank = (j * 4 + kq) % 4
                for kk in range(4):
                    k = kq * 4 + kk
                    S.op("pe", lambda e, k=k, kk=kk, bank=bank, st=st: e.transpose(
                        out=PS[bank].ap[:, kk * 128:(kk + 1) * 128], in_=st.ap[:, k * 128:(k + 1) * 128], identity=ident),
                        reads=[st.rng(k * 128, (k + 1) * 128), CONST.all()], writes=[PS[bank].rng(kk * 128, (kk + 1) * 128)],
                        inc=(kk == 3))
                eng = "act" if kq % 2 == 0 else "dve"
                outap = X.ap[:, kq * 4:(kq + 1) * 4, j * 128:(j + 1) * 128]
                inap = PS[bank].ap.rearrange("p (a b) -> p a b", b=128)
                wr = [X.rng(k * 1024 + j * 128, k * 1024 + (j + 1) * 128) for k in range(kq * 4, kq * 4 + 4)]
                if eng == "act":
                    S.op("act", lambda e, o=outap, i=inap: e.copy(out=o, in_=i), reads=[PS[bank].all()], writes=wr)
                else:
                    S.op("dve", lambda e, o=outap, i=inap: e.tensor_copy(out=o, in_=i), reads=[PS[bank].all()], writes=wr)

        tap("x0", X, [16, 1024], F32)
        def rmsnorm(gcol_fn, out_h):
            RT.reset()
            SQ = [RT.alloc([1024], BF16) for _ in range(2)]
            RSTD = RT.alloc([1024], F32)
            for k in range(16):
                sq = SQ[k % 2]
                S.op("act", lambda e, k=k, sq=sq: e.activation(out=sq.ap, in_=X.ap[:, k, :], func=AF.Square),
                     reads=[X.rng(k * 1024, (k + 1) * 1024)], writes=[sq.all()])
                for hf in range(2):
                    S.op("pe", lambda e, k=k, hf=hf, sq=sq: e.matmul(PS[hf].ap, lhsT=ONB.ap, rhs=sq.ap[:, hf * 512:(hf + 1) * 512],
                                                                  start=(k == 0), stop=(k == 15)),
                         reads=[sq.rng(hf * 512, (hf + 1) * 512), ONB.all()], writes=[PS[hf].all()], inc=True)
            for hf in range(2):
                S.op("act", lambda e, hf=hf: e.activation(out=RSTD.ap[:, hf * 512:(hf + 1) * 512], in_=PS[hf].ap, func=AF.Sqrt,
                                                          scale=1.0 / D, bias=ccol(C_EPS)),
                     reads=[PS[hf].all(), CONST.all()], writes=[RSTD.rng(hf * 512, (hf + 1) * 512)])
            S.op("dve", lambda e: e.reciprocal(out=RSTD.ap, in_=RSTD.ap), reads=[RSTD.all()], writes=[RSTD.all()])
            for k in range(16):
                if out_h:
                    S.op("dve", lambda e, k=k: e.scalar_tensor_tensor(out=H.ap[:, k, :], in0=X.ap[:, k, :], scalar=gcol_fn(k),
                                                                      in1=RSTD.ap, op0=ALU.mult, op1=ALU.mult),
                         reads=[X.rng(k * 1024, (k + 1) * 1024), RSTD.all(), NG.all()], writes=[H.rng(k * 1024, (k + 1) * 1024)])
                else:
                    S.op("dve", lambda e, k=k: e.scalar_tensor_tensor(out=X.ap[:, k, :], in0=X.ap[:, k, :], scalar=gcol_fn(k),
                                                                      in1=RSTD.ap, op0=ALU.mult, op1=ALU.mult),
                         reads=[X.rng(k * 1024, (k + 1) * 1024), RSTD.all(), FG.all()], writes=[X.rng(k * 1024, (k + 1) * 1024)])

        wslot = [0]

        def wload(src_ap, nk):
            s = WB[wslot[0] % NW]
            wslot[0] += 1
            S.dma("pool", s.ap[:, 0:nk, :], src_ap.rearrange("(k p) n -> p k n", p=128),
                  writes=[s.rng(0, nk * 128)])
            return s

        def proj(slot, nk, rhs_buf, banks):
            for k in range(nk):
                for hf in range(2):
                    S.op("pe", lambda e, k=k, hf=hf: e.matmul(PS[banks[hf]].ap, lhsT=slot.ap[:, k, :],
                                                              rhs=rhs_buf.ap[:, k, hf * 512:(hf + 1) * 512],
                                                              start=(k == 0), stop=(k == nk - 1)),
                         reads=[slot.rng(k * 128, (k + 1) * 128), rhs_buf.rng(k * 1024 + hf * 512, k * 1024 + (hf + 1) * 512)],
                         writes=[PS[banks[hf]].all()], inc=(k == nk - 1))

        def run_jobs(jobs, pf=3):
            slots = [None] * len(jobs)
            for i in range(min(pf, len(jobs))):
                slots[i] = wload(jobs[i][0], jobs[i][1])
            for i, (src, nk, consume) in enumerate(jobs):
                if i + pf < len(jobs):
                    slots[i + pf] = wload(jobs[i + pf][0], jobs[i + pf][1])
                banks = ((0, 1), (2, 3), (4, 5), (6, 7))[i % 4]
                consume(slots[i], banks)

        for l in range(DEPTH):
            if l == 1:
                tap("x1", X, [16, 1024], F32)
            rmsnorm(lambda k, l=l: NG.ap[:, l, k:k + 1], True)

            tap("h", H, [16, 1024], BF16, l)
            tt = lambda eng, o, a_, b_, op, rd, wr: S.op(eng, lambda e: e.tensor_tensor(out=o, in0=a_, in1=b_, op=op), reads=rd, writes=wr)

            def trig(Yb, Rb, SNb, CSb):
                S.op("dve", lambda e: e.tensor_scalar(out=Rb.ap, in0=Yb.ap, scalar1=MAGIC, scalar2=MAGIC, op0=ALU.add, op1=ALU.subtract),
                     reads=[Yb.all()], writes=[Rb.all()])
                S.op("dve", lambda e: e.tensor_tensor(out=Rb.ap, in0=Yb.ap, in1=Rb.ap, op=ALU.subtract), reads=[Yb.all(), Rb.all()], writes=[Rb.all()])
                S.op("act", lambda e: e.activation(out=SNb.ap, in_=Rb.ap, func=AF.Sin, scale=TWO_PI), reads=[Rb.all()], writes=[SNb.all()])
                S.op("dve", lambda e: e.tensor_scalar(out=Yb.ap, in0=Yb.ap, scalar1=0.25, scalar2=None, op0=ALU.add), reads=[Yb.all()], writes=[Yb.all()])
                S.op("dve", lambda e: e.tensor_scalar(out=Rb.ap, in0=Yb.ap, scalar1=MAGIC, scalar2=MAGIC, op0=ALU.add, op1=ALU.subtract),
                     reads=[Yb.all(), SNb.all()], writes=[Rb.all()])
                S.op("dve", lambda e: e.tensor_tensor(out=Rb.ap, in0=Yb.ap, in1=Rb.ap, op=ALU.subtract), reads=[Yb.all(), Rb.all()], writes=[Rb.all()])
                S.op("act", lambda e: e.activation(out=CSb.ap, in_=Rb.ap, func=AF.Sin, scale=TWO_PI), reads=[Rb.all()], writes=[CSb.all()])

            RAO.reset()
            ZALL = RAO.alloc([64, 2, 64], BF16)
            RAO.reset()
            WUQ = [RAO.alloc([16, 256], BF16) for _ in range(2)]
            QTA = RAO.alloc([64, 16], F32)
            QTB = RAO.alloc([64, 16], F32)
            LTA = RAO.alloc([64, 8], F32)
            LTB = RAO.alloc([64, 8], F32)
            T2G = RAO.alloc([8, 8, 16], BF16)
            SLAB = []
            for b in range(16):
                sa = RYB.sub(b * 1024, 1024)
                SLAB.append(tuple(sa.alloc([4, 16], F32) for _ in range(4)))
            A = RWB
            A.reset()
            sm = lambda: A.alloc([64], F32)
            DCOL, A1R, A1I, HV1, HV2, HEND = sm(), sm(), sm(), sm(), sm(), sm()
            HENDT = A.alloc([128], F32, parts=64)
            XIK = A.alloc([64], F32)
            XRK = A.alloc([64], F32)
            keep = A.off
            NAT = A.alloc([2, 128], F32, parts=64)
            DNAT = A.alloc([128], F32, parts=64)
            MLD = [(A.alloc([128], F32, parts=64), A.alloc([128], F32, parts=64)) for _ in range(2)]
            ART, AIT, LDT, DTT, XR, XI = sm(), sm(), sm(), sm(), sm(), sm()
            Yp, Rp, SNp, CSp, MGp, MGI = sm(), sm(), sm(), sm(), sm(), sm()
            LBR, LBI, LIR, LII, BR, BI = sm(), sm(), sm(), sm(), sm(), sm()
            E1, E2, E3, E4 = sm(), sm(), sm(), sm()
            id64 = CONST.ap[0:64, 0:64]
            for hh in range(2):
                S.dma("sp", NAT.ap[:, 0, hh * 64:(hh + 1) * 64], are_d[l], writes=[NAT.rng(hh * 64, (hh + 1) * 64)])
                S.dma("sp", NAT.ap[:, 1, hh * 64:(hh + 1) * 64], aim_d[l], writes=[NAT.rng(128 + hh * 64, 128 + (hh + 1) * 64)])
            for j in range(8):
                S.dma("sp", DNAT.ap[:, j * 16:(j + 1) * 16], dsk_d[l], writes=[DNAT.rng(j * 16, (j + 1) * 16)])
            S.dma("sp", LDT.ap, ldt_d[l].partition_broadcast(128), writes=[LDT.all()])
            for i, (src, dst) in enumerate(((NAT.ap[:, 0, :], ART), (NAT.ap[:, 1, :], AIT), (DNAT.ap, DCOL))):
                rd = NAT.all() if i < 2 else DNAT.all()
                S.op("pe", lambda e, src=src: e.transpose(out=PS[6].ap[:, 0:64], in_=src, identity=id64),
                     reads=[rd, CONST.all()], writes=[PS[6].rng(0, 64)])
                S.op("dve", lambda e, dst=dst: e.tensor_copy(out=dst.ap, in_=PS[6].ap[:, 0:64]), reads=[PS[6].rng(0, 64)], writes=[dst.all()])
            S.op("act", lambda e: e.activation(out=DTT.ap, in_=LDT.ap, func=AF.Exp), reads=[LDT.all()], writes=[DTT.all()])
            tt("dve", XR.ap, ART.ap, DTT.ap, ALU.mult, [ART.all(), DTT.all()], [XR.all()])
            tt("dve", XI.ap, AIT.ap, DTT.ap, ALU.mult, [AIT.all(), DTT.all()], [XI.all()])
            S.op("dve", lambda e: e.tensor_scalar(out=Yp.ap, in0=XI.ap, scalar1=1.0 / TWO_PI, scalar2=None, op0=ALU.mult), reads=[XI.all()], writes=[Yp.all()])
            trig(Yp, Rp, SNp, CSp)
            S.op("act", lambda e: e.activation(out=MGp.ap, in_=XR.ap, func=AF.Exp), reads=[XR.all()], writes=[MGp.all()])
            S.op("act", lambda e: e.activation(out=MGI.ap, in_=XR.ap, func=AF.Exp, scale=-1.0), reads=[XR.all()], writes=[MGI.all()])
            tt("dve", LBR.ap, CSp.ap, MGp.ap, ALU.mult, [CSp.all(), MGp.all()], [LBR.all()])
            tt("dve", LBI.ap, SNp.ap, MGp.ap, ALU.mult, [SNp.all(), MGp.all()], [LBI.all()])
            tt("dve", LIR.ap, CSp.ap, MGI.ap, ALU.mult, [CSp.all(), MGI.all()], [LIR.all()])
            S.op("dve", lambda e: e.scalar_tensor_tensor(out=LII.ap, in0=SNp.ap, scalar=-1.0, in1=MGI.ap, op0=ALU.mult, op1=ALU.mult),
                 reads=[SNp.all(), MGI.all()], writes=[LII.all()])
            S.op("dve", lambda e: e.tensor_scalar(out=E1.ap, in0=LBR.ap, scalar1=-1.0, scalar2=None, op0=ALU.add), reads=[LBR.all()], writes=[E1.all()])
            tt("dve", E2.ap, ART.ap, ART.ap, ALU.mult, [ART.all()], [E2.all()])
            tt("dve", E3.ap, AIT.ap, AIT.ap, ALU.mult, [AIT.all()], [E3.all()])
            tt("dve", E2.ap, E2.ap, E3.ap, ALU.add, [E2.all(), E3.all()], [E2.all()])
            S.op("dve", lambda e: e.reciprocal(out=E2.ap, in_=E2.ap), reads=[E2.all()], writes=[E2.all()])
            tt("dve", E3.ap, E1.ap, ART.ap, ALU.mult, [E1.all(), ART.all()], [E3.all()])
            tt("dve", E4.ap, LBI.ap, AIT.ap, ALU.mult, [LBI.all(), AIT.all()], [E4.all()])
            tt("dve", E3.ap, E3.ap, E4.ap, ALU.add, [E3.all(), E4.all()], [E3.all()])
            tt("dve", BR.ap, E3.ap, E2.ap, ALU.mult, [E3.all(), E2.all()], [BR.all()])
            tt("dve", E3.ap, LBI.ap, ART.ap, ALU.mult, [LBI.all(), ART.all()], [E3.all()])
            tt("dve", E4.ap, E1.ap, AIT.ap, ALU.mult, [E1.all(), AIT.all()], [E4.all()])
            tt("dve", E3.ap, E3.ap, E4.ap, ALU.subtract, [E3.all(), E4.all()], [E3.all()])
            tt("dve", BI.ap, E3.ap, E2.ap, ALU.mult, [E3.all(), E2.all()], [BI.all()])

            def cmul_small(ore, oim, are_, aim_, bre_, bim_, rd, wr):
                tt("dve", E3.ap, are_, bre_, ALU.mult, rd, [E3.all()])
                tt("dve", E4.ap, aim_, bim_, ALU.mult, rd, [E4.all()])
                tt("dve", E1.ap, are_, bim_, ALU.mult, rd, [E1.all()])
                tt("dve", E2.ap, aim_, bre_, ALU.mult, rd, [E2.all()])
                tt("dve", ore, E3.ap, E4.ap, ALU.subtract, [E3.all(), E4.all()], wr)
                tt("dve", oim, E1.ap, E2.ap, ALU.add, [E1.all(), E2.all()], wr)
            S.op("dve", lambda e: e.tensor_copy(out=QTA.ap[:, :, 7], in_=BR.ap), reads=[BR.all()], writes=[QTA.all()])
            S.op("dve", lambda e: e.tensor_copy(out=QTB.ap[:, :, 7], in_=BI.ap), reads=[BI.all()], writes=[QTB.all()])
            for j in range(6, -1, -1):
                cmul_small(QTA.ap[:, :, j], QTB.ap[:, :, j], QTA.ap[:, :, j + 1], QTB.ap[:, :, j + 1], LBR.ap, LBI.ap,
                           [QTA.all(), QTB.all(), LBR.all(), LBI.all()], [QTA.all(), QTB.all()])
            cmul_small(QTA.ap[:, :, 8], QTB.ap[:, :, 8], BR.ap, BI.ap, LIR.ap, LII.ap, [BR.all(), BI.all(), LIR.all(), LII.all()], [QTA.all(), QTB.all()])
            for j in range(1, 8):
                cmul_small(QTA.ap[:, :, 8 + j], QTB.ap[:, :, 8 + j], QTA.ap[:, :, 7 + j], QTB.ap[:, :, 7 + j], LIR.ap, LII.ap,
                           [QTA.all(), QTB.all(), LIR.all(), LII.all()], [QTA.all(), QTB.all()])
            S.op("dve", lambda e: e.tensor_copy(out=LTA.ap[:, :, 0], in_=LBR.ap), reads=[LBR.all()], writes=[LTA.all()])
            S.op("dve", lambda e: e.tensor_copy(out=LTB.ap[:, :, 0], in_=LBI.ap), reads=[LBI.all()], writes=[LTB.all()])
            for i in range(1, 8):
                cmul_small(LTA.ap[:, :, i], LTB.ap[:, :, i], LTA.ap[:, :, i - 1], LTB.ap[:, :, i - 1], LBR.ap, LBI.ap,
                           [LTA.all(), LTB.all(), LBR.all(), LBI.all()], [LTA.all(), LTB.all()])
            S.op("dve", lambda e: e.tensor_copy(out=A1R.ap, in_=LTA.ap[:, :, 7]), reads=[LTA.all()], writes=[A1R.all()])
            S.op("dve", lambda e: e.tensor_copy(out=A1I.ap, in_=LTB.ap[:, :, 7]), reads=[LTB.all()], writes=[A1I.all()])
            for _ in range(7):
                tt("dve", E1.ap, A1R.ap, A1R.ap, ALU.mult, [A1R.all()], [E1.all()])
                tt("dve", E3.ap, A1I.ap, A1I.ap, ALU.mult, [A1I.all()], [E3.all()])
                tt("dve", E4.ap, A1R.ap, A1I.ap, ALU.mult, [A1R.all(), A1I.all()], [E4.all()])
                tt("dve", A1R.ap, E1.ap, E3.ap, ALU.subtract, [E1.all(), E3.all()], [A1R.all()])
                S.op("dve", lambda e: e.tensor_scalar(out=A1I.ap, in0=E4.ap, scalar1=2.0, scalar2=None, op0=ALU.mult), reads=[E4.all()], writes=[A1I.all()])
            sgn = ccol(C_SGN)
            S.op("dve", lambda e: e.tensor_scalar(out=A1I.ap, in0=A1I.ap, scalar1=sgn, scalar2=None, op0=ALU.mult), reads=[A1I.all(), CONST.all()], writes=[A1I.all()])
            S.op("dve", lambda e: e.tensor_scalar(out=QTB.ap, in0=QTB.ap, scalar1=sgn, scalar2=None, op0=ALU.mult), reads=[QTB.all(), CONST.all()], writes=[QTB.all()])
            S.op("dve", lambda e: e.tensor_scalar(out=LTA.ap, in0=LTA.ap, scalar1=sgn, scalar2=-1.0, op0=ALU.mult, op1=ALU.mult),
                 reads=[LTA.all(), CONST.all()], writes=[LTA.all()])
            S.op("dve", lambda e: e.tensor_scalar(out=LTB.ap, in0=LTB.ap, scalar1=-1.0, scalar2=None, op0=ALU.mult), reads=[LTB.all()], writes=[LTB.all()])
            S.op("dve", lambda e: e.tensor_copy(out=XIK.ap, in_=XI.ap), reads=[XI.all()], writes=[XIK.all()])
            S.op("dve", lambda e: e.tensor_copy(out=XRK.ap, in_=XR.ap), reads=[XR.all()], writes=[XRK.all()])

            for nq in range(4):
                wq_ = WUQ[nq % 2]
                S.dma("pool", wq_.ap, win_d[l, :, OFF_U + nq * 256:OFF_U + (nq + 1) * 256].rearrange("(k p) n -> p k n", p=128),
                      writes=[wq_.all()])
                for j in range(8):
                    bank = (nq * 8 + j) % 4
                    for k in range(16):
                        S.op("pe", lambda e, j=j, k=k, bank=bank, wq_=wq_: e.matmul(
                            PS[bank].ap[:, 0:256], lhsT=H.ap[:, k, j * 128:(j + 1) * 128], rhs=wq_.ap[:, k, :],
                            start=(k == 0), stop=(k == 15)),
                            reads=[H.rng(k * 1024 + j * 128, k * 1024 + (j + 1) * 128), wq_.rng(k * 256, (k + 1) * 256)],
                            writes=[PS[bank].rng(0, 256)], inc=(k == 15))
                    outap = UT.ap[:, nq * 16:(nq + 1) * 16, j, :]
                    inap = PS[bank].ap[:, 0:256].rearrange("p (g c) -> p g c", c=16)
                    S.op("act", lambda e, o=outap, i=inap: e.copy(out=o, in_=i), reads=[PS[bank].rng(0, 256)],
                         writes=[UT.rng(nq * 16 * 128, (nq + 1) * 16 * 128)])
            tap("ut", UT, [64, 8, 16], BF16, l)
            for gq in range(16):
                bank = 4 + gq % 2
                for gl in range(4):
                    g = gq * 4 + gl
                    S.op("pe", lambda e, g=g, gl=gl, bank=bank: e.transpose(
                        out=psb(bank)[:, gl * 128:(gl + 1) * 128], in_=UT.ap[:, g].rearrange("p j c -> p (j c)"), identity=IDB.ap),
                        reads=[UT.rng(g * 128, (g + 1) * 128), IDB.all()], writes=[PS[bank].rng(gl * 64, (gl + 1) * 64)], inc=(gl == 3))
                outap = UBLK.ap[:, gq * 4:(gq + 1) * 4, :].rearrange("p g t -> p (g t)")
                inap = psb(bank)[:, 0:512]
                S.op("act", lambda e, o=outap, i=inap: e.copy(out=o, in_=i), reads=[PS[bank].all()], writes=[UBLK.rng(gq * 512, (gq + 1) * 512)])

            for b in range(16):
                g0 = b * 4
                BT1, BT2, CT1, CT2 = SLAB[b]
                S.dma("sp", BT1.ap[0:64], bre_d[l, g0:g0 + 4].rearrange("g p c -> p g c"), writes=[BT1.all()])
                S.dma("sp", BT1.ap[64:128], bim_d[l, g0:g0 + 4].rearrange("g p c -> p g c"), writes=[BT1.all()])
                S.dma("sp", BT2.ap[0:64], bim_d[l, g0:g0 + 4].rearrange("g p c -> p g c"), writes=[BT2.all()])
                S.dma("sp", BT2.ap[64:128], bre_d[l, g0:g0 + 4].rearrange("g p c -> p g c"), writes=[BT2.all()])
            ARN = [RUT, RT, RWB.sub(keep, RWB.nbytes - keep)]
            for a_ in ARN:
                a_.reset()

            def palloc(shape, dt, parts=128):
                for a_ in ARN:
                    esz = 4 if dt == F32 else 2
                    n = 1
                    for s_ in shape:
                        n *= s_
                    if a_.off + (n * esz + 31) // 32 * 32 <= a_.nbytes:
                        return a_.alloc(shape, dt, parts)
                raise AssertionError("pass arena overflow")

            def view(buf, shape, dt):
                esz = 4 if dt == F32 else 2
                n = 1
                for s_ in shape:
                    n *= s_
                assert n * esz <= buf.hi - buf.lo
                v = buf.ap
                while len(v.shape) > 2:
                    v = v.rearrange("p a b -> p (a b)") if len(v.shape) == 3 else v.rearrange("p a b c -> p (a b c)")
                if dt != F32 and buf.esz == 4:
                    v = v.bitcast(dt)
                v = v[:, 0:n]
                if len(shape) == 2:
                    v = v.rearrange("p (a b) -> p a b", b=shape[1])
                elif len(shape) == 3:
                    v = v.rearrange("p (a b c) -> p a b c", b=shape[1], c=shape[2])
                return Buf("sb", buf.lo, n * esz, v, esz)

            def run_pipelined(genf, n, depth=2):
                active = []
                nxt = 0
                while nxt < n or active:
                    if nxt < n and len(active) < depth:
                        active.append(genf(nxt))
                        nxt += 1
                    for g_ in list(active):
                        try:
                            next(g_)
                        except StopIteration:
                            active.remove(g_)

            def g_trig(Yb, Rb, SNb, CSb, Y2b, R2b):
                S.op("dve", lambda e: e.tensor_scalar(out=Rb.ap, in0=Yb.ap, scalar1=MAGIC, scalar2=MAGIC, op0=ALU.add, op1=ALU.subtract),
                     reads=[Yb.all()], writes=[Rb.all()])
                yield
                S.op("act", lambda e: e.activation(out=Y2b.ap, in_=Yb.ap, func=AF.Identity, bias=ccol(C_Q), scale=1.0),
                     reads=[Yb.all(), CONST.all()], writes=[Y2b.all()])
                yield
                S.op("dve", lambda e: e.tensor_tensor(out=Rb.ap, in0=Yb.ap, in1=Rb.ap, op=ALU.subtract), reads=[Yb.all(), Rb.all()], writes=[Rb.all()])
                yield
                S.op("dve", lambda e: e.tensor_scalar(out=R2b.ap, in0=Y2b.ap, scalar1=MAGIC, scalar2=MAGIC, op0=ALU.add, op1=ALU.subtract),
                     reads=[Y2b.all()], writes=[R2b.all()])
                yield
                S.op("act", lambda e: e.activation(out=SNb.ap, in_=Rb.ap, func=AF.Sin, scale=TWO_PI), reads=[Rb.all()], writes=[SNb.all()])
                yield
                S.op("dve", lambda e: e.tensor_tensor(out=R2b.ap, in0=Y2b.ap, in1=R2b.ap, op=ALU.subtract), reads=[Y2b.all(), R2b.all()], writes=[R2b.all()])
                yield
                S.op("act", lambda e: e.activation(out=CSb.ap, in_=R2b.ap, func=AF.Sin, scale=TWO_PI), reads=[R2b.all()], writes=[CSb.all()])
                yield

            def g_tables(g0, bank, cw, ct, neg_im, DG, Yv, Rv, SNv, CSv, Y2v, R2v, WRE, WIM):
                i64b = CONST.ap[0:64, 0:64].unsqueeze(1).broadcast_to([64, 4, 64])
                tt("dve", DG.ap[:, 0], i64b, XIK.ap[0:64, g0:g0 + 4].unsqueeze(2).broadcast_to([64, 4, 64]), ALU.mult, [CONST.all(), XIK.all()], [DG.rng(0, 256)])
                tt("pool", DG.ap[:, 1], i64b, XRK.ap[0:64, g0:g0 + 4].unsqueeze(2).broadcast_to([64, 4, 64]), ALU.mult, [CONST.all(), XRK.all()], [DG.rng(256, 512)])
                yield
                S.op("pe", lambda e: e.matmul(PS[bank].ap, lhsT=ONF.ap[0:64, :], rhs=DG.ap.rearrange("p a g q -> p (a g q)"), start=True, stop=True),
                     reads=[DG.all(), ONF.all()], writes=[PS[bank].all()])
                yield
                aimv = PS[bank].ap[:, 0:256].rearrange("p (g q) -> p g q", q=64)
                arev = PS[bank].ap[:, 256:512].rearrange("p (g q) -> p g q", q=64)
                S.op("act", lambda e: e.activation(out=Yv.ap, in_=aimv, func=AF.Copy, scale=ccol(cw)),
                     reads=[PS[bank].all(), CONST.all()], writes=[Yv.all()])
                MGv = WRE
                S.op("act", lambda e: e.activation(out=MGv.ap, in_=arev, func=AF.Exp, scale=ccol(ct)),
                     reads=[PS[bank].all(), CONST.all()], writes=[MGv.all()])
                yield
                yield from g_trig(Yv, Rv, SNv, CSv, Y2v, R2v)
                if neg_im:
                    S.op("dve", lambda e: e.scalar_tensor_tensor(out=WIM.ap, in0=SNv.ap, scalar=-1.0, in1=MGv.ap, op0=ALU.mult, op1=ALU.mult),
                         reads=[SNv.all(), MGv.all()], writes=[WIM.all()])
                else:
                    tt("dve", WIM.ap, SNv.ap, MGv.ap, ALU.mult, [SNv.all(), MGv.all()], [WIM.all()])
                yield
                tt("dve", WRE.ap, CSv.ap, MGv.ap, ALU.mult, [CSv.all(), MGv.all()], [WRE.all()])
                yield

            def g_cmul(dre, dim_, drd, Z, wre, wim, xre, xim, xrd):
                tt("dve", Z[0].ap, wre.ap, xre, ALU.mult, [wre.all()] + xrd, [Z[0].all()])
                yield
                tt("dve", Z[1].ap, wim.ap, xim, ALU.mult, [wim.all()] + xrd, [Z[1].all()])
                yield
                tt("dve", Z[2].ap, wre.ap, xim, ALU.mult, [wre.all()] + xrd, [Z[2].all()])
                yield
                tt("dve", Z[3].ap, wim.ap, xre, ALU.mult, [wim.all()] + xrd, [Z[3].all()])
                yield
                tt("pool", dre, Z[0].ap, Z[1].ap, ALU.subtract, [Z[0].all(), Z[1].all()], drd)
                yield
                tt("pool", dim_, Z[2].ap, Z[3].ap, ALU.add, [Z[2].all(), Z[3].all()], drd)
                yield

            def emit_ct(b):
                g0 = b * 4
                M1, M2 = MLD[b % 2]
                CT1, CT2 = SLAB[b][2], SLAB[b][3]
                cv_re = cre_d[l, g0:g0 + 4].rearrange("g c p -> (g c) p")
                cv_im = cim_d[l, g0:g0 + 4].rearrange("g c p -> (g c) p")
                S.dma("sp", M1.ap[:, 0:64], cv_re, writes=[M1.rng(0, 64)])
                S.dma("sp", M1.ap[:, 64:128], cv_im, writes=[M1.rng(64, 128)])
                S.dma("sp", M2.ap[:, 0:64], cv_im, writes=[M2.rng(0, 64)])
                S.dma("sp", M2.ap[:, 64:128], cv_re, writes=[M2.rng(64, 128)])
                for (mld, ct, off) in ((M1, CT1, 0), (M2, CT2, 64)):
                    S.op("pe", lambda e, mld=mld, off=off: e.transpose(out=PS[6].ap[:, off:off + 64], in_=mld.ap, identity=id64),
                         reads=[mld.all(), CONST.all()], writes=[PS[6].rng(off, off + 64)])
                    S.op("act", lambda e, ct=ct, off=off: e.copy(out=ct.ap.rearrange("p g c -> p (g c)"), in_=PS[6].ap[:, off:off + 64]),
                         reads=[PS[6].rng(off, off + 64)], writes=[ct.all()])

            FS = []
            for s_ in range(2):
                tr = [palloc([4, 64], F32) for _ in range(6)]
                FS.append(dict(T1=palloc([4, 8, 16], F32), T2=palloc([4, 8, 16], F32), BNAT=palloc([4, 128], BF16), BBR=palloc([4, 128], BF16),
                               DG=palloc([2, 4, 64], F32, parts=64), TR=tr, WRE=palloc([4, 64], F32), WIM=palloc([4, 64], F32),
                               Z=[tr[0], tr[1], tr[4], tr[5]]))

            def front_batch(b):
                g0 = b * 4
                s_ = b % 2
                f = FS[s_]
                T1, T2, BNAT, BBR, DG, TR, WRE, WIM, Z = (f[k] for k in ("T1", "T2", "BNAT", "BBR", "DG", "TR", "WRE", "WIM", "Z"))
                emit_ct(b)
                yield
                BT1, BT2 = SLAB[b][0], SLAB[b][1]
                bt1b = BT1.ap.unsqueeze(2).broadcast_to([128, 4, 8, 16])
                bt2b = BT2.ap.unsqueeze(2).broadcast_to([128, 4, 8, 16])
                qa = QTA.ap[:, g0:g0 + 4, 0:8].unsqueeze(3).broadcast_to([128, 4, 8, 16])
                qb = QTB.ap[:, g0:g0 + 4, 0:8].unsqueeze(3).broadcast_to([128, 4, 8, 16])
                tt("dve", T1.ap, bt1b, qa, ALU.mult, [BT1.all(), QTA.all()], [T1.all()])
                tt("pool", T2.ap, bt2b, qb, ALU.mult, [BT2.all(), QTB.all()], [T2.all()])
                yield
                tg = g_tables(g0, s_, C_WM, C_TM, True, DG, TR[0], TR[1], TR[2], TR[3], TR[4], TR[5], WRE, WIM)
                next(tg)
                yield
                tt("dve", BNAT.ap.rearrange("p g (j c) -> p g j c", c=16), T1.ap, T2.ap, ALU.add, [T1.all(), T2.all()], [BNAT.all()])
                yield
                next(tg)
                yield
                bk = 4 + s_
                for gl in range(4):
                    S.op("pe", lambda e, gl=gl: e.transpose(out=psb(bk)[:, gl * 128:(gl + 1) * 128], in_=BNAT.ap[:, gl, :], identity=IDB.ap),
                         reads=[BNAT.rng(gl * 128, (gl + 1) * 128), IDB.all()], writes=[PS[bk].rng(gl * 64, (gl + 1) * 64)], inc=(gl == 3))
                yield
                next(tg)
                yield
                S.op("act", lambda e: e.copy(out=BBR.ap.rearrange("p g t -> p (g t)"), in_=psb(bk)[:, 0:512]), reads=[PS[bk].all()], writes=[BBR.all()])
                yield
                sb_ = 2 + s_
                for gl in range(4):
                    g = g0 + gl
                    S.op("pe", lambda e, gl=gl, g=g: e.matmul(PS[sb_].ap[:, gl * 128:(gl + 1) * 128], lhsT=UBLK.ap[:, g, :], rhs=BBR.ap[:, gl, :],
                                                              start=True, stop=True),
                         reads=[UBLK.rng(g * 128, (g + 1) * 128), BBR.rng(gl * 128, (gl + 1) * 128)],
                         writes=[PS[sb_].rng(gl * 128, (gl + 1) * 128)], inc=(gl == 3))
                yield
                for _ in tg:
                    yield
                sv = PS[sb_].ap.rearrange("p (g r q) -> p g r q", r=2, q=64)
                zw = [ZALL.rng(g0 * 128, (g0 + 4) * 128)]
                yield from g_cmul(ZALL.ap[:, g0:g0 + 4, 0, :], ZALL.ap[:, g0:g0 + 4, 1, :], zw, Z, WRE, WIM, sv[:, :, 0, :], sv[:, :, 1, :], [PS[sb_].all()])
                for gl in range(4):
                    g = g0 + gl
                    S.op("pe", lambda e, g=g: e.matmul(PS[7].ap[:, g:g + 1], lhsT=ZALL.ap[:, g].rearrange("p r q -> p (r q)"), rhs=ONB.ap[:, 0:1],
                                                       start=True, stop=True),
                         reads=[ZALL.rng(g * 128, (g + 1) * 128), ONB.all()], writes=[PS[7].rng(g, g + 1)], inc=(gl == 3))
            run_pipelined(front_batch, 16, depth=2)
            S.op("dve", lambda e: e.tensor_copy(out=HV1.ap, in_=PS[7].ap[:, 0:64]), reads=[PS[7].rng(0, 64)], writes=[HV1.all()])
            S.op("pe", lambda e: e.matmul(PS[6].ap[:, 0:64], lhsT=swapm, rhs=HV1.ap, start=True, stop=True),
                 reads=[HV1.all(), CONST.all()], writes=[PS[6].rng(0, 64)])
            tt("dve", HV2.ap, PS[6].ap[:, 0:64], A1I.ap, ALU.mult, [PS[6].rng(0, 64), A1I.all()], [HV2.all()])
            tt("dve", HV1.ap, HV1.ap, A1R.ap, ALU.mult, [HV1.all(), A1R.all()], [HV1.all()])
            tt("dve", HEND.ap, HV1.ap, HV2.ap, ALU.add, [HV1.all(), HV2.all()], [HEND.all()])
            S.op("pe", lambda e: e.transpose(out=PS[6].ap[0:64, 128:256], in_=HEND.ap, identity=ident),
                 reads=[HEND.all(), CONST.all()], writes=[PS[6].rng(128, 256)])
            S.op("dve", lambda e: e.tensor_copy(out=HENDT.ap, in_=PS[6].ap[0:64, 128:256]), reads=[PS[6].rng(128, 256)], writes=[HENDT.all()])
            S.dma("sp", cc1in[l].ap()[:, :], HENDT.ap, reads=[HENDT.all()], writes=[Acc("cc1in%d" % l, 0, 1)])
            S.op("pool", lambda e, l=l: e.collective_compute("AllGather", ALU.bypass, replica_groups=[[0, 1], [2, 3], [4, 5], [6, 7]],
                                                             ins=[cc1in[l].ap().opt()], outs=[cc1out[l].ap().opt()]),
                 reads=[Acc("cc1in%d" % l, 0, 1)], writes=[Acc("cc1out%d" % l, 0, 1)], custom_sem=("cc1", l))

            for a_ in ARN:
                a_.reset()
            BS = []
            for s_ in range(2):
                tr = [palloc([4, 64], F32) for _ in range(6)]
                BS.append(dict(A=palloc([4, 8, 16], F32), B=palloc([4, 8, 16], F32), CBR=palloc([4, 128], F32),
                               DG=palloc([2, 4, 64], F32, parts=64), TR=tr, WRE=palloc([4, 64], F32), WIM=palloc([4, 64], F32),
                               CBRb=palloc([4, 128], BF16), A0R=palloc([4, 128], BF16),
                               Z=[tr[0], tr[1], tr[4], tr[5]], HT=view(tr[2], [4, 2, 64], BF16), HP=view(tr[3], [4, 128], BF16)))

            def back_batch(b):
                g0 = b * 4
                s_ = b % 2
                f = BS[s_]
                A_, B_, CBR, DG, TR, WRE, WIM, CBRb, A0R, Z, HT, HP = (f[k] for k in ("A", "B", "CBR", "DG", "TR", "WRE", "WIM", "CBRb", "A0R", "Z", "HT", "HP"))
                DI = A0R
                tt("pool", DI.ap, ident.unsqueeze(1).broadcast_to([128, 4, 128]), DCOL.ap[:, g0:g0 + 4].unsqueeze(2).broadcast_to([128, 4, 128]),
                   ALU.mult, [CONST.all(), DCOL.all()], [DI.all()])
                BT1, BT2, CT1, CT2 = SLAB[b]
                bt1b = BT1.ap.unsqueeze(2).broadcast_to([128, 4, 8, 16])
                bt2b = BT2.ap.unsqueeze(2).broadcast_to([128, 4, 8, 16])
                qa2 = QTA.ap[:, g0:g0 + 4, 8:16].unsqueeze(3).broadcast_to([128, 4, 8, 16])
                qb2 = QTB.ap[:, g0:g0 + 4, 8:16].unsqueeze(3).broadcast_to([128, 4, 8, 16])
                tt("dve", A_.ap, bt1b, qa2, ALU.mult, [BT1.all(), QTA.all()], [A_.all()])
                tt("pool", B_.ap, bt2b, qb2, ALU.mult, [BT2.all(), QTB.all()], [B_.all()])
                yield
                tg = g_tables(g0, s_, C_WP, C_TP, False, DG, TR[0], TR[1], TR[2], TR[3], TR[4], TR[5], WRE, WIM)
                next(tg)
                yield
                XG = view(A_, [4, 128], F32)
                tt("dve", A_.ap, A_.ap, B_.ap, ALU.add, [A_.all(), B_.all()], [A_.all()])
                yield
                next(tg)
                yield
                ct1b = CT1.ap.unsqueeze(2).broadcast_to([128, 4, 8, 16])
                ct2b = CT2.ap.unsqueeze(2).broadcast_to([128, 4, 8, 16])
                la = LTA.ap[:, g0:g0 + 4, :].unsqueeze(3).broadcast_to([128, 4, 8, 16])
                lb = LTB.ap[:, g0:g0 + 4, :].unsqueeze(3).broadcast_to([128, 4, 8, 16])
                cbr4 = CBR.ap.rearrange("p g (j c) -> p g j c", c=16)
                tt("dve", cbr4, ct1b, la, ALU.mult, [CT1.all(), LTA.all()], [CBR.all()])
                tt("pool", B_.ap, ct2b, lb, ALU.mult, [CT2.all(), LTB.all()], [B_.all()])
                yield
                next(tg)
                yield
                tt("dve", cbr4, cbr4, B_.ap, ALU.add, [CBR.all(), B_.all()], [CBR.all()])
                yield
                next(tg)
                yield
                S.op("act", lambda e: e.copy(out=CBRb.ap, in_=CBR.ap), reads=[CBR.all()], writes=[CBRb.all()])
                ab = 2 + s_
                for gl in range(4):
                    S.op("pe", lambda e, gl=gl: e.matmul(PS[ab].ap[:, gl * 128:(gl + 1) * 128], lhsT=XG.ap[:, gl, :], rhs=CBR.ap[:, gl, :],
                                                         start=True, stop=True),
                         reads=[XG.rng(gl * 128, (gl + 1) * 128), CBR.rng(gl * 128, (gl + 1) * 128)],
                         writes=[PS[ab].rng(gl * 128, (gl + 1) * 128)], inc=(gl == 3))
                yield
                HINB = view(B_, [4, 2, 64], F32)
                S.dma("sp", HINB.ap.rearrange("p g r q -> p (g r q)"), cc1out[l].ap()[g0:g0 + 4, :].rearrange("g n -> (g n)").partition_broadcast(128),
                      reads=[Acc("cc1out%d" % l, 0, 1)], writes=[HINB.all()])
                pb = 4 + s_
                S.op("pe", lambda e: e.matmul(PS[pb].ap, lhsT=LSB.ap, rhs=ZALL.ap[:, g0:g0 + 4].rearrange("p g r q -> p (g r q)"), start=True, stop=True),
                     reads=[ZALL.rng(g0 * 128, (g0 + 4) * 128), LSB.all()], writes=[PS[pb].all()])
                yield
                next(tg)
                yield
                t3v = XG.ap
                tt("dve", t3v, PS[ab].ap.rearrange("p (g n) -> p g n", n=128), a0mask.unsqueeze(1).broadcast_to([128, 4, 128]),
                   ALU.mult, [PS[ab].all(), CONST.all()], [A_.all()])
                yield
                tt("dve", A0R.ap, DI.ap, t3v, ALU.add, [DI.all(), A_.all()], [A0R.all()])
                yield
                for _ in tg:
                    yield
                PF = view(A_, [4, 2, 64], F32)
                S.op("dve", lambda e: e.scalar_tensor_tensor(out=PF.ap.rearrange("p g r q -> p (g r q)"), in0=HINB.ap.rearrange("p g r q -> p (g r q)"),
                                                             scalar=MCOL.ap[:, 0:1], in1=PS[pb].ap, op0=ALU.mult, op1=ALU.add),
                     reads=[HINB.all(), MCOL.all(), PS[pb].all()], writes=[PF.all()])
                yield
                pfv = PF.ap
                yield from g_cmul(HT.ap[:, :, 0, :], HT.ap[:, :, 1, :], [HT.all()], Z, WRE, WIM, pfv[:, :, 0, :], pfv[:, :, 1, :], [PF.all()])
                for gl in range(4):
                    S.op("pe", lambda e, gl=gl: e.transpose(out=psb(pb)[:, gl * 128:(gl + 1) * 128], in_=HT.ap[:, gl].rearrange("p r q -> p (r q)"), identity=IDB.ap),
                         reads=[HT.rng(gl * 128, (gl + 1) * 128), IDB.all()], writes=[PS[pb].rng(gl * 64, (gl + 1) * 64)], inc=(gl == 3))
                yield
                S.op("act", lambda e: e.copy(out=HP.ap.rearrange("p g t -> p (g t)"), in_=psb(pb)[:, 0:512]), reads=[PS[pb].all()], writes=[HP.all()])
                yield
                for gl in range(4):
                    g = g0 + gl
                    S.op("pe", lambda e, gl=gl: e.matmul(PS[6].ap[:, gl * 128:(gl + 1) * 128], lhsT=HP.ap[:, gl, :], rhs=CBRb.ap[:, gl, :],
                                                         start=True, stop=False),
                         reads=[HP.rng(gl * 128, (gl + 1) * 128), CBRb.rng(gl * 128, (gl + 1) * 128)],
                         writes=[PS[6].rng(gl * 128, (gl + 1) * 128)], inc=False)
                    S.op("pe", lambda e, gl=gl, g=g: e.matmul(PS[6].ap[:, gl * 128:(gl + 1) * 128], lhsT=UBLK.ap[:, g, :], rhs=A0R.ap[:, gl, :],
                                                              start=False, stop=True),
                         reads=[UBLK.rng(g * 128, (g + 1) * 128), A0R.rng(gl * 128, (gl + 1) * 128)],
                         writes=[PS[6].rng(gl * 128, (gl + 1) * 128)], inc=True)
                yield
                hb = b % 2
                S.op("act", lambda e: e.activation(out=T2G.ap[:, :, hb * 4:(hb + 1) * 4, :],
                                                   in_=PS[6].ap.rearrange("p (g i c) -> p i g c", g=4, i=8), func=AF.Gelu_apprx_tanh),
                     reads=[PS[6].all()], writes=[T2G.all()])
                if hb == 1:
                    yield
                    q = b // 2
                    for i in range(8):
                        S.op("pe", lambda e, i=i: e.transpose(out=psb(7)[:, i * 128:(i + 1) * 128], in_=T2G.ap[:, i].rearrange("p g c -> p (g c)"), identity=IDB.ap),
                             reads=[T2G.rng(i * 128, (i + 1) * 128), IDB.all()], writes=[PS[7].rng(i * 64, (i + 1) * 64)], inc=(i == 7))
                    yield
                    S.op("dve", lambda e: e.tensor_copy(out=YB.ap[:, q, :], in_=psb(7)), reads=[PS[7].all()], writes=[YB.rng(q * 1024, (q + 1) * 1024)])
            run_pipelined(back_batch, 16, depth=2)

            tap("yb", YB, [8, 1024], BF16, l)
            tap("zall", ZALL, [64, 2, 64], BF16, l)
            RUT.reset()
            VT = RUT.alloc([1024], F32)
            CV = RUT.alloc([1024], F32)
            ACC = RUT.alloc([1024], F32)
            SZ = RUT.alloc([1024], F32)
            jobs = []
            for c in range(16):
                def mk_v(c):
                    def f(slot, banks):
                        proj(slot, 16, H, banks)
                        for hf in range(2):
                            S.op("act", lambda e, hf=hf: e.copy(out=VT.ap[:, hf * 512:(hf + 1) * 512], in_=PS[banks[hf]].ap),
                                 reads=[PS[banks[hf]].all()], writes=[VT.rng(hf * 512, (hf + 1) * 512)])
                    return f

                def mk_cg(c):
                    def f(slot, banks):
                        proj(slot, 16, H, banks)
                        for hf in range(2):
                            S.op("dve", lambda e, hf=hf: e.tensor_tensor(out=CV.ap[:, hf * 512:(hf + 1) * 512], in0=PS[banks[hf]].ap,
                                                                         in1=VT.ap[:, hf * 512:(hf + 1) * 512], op=ALU.mult),
                                 reads=[PS[banks[hf]].all(), VT.rng(hf * 512, (hf + 1) * 512)], writes=[CV.rng(hf * 512, (hf + 1) * 512)])
                        w = lambda t, l=l: CW.ap[:, l, t, c:c + 1]
                        cv3 = CV.ap.rearrange("p (j t) -> p j t", t=128)
                        ac3 = ACC.ap.rearrange("p (j t) -> p j t", t=128)
                        S.op("dve", lambda e: e.tensor_scalar(out=ACC.ap, in0=CV.ap, scalar1=w(2), scalar2=None, op0=ALU.mult),
                             reads=[CV.all(), CW.all()], writes=[ACC.all()])
                        for (o, i, t) in ((ac3[:, 1:8, :], cv3[:, 0:7, :], 1), (ac3[:, 0, 1:128], cv3[:, 7, 0:127], 1),
                                          (ac3[:, 2:8, :], cv3[:, 0:6, :], 0), (ac3[:, 0:2, 1:128], cv3[:, 6:8, 0:127], 0)):
                            S.op("dve", lambda e, o=o, i=i, t=t: e.scalar_tensor_tensor(out=o, in0=i, scalar=w(t), in1=o, op0=ALU.mult, op1=ALU.add),
                                 reads=[CV.all(), ACC.all(), CW.all()], writes=[ACC.all()])
                        S.op("act", lambda e: e.copy(out=TAIL.ap[:, c, :], in_=cv3[:, 6:8, 127]), reads=[CV.all()], writes=[TAIL.rng(c * 2, c * 2 + 2)])
                    return f

                def mk_bg(c):
                    def f(slot, banks):
                        proj(slot, 16, H, banks)
                        S.op("act", lambda e: e.copy(out=BGT.ap, in_=PS[banks[0]].ap.rearrange("p (j t) -> p j t", t=128)[:, 0:2, 0]),
                             reads=[PS[banks[0]].all()], writes=[BGT.all()])
                        for hf in range(2):
                            S.op("dve", lambda e, hf=hf: e.tensor_tensor(out=VT.ap[:, hf * 512:(hf + 1) * 512], in0=PS[banks[hf]].ap,
                                                                         in1=ACC.ap[:, hf * 512:(hf + 1) * 512], op=ALU.mult),
                                 reads=[PS[banks[hf]].all(), ACC.rng(hf * 512, (hf + 1) * 512)], writes=[VT.rng(hf * 512, (hf + 1) * 512)])
                    return f

                def mk_za(c):
                    def f(slot, banks):
                        proj(slot, 16, H, banks)
                        for hf in range(2):
                            S.op("act", lambda e, hf=hf: e.activation(out=SZ.ap[:, hf * 512:(hf + 1) * 512], in_=PS[banks[hf]].ap, func=AF.Silu),
                                 reads=[PS[banks[hf]].all()], writes=[SZ.rng(hf * 512, (hf + 1) * 512)])
                        S.op("dve", lambda e: e.tensor_tensor(out=GATE01.ap[:, c, :], in0=BGT.ap, in1=SZ.ap.rearrange("p (j t) -> p j t", t=128)[:, 0:2, 0], op=ALU.mult),
                             reads=[BGT.all(), SZ.all()], writes=[GATE01.rng(c * 2, c * 2 + 2)])
                        S.op("dve", lambda e: e.tensor_tensor(out=AO.ap[:, c, :], in0=VT.ap, in1=SZ.ap, op=ALU.mult),
                             reads=[VT.all(), SZ.all()], writes=[AO.rng(c * 1024, (c + 1) * 1024)])
                    return f
                jobs.append((win_d[l, :, OFF_V + c * 128:OFF_V + (c + 1) * 128], 16, mk_v(c)))
                jobs.append((win_d[l, :, OFF_CG + c * 128:OFF_CG + (c + 1) * 128], 16, mk_cg(c)))
                jobs.append((win_d[l, :, OFF_BG + c * 128:OFF_BG + (c + 1) * 128], 16, mk_bg(c)))
                jobs.append((win_d[l, :, OFF_ZA + c * 128:OFF_ZA + (c + 1) * 128], 16, mk_za(c)))
            run_jobs(jobs)

            S.dma("sp", cc2in[l].ap()[:, :].rearrange("r (a j) -> (r a) j", j=32), TAIL.ap.rearrange("p c t -> p (c t)"),
                  reads=[TAIL.all()], writes=[Acc("cc2in%d" % l, 0, 1)])
            S.op("pool", lambda e, l=l: e.collective_compute("AllGather", ALU.bypass, replica_groups=[[0, 1], [2, 3], [4, 5], [6, 7]],
                                                             ins=[cc2in[l].ap().opt()], outs=[cc2out[l].ap().opt()]),
                 reads=[Acc("cc2in%d" % l, 0, 1)], writes=[Acc("cc2out%d" % l, 0, 1)], custom_sem=("cc2", l))
            S.dma("sp", TAILP.ap.rearrange("p c t -> p (c t)"), cc2out[l].ap()[0:32, :].rearrange("r (a j) -> (r a) j", j=32),
                  reads=[Acc("cc2out%d" % l, 0, 1)], writes=[TAILP.all()])
            S.op("dve", lambda e: e.tensor_scalar(out=TAILP.ap, in0=TAILP.ap, scalar1=MCOL.ap[:, 0:1], scalar2=None, op0=ALU.mult),
                 reads=[TAILP.all(), MCOL.all()], writes=[TAILP.all()])
            tts = lambda o, a_, b_, op, rd, wr: S.op("dve", lambda e: e.tensor_tensor(out=o, in0=a_, in1=b_, op=op), reads=rd, writes=wr)
            cm1, cm2 = TAILP.ap[:, :, 1], TAILP.ap[:, :, 0]
            w0, w1 = CW.ap[:, l, 0, :], CW.ap[:, l, 1, :]
            tts(DL.ap[:, 0, :], cm1, w1, ALU.mult, [TAILP.all(), CW.all()], [DL.rng(0, 16)])
            tts(DL.ap[:, 1, :], cm2, w0, ALU.mult, [TAILP.all(), CW.all()], [DL.rng(16, 32)])
            tts(DL.ap[:, 0, :], DL.ap[:, 0, :], DL.ap[:, 1, :], ALU.add, [DL.rng(0, 32)], [DL.rng(0, 16)])
            tts(DL.ap[:, 0, :], DL.ap[:, 0, :], GATE01.ap[:, :, 0], ALU.mult, [DL.rng(0, 16), GATE01.all()], [DL.rng(0, 16)])
            tts(DL.ap[:, 2, :], cm1, w0, ALU.mult, [TAILP.all(), CW.all()], [DL.rng(32, 48)])
            tts(DL.ap[:, 2, :], DL.ap[:, 2, :], GATE01.ap[:, :, 1], ALU.mult, [DL.rng(32, 48), GATE01.all()], [DL.rng(32, 48)])
            tts(AO.ap[:, :, 0], AO.ap[:, :, 0], DL.ap[:, 0, :], ALU.add, [AO.all(), DL.rng(0, 16)], [AO.all()])
            tts(AO.ap[:, :, 128], AO.ap[:, :, 128], DL.ap[:, 2, :], ALU.add, [AO.all(), DL.rng(32, 48)], [AO.all()])

            tap("ao", AO, [16, 1024], BF16, l)
            RT.reset()
            TA = RT.alloc([1024], F32)
            TB = RT.alloc([1024], F32)
            jobs = []
            for eo in range(8):
                def mk_glu(eo):
                    def f(slot, banks):
                        proj(slot, 8, YB, banks)
                        for hf in range(2):
                            S.op("act", lambda e, hf=hf, l=l: e.activation(out=TA.ap[:, hf * 512:(hf + 1) * 512], in_=PS[banks[hf]].ap, func=AF.Sigmoid,
                                                                           bias=BGL.ap[:, l, eo:eo + 1]),
                                 reads=[PS[banks[hf]].all(), BGL.all()], writes=[TA.rng(hf * 512, (hf + 1) * 512)])
                    return f

                def mk_zb(eo):
                    def f(slot, banks):
                        proj(slot, 16, H, banks)
                        for hf in range(2):
                            S.op("act", lambda e, hf=hf: e.activation(out=TB.ap[:, hf * 512:(hf + 1) * 512], in_=PS[banks[hf]].ap, func=AF.Silu),
                                 reads=[PS[banks[hf]].all()], writes=[TB.rng(hf * 512, (hf + 1) * 512)])
                        S.op("dve", lambda e: e.tensor_tensor(out=GATE.ap[:, eo, :], in0=TA.ap, in1=TB.ap, op=ALU.mult),
                             reads=[TA.all(), TB.all()], writes=[GATE.rng(eo * 1024, (eo + 1) * 1024)])
                        if eo == 7:
                            for e2 in range(8):
                                S.op("dve", lambda e, e2=e2: e.tensor_tensor(out=YB.ap[:, e2, :], in0=YB.ap[:, e2, :], in1=GATE.ap[:, e2, :], op=ALU.mult),
                                     reads=[YB.rng(e2 * 1024, (e2 + 1) * 1024), GATE.rng(e2 * 1024, (e2 + 1) * 1024)], writes=[YB.rng(e2 * 1024, (e2 + 1) * 1024)])
                            tap("yb2", YB, [8, 1024], BF16, l)
                    return f
                jobs.append((wglu_d[l, :, eo * 128:(eo + 1) * 128], 8, mk_glu(eo)))
                jobs.append((win_d[l, :, OFF_ZB + eo * 128:OFF_ZB + (eo + 1) * 128], 16, mk_zb(eo)))
            for c in range(16):
                def mk_g(c, dst):
                    def f(slot, banks):
                        proj(slot, 16, H, banks)
                        for hf in range(2):
                            S.op("act", lambda e, hf=hf: e.activation(out=dst.ap[:, hf * 512:(hf + 1) * 512], in_=PS[banks[hf]].ap, func=AF.Sigmoid),
                                 reads=[PS[banks[hf]].all()], writes=[dst.rng(hf * 512, (hf + 1) * 512)])
                    return f

                def mk_wa(c):
                    def f(slot, banks):
                        proj(slot, 16, AO, banks)
                        for hf in range(2):
                            S.op("dve", lambda e, hf=hf: e.tensor_tensor(out=TA.ap[:, hf * 512:(hf + 1) * 512], in0=PS[banks[hf]].ap,
                                                                         in1=TA.ap[:, hf * 512:(hf + 1) * 512], op=ALU.mult),
                                 reads=[PS[banks[hf]].all(), TA.rng(hf * 512, (hf + 1) * 512)], writes=[TA.rng(hf * 512, (hf + 1) * 512)])
                    return f

                def mk_wb(c):
                    def f(slot, banks):
                        proj(slot, 8, YB, banks)
                        for hf in range(2):
                            S.op("dve", lambda e, hf=hf: e.tensor_tensor(out=TB.ap[:, hf * 512:(hf + 1) * 512], in0=PS[banks[hf]].ap,
                                                                         in1=TB.ap[:, hf * 512:(hf + 1) * 512], op=ALU.mult),
                                 reads=[PS[banks[hf]].all(), TB.rng(hf * 512, (hf + 1) * 512)], writes=[TB.rng(hf * 512, (hf + 1) * 512)])
                        S.op("dve", lambda e: e.tensor_tensor(out=M.ap[:, c, :], in0=TA.ap, in1=TB.ap, op=ALU.add),
                             reads=[TA.all(), TB.all()], writes=[M.rng(c * 1024, (c + 1) * 1024)])
                    return f
                jobs.append((win_d[l, :, OFF_GA + c * 128:OFF_GA + (c + 1) * 128], 16, mk_g(c, TA)))
                jobs.append((wa_d[l, :, c * 128:(c + 1) * 128], 16, mk_wa(c)))
                jobs.append((win_d[l, :, OFF_GB + c * 128:OFF_GB + (c + 1) * 128], 16, mk_g(c, TB)))
                jobs.append((wb_d[l, :, c * 128:(c + 1) * 128], 8, mk_wb(c)))
            for c in range(16):
                def mk_wo(c):
                    def f(slot, banks):
                        proj(slot, 16, M, banks)
                        for hf in range(2):
                            S.op("dve", lambda e, hf=hf: e.tensor_tensor(out=X.ap[:, c, hf * 512:(hf + 1) * 512], in0=X.ap[:, c, hf * 512:(hf + 1) * 512],
                                                                         in1=PS[banks[hf]].ap, op=ALU.add),
                                 reads=[PS[banks[hf]].all(), X.rng(c * 1024 + hf * 512, c * 1024 + (hf + 1) * 512)],
                                 writes=[X.rng(c * 1024 + hf * 512, c * 1024 + (hf + 1) * 512)])
                    return f
                jobs.append((wo_d[l, :, c * 128:(c + 1) * 128], 16, mk_wo(c)))
            run_jobs(jobs)

        tap("xf", X, [16, 1024], F32)
        rmsnorm(lambda k: FG.ap[:, k:k + 1], False)
        RAO.reset()
        OST = [RAO.alloc([2048], F32) for _ in range(2)]
        ov = out_d.rearrange("(t j) d -> j t d", j=8)
        out_events = []
        for j in range(8):
            st = OST[j % 2]
            for kq in range(4):
                bank = (j * 4 + kq) % 4
                for kk in range(4):
                    k = kq * 4 + kk
                    S.op("pe", lambda e, k=k, kk=kk, bank=bank, j=j: e.transpose(
                        out=PS[bank].ap[:, kk * 128:(kk + 1) * 128], in_=X.ap[:, k, j * 128:(j + 1) * 128], identity=ident),
                        reads=[X.rng(k * 1024 + j * 128, k * 1024 + (j + 1) * 128), CONST.all()],
                        writes=[PS[bank].rng(kk * 128, (kk + 1) * 128)], inc=(kk == 3))
                eng = "act" if kq % 2 == 0 else "dve"
                o = st.ap[:, kq * 512:(kq + 1) * 512]
                if eng == "act":
                    S.op("act", lambda e, o=o, bank=bank: e.copy(out=o, in_=PS[bank].ap), reads=[PS[bank].all()], writes=[st.rng(kq * 512, (kq + 1) * 512)])
                else:
                    S.op("dve", lambda e, o=o, bank=bank: e.tensor_copy(out=o, in_=PS[bank].ap), reads=[PS[bank].all()], writes=[st.rng(kq * 512, (kq + 1) * 512)])
            out_events.append(S.dma("sp", ov[j], st.ap, reads=[st.all()], writes=[Acc("out", j, j + 1)]))
        S.wait_all("sp", out_events + dbg_events)

        sems = {k: es.enter_context(nc.semaphore("s_" + "_".join(map(str, k)))) for k in sorted(S.sem_names, key=str)}
        es.enter_context(nc.allow_non_contiguous_dma(reason="small strided parameter loads"))
        block = es.enter_context(nc.Block())
        S.replay(block, sems)
    return nc, S


_CACHE = {}


def _consts():
    c = np.zeros((128, NCONST), np.float32)
    c[:, C_IDENT:C_IDENT + 128] = np.eye(128, dtype=np.float32)
    jj = np.arange(128) // 16
    c[:, C_A0MASK:C_A0MASK + 128] = (jj[None, :] >= jj[:, None]).astype(np.float32)
    sw = np.zeros((128, 128), np.float32)
    for p in range(64):
        sw[p, 64 + p] = 1.0
        sw[64 + p, p] = 1.0
    c[:, C_SWAP:C_SWAP + 128] = sw
    t = np.arange(128, dtype=np.float64)
    c[:, C_TM] = -8.0 * (t + 1)
    c[:, C_TP] = 8.0 * t
    c[:, C_WM] = (t + 1) * 8.0 / (2 * np.pi)
    c[:, C_WP] = t * 8.0 / (2 * np.pi)
    c[:64, C_SGN] = -1.0
    c[64:, C_SGN] = 1.0
    c[:, C_EPS] = 1e-6
    c[:, C_Q] = 0.25
    ii = np.arange(128)
    c[:, C_LS:C_LS + 128] = (ii[:, None] < ii[None, :]).astype(np.float32)
    return c


def kernel(_debug=(), **inputs):
    key = ("nc", tuple(sorted(_debug)))
    if key not in _CACHE:
        _CACHE[key] = build_program(debug=set(_debug))[0]
    nc = _CACHE[key]
    x = np.ascontiguousarray(inputs["x"], dtype=np.float32)
    consts = _consts()
    shared = {k: np.ascontiguousarray(inputs[k], dtype=np.float32) for k in
              ("norm_g", "w_in", "conv_w", "w_out_a", "a_re", "a_im", "log_dt", "b_re", "b_im", "c_re", "c_im",
               "d_skip", "w_glu", "b_glu", "w_out_b", "w_o", "final_g")}
    in_maps = []
    for c in range(8):
        b, half = c // 2, c % 2
        m = dict(shared)
        m["x"] = np.ascontiguousarray(x[b, half * NT:(half + 1) * NT, :])
        m["consts"] = consts
        m["maskcol"] = np.full((128, 1), float(half), np.float32)
        in_maps.append(m)
    res = run_bass_kernel_spmd(nc, in_maps, core_ids=list(range(8)))
    out = np.empty((4, 2048, 2048), np.float32)
    for c in range(8):
        b, half = c // 2, c % 2
        out[b, half * NT:(half + 1) * NT, :] = res.results[c]["out"]
    if _debug:
        return out, res.results
    return out
```
